# Optimizing a Trainium2 kernel written in Bass

```python
import math
import jax, jax.numpy as jnp
from jax import lax
import numpy as np

D_MODEL = 1024
BATCH = 16
SEQ = 2048
DEPTH = 1
DEC_BATCH = 128
DEC_SEQ = 1
PAST_LEN = 8192
PAGE_SIZE = 128

MIX_WIDTH = D_MODEL
ATTN_WIDTH = MIX_WIDTH // 2
RET_WIDTH = MIX_WIDTH - ATTN_WIDTH
ATTN_HEAD_DIM = 64
ATTN_HEADS = ATTN_WIDTH // ATTN_HEAD_DIM
RET_HEADS = 4
RET_HEAD_DIM = RET_WIDTH // RET_HEADS
DILATED_GROUPS = ((128, 1), (512, 4), (2048, 16))
WINDOW_MAX = max(w for w, _ in DILATED_GROUPS)
RET_CHUNK = 128
PEER_HEADS = 8
PEER_N_KEYS = 128
PEER_N_EXPERTS = PEER_N_KEYS ** 2
PEER_TOPK = 16
PEER_QUERY_DIM = 256
PEER_HALF = PEER_QUERY_DIM // 2
PEER_TOKEN_BLOCK = 128
NORM_EPS = 1e-6
GN_EPS = 1e-5
IN_COLS = 3 * ATTN_WIDTH + 4 * RET_WIDTH
SPLIT_POINTS = [ATTN_WIDTH, 2 * ATTN_WIDTH, 3 * ATTN_WIDTH,
                3 * ATTN_WIDTH + RET_WIDTH, 3 * ATTN_WIDTH + 2 * RET_WIDTH, 3 * ATTN_WIDTH + 3 * RET_WIDTH]

kernel_name = 'hymba_dilated_retention_peer_step'


def _alibi_slopes():
    n = ATTN_HEADS
    return jnp.asarray((2.0 ** (-8.0 * (np.arange(n) + 1) / n)).astype(np.float32))


def _ret_log_gamma():
    return jnp.asarray(np.log(1.0 - 2.0 ** (-5.0 - np.arange(RET_HEADS))).astype(np.float32))


def _rmsnorm(x, g):
    xf = x.astype(jnp.float32)
    r = lax.rsqrt(jnp.mean(xf * xf, -1, keepdims=True) + NORM_EPS)
    return (xf * r * g.astype(jnp.float32)).astype(x.dtype)


def _head_groupnorm(o, g):
    of = o.astype(jnp.float32)
    mu = jnp.mean(of, -1, keepdims=True)
    var = jnp.mean(jnp.square(of - mu), -1, keepdims=True)
    y = (of - mu) * lax.rsqrt(var + GN_EPS) * g.astype(jnp.float32).reshape(RET_HEADS, RET_HEAD_DIM)
    return y.astype(o.dtype)


def _softmax_stats(logits, valid):
    logits = jnp.where(valid, logits, -jnp.inf)
    m = jnp.max(logits, -1, keepdims=True)
    p = jnp.exp(logits - m)
    s = jnp.sum(p, -1, keepdims=True)
    return p / s, (m + jnp.log(s))[..., 0]


def _dilated_group_prompt(q, k, v, dil, n_steps, slopes):
    B, S, H, C = q.shape
    L = S // dil
    qb = math.gcd(L, 128)
    nblk = L // qb
    band = qb + n_steps
    qr = q.reshape(B, nblk, qb, dil, H, C)
    pad = ((0, 0), (n_steps, 0), (0, 0), (0, 0), (0, 0))
    kp = jnp.pad(k.reshape(B, L, dil, H, C), pad)
    vp = jnp.pad(v.reshape(B, L, dil, H, C), pad)
    idx = np.arange(nblk)[:, None] * qb + np.arange(band)[None, :]
    kb = kp[:, idx]
    vb = vp[:, idx]
    scores = jnp.einsum('bnqrhc,bnkrhc->bnrhqk', qr, kb).astype(jnp.float32) * (ATTN_HEAD_DIM ** -0.5)
    step = np.arange(qb)[:, None] + n_steps - np.arange(band)[None, :]
    true_key = idx[:, None, :] - n_steps
    valid = (step >= 0) & (step <= n_steps) & (true_key >= 0)
    dist = jnp.asarray((step * dil).astype(np.float32))
    logits = scores - slopes[:, None, None] * dist[None]
    p, lse = _softmax_stats(logits, valid[None, :, None, None])
    o = jnp.einsum('bnrhqk,bnkrhc->bnqrhc', p.astype(vb.dtype), vb).reshape(B, S, H, C)
    lse = jnp.transpose(lse, (0, 1, 4, 2, 3)).reshape(B, S, H)
    return o, lse


def _dilated_group_sample(q, kc, vc, dil, n_steps, slopes, offset):
    T = q.shape[1]
    j = np.arange(n_steps + 1)
    rows = offset + np.arange(T)[:, None] - j[None, :] * dil
    valid = rows >= 0
    rows_c = np.maximum(rows, 0)
    kg = kc[:, rows_c]
    vg = vc[:, rows_c]
    scores = jnp.einsum('bthc,btjhc->bthj', q, kg).astype(jnp.float32) * (ATTN_HEAD_DIM ** -0.5)
    dist = jnp.asarray((j * dil).astype(np.float32))
    logits = scores - slopes[:, None] * dist[None, :]
    p, lse = _softmax_stats(logits, valid[None, :, None, :])
    o = jnp.einsum('bthj,btjhc->bthc', p.astype(vg.dtype), vg)
    return o, lse


def _combine_groups(outs, lses):
    w = jax.nn.softmax(jnp.stack(lses, -1), -1)
    o = jnp.stack(outs, -2)
    return jnp.einsum('bthg,bthgc->bthc', w.astype(o.dtype), o)


def _retention_chunk(state, q, k, v, log_gamma):
    C = q.shape[1]
    dt = q.dtype
    pos = jnp.arange(C, dtype=jnp.float32)
    rel = pos[:, None] - pos[None, :]
    dmat = jnp.where(rel[None] >= 0, jnp.exp(jnp.maximum(rel, 0.0)[None] * log_gamma[:, None, None]), 0.0)
    inner = jnp.einsum('bqhd,bkhd->bhqk', q, k) * dmat.astype(dt)[None]
    o_in = jnp.einsum('bhqk,bkhv->bqhv', inner, v)
    q_dec = jnp.exp((pos[:, None] + 1.0) * log_gamma[None, :]).astype(dt)
    o_x = jnp.einsum('bqhd,bhdv->bqhv', q, state) * q_dec[None, :, :, None]
    k_dec = jnp.exp((C - 1.0 - pos)[:, None] * log_gamma[None, :]).astype(dt)
    s_new = (jnp.exp(C * log_gamma).astype(dt)[None, :, None, None] * state
             + jnp.einsum('bkhd,bkhv->bhdv', k * k_dec[None, :, :, None], v))
    return s_new, o_in + o_x


def _retention_prompt(q, k, v, log_gamma):
    B, S, H, dk = q.shape
    dv = v.shape[-1]
    n_chunks = S // RET_CHUNK

    def to_chunks(a):
        return jnp.moveaxis(a.reshape(B, n_chunks, RET_CHUNK, H, a.shape[-1]), 1, 0)

    def step(s, xs):
        qc, kc, vc = xs
        return _retention_chunk(s, qc, kc, vc, log_gamma)

    s0 = jnp.zeros((B, H, dk, dv), q.dtype)
    s_final, o = lax.scan(step, s0, (to_chunks(q), to_chunks(k), to_chunks(v)))
    return jnp.moveaxis(o, 0, 1).reshape(B, S, H, dv), s_final


def _mixer_inputs(x, norm_g, w_in):
    B, T, _ = x.shape
    xn = _rmsnorm(x, norm_g)
    aq, ak, av, rq, rk, rv, rg = jnp.split(xn @ w_in, SPLIT_POINTS, axis=-1)
    ah = lambda a: a.reshape(B, T, ATTN_HEADS, ATTN_HEAD_DIM)
    rh = lambda a: a.reshape(B, T, RET_HEADS, RET_HEAD_DIM)
    return ah(aq), ah(ak), ah(av), rh(rq), rh(rk) * (RET_HEAD_DIM ** -0.5), rh(rv), rg


def _mixer_output(attn_o, ret_o, gate, gn_g, w_out):
    B, T = attn_o.shape[:2]
    ret_y = jax.nn.silu(gate) * _head_groupnorm(ret_o, gn_g).reshape(B, T, RET_WIDTH)
    cat = jnp.concatenate([attn_o.reshape(B, T, ATTN_WIDTH), ret_y], -1)
    return cat @ w_out


def _peer_block(xb, w_pq, sub_keys, peer_u, peer_v):
    blk = xb.shape[0]
    q = (xb @ w_pq).reshape(blk, PEER_HEADS, 2, PEER_HALF)
    s = jnp.einsum('thcd,hcnd->thcn', q, sub_keys).astype(jnp.float32)
    sv, si = lax.top_k(s, PEER_TOPK)
    cand = (sv[:, :, 0, :, None] + sv[:, :, 1, None, :]).reshape(blk, PEER_HEADS, PEER_TOPK * PEER_TOPK)
    cidx = (si[:, :, 0, :, None] * PEER_N_KEYS + si[:, :, 1, None, :]).reshape(blk, PEER_HEADS, PEER_TOPK * PEER_TOPK)
    best, pos = lax.top_k(cand, PEER_TOPK)
    eid = jnp.take_along_axis(cidx, pos, -1)
    g = jax.nn.softmax(best, -1).astype(xb.dtype)
    u = peer_u[eid]
    a = jax.nn.gelu(jnp.einsum('thkd,td->thk', u, xb))
    return jnp.einsum('thk,thkd->td', g * a, peer_v[eid])


def _peer(xf, w_pq, sub_keys, peer_u, peer_v):
    n, d = xf.shape
    nb = -(-n // PEER_TOKEN_BLOCK)
    xp = jnp.pad(xf, ((0, nb * PEER_TOKEN_BLOCK - n), (0, 0))).reshape(nb, PEER_TOKEN_BLOCK, d)
    y = lax.map(lambda xb: _peer_block(xb, w_pq, sub_keys, peer_u, peer_v), xp)
    return y.reshape(nb * PEER_TOKEN_BLOCK, d)[:n]


def _channel(h, norm_g, w_pq, sub_keys, peer_u, peer_v):
    B, T, D = h.shape
    hn = _rmsnorm(h, norm_g).reshape(B * T, D)
    return h + _peer(hn, w_pq, sub_keys, peer_u, peer_v).reshape(B, T, D)


def setup_inputs(seed: int = 0) -> dict:
    key = jax.random.key(seed)
    ks = jax.random.split(key, 16)
    wbuf = min(WINDOW_MAX, PAST_LEN)
    nrm = lambda k, shape, s: jax.random.normal(k, shape, jnp.float32) * s
    return {
        'x_prompt': nrm(ks[0], (BATCH, SEQ, D_MODEL), 1.0),
        'x_sample': nrm(ks[1], (DEC_BATCH, DEC_SEQ, D_MODEL), 1.0),
        'cache_k_win': nrm(ks[2], (DEPTH, DEC_BATCH, wbuf, ATTN_HEADS, ATTN_HEAD_DIM), 1.0),
        'cache_v_win': nrm(ks[3], (DEPTH, DEC_BATCH, wbuf, ATTN_HEADS, ATTN_HEAD_DIM), 1.0),
        'state_ret': nrm(ks[4], (DEPTH, DEC_BATCH, RET_HEADS, RET_HEAD_DIM, RET_HEAD_DIM), 0.5),
        'norm1_g': 1.0 + nrm(ks[5], (DEPTH, D_MODEL), 0.01),
        'w_in': nrm(ks[6], (DEPTH, D_MODEL, IN_COLS), D_MODEL ** -0.5),
        'ret_gn_g': 1.0 + nrm(ks[7], (DEPTH, RET_WIDTH), 0.01),
        'w_out': nrm(ks[8], (DEPTH, MIX_WIDTH, D_MODEL), MIX_WIDTH ** -0.5),
        'norm2_g': 1.0 + nrm(ks[9], (DEPTH, D_MODEL), 0.01),
        'w_pq': nrm(ks[10], (DEPTH, D_MODEL, PEER_HEADS * PEER_QUERY_DIM), D_MODEL ** -0.5),
        'peer_sub_keys': nrm(ks[11], (DEPTH, PEER_HEADS, 2, PEER_N_KEYS, PEER_HALF), PEER_HALF ** -0.5),
        'peer_u': nrm(ks[12], (DEPTH, PEER_N_EXPERTS, D_MODEL), D_MODEL ** -0.5),
        'peer_v': nrm(ks[13], (DEPTH, PEER_N_EXPERTS, D_MODEL), (PEER_HEADS * PEER_TOPK) ** -0.5),
        'norm_f_g': 1.0 + nrm(ks[14], (D_MODEL,), 0.01),
    }


def reference(x_prompt, x_sample, cache_k_win, cache_v_win, state_ret, norm1_g, w_in, ret_gn_g, w_out,
              norm2_g, w_pq, peer_sub_keys, peer_u, peer_v, norm_f_g):
    slopes = _alibi_slopes()
    log_gamma = _ret_log_gamma()
    S = x_prompt.shape[1]
    keep = min(WINDOW_MAX, S)
    offset = cache_k_win.shape[2]
    hp, hs = x_prompt, x_sample
    kp_l, vp_l, rp_l, ks_l, vs_l, rs_l = [], [], [], [], [], []
    for l in range(DEPTH):
        aq, ak, av, rq, rk, rv, rg = _mixer_inputs(hp, norm1_g[l], w_in[l])
        outs, lses = [], []
        for window, dil in DILATED_GROUPS:
            o, lse = _dilated_group_prompt(aq, ak, av, dil, window // dil, slopes)
            outs.append(o)
            lses.append(lse)
        attn_o = _combine_groups(outs, lses)
        ret_o, s_fin = _retention_prompt(rq, rk, rv, log_gamma)
        h = hp + _mixer_output(attn_o, ret_o, rg, ret_gn_g[l], w_out[l])
        hp = _channel(h, norm2_g[l], w_pq[l], peer_sub_keys[l], peer_u[l], peer_v[l])
        kp_l.append(ak[:, S - keep:])
        vp_l.append(av[:, S - keep:])
        rp_l.append(s_fin)
        aq, ak, av, rq, rk, rv, rg = _mixer_inputs(hs, norm1_g[l], w_in[l])
        kc = jnp.concatenate([cache_k_win[l], ak], 1)
        vc = jnp.concatenate([cache_v_win[l], av], 1)
        outs, lses = [], []
        for window, dil in DILATED_GROUPS:
            o, lse = _dilated_group_sample(aq, kc, vc, dil, window // dil, slopes, offset)
            outs.append(o)
            lses.append(lse)
        attn_o = _combine_groups(outs, lses)
        s_new, ret_o = _retention_chunk(state_ret[l], rq, rk, rv, log_gamma)
        h = hs + _mixer_output(attn_o, ret_o, rg, ret_gn_g[l], w_out[l])
        hs = _channel(h, norm2_g[l], w_pq[l], peer_sub_keys[l], peer_u[l], peer_v[l])
        ks_l.append(ak)
        vs_l.append(av)
        rs_l.append(s_new)
    y_prompt = _rmsnorm(hp, norm_f_g)
    y_sample = _rmsnorm(hs, norm_f_g)
    return (y_prompt, y_sample, jnp.stack(kp_l, 0), jnp.stack(vp_l, 0), jnp.stack(rp_l, 0),
            jnp.stack(ks_l, 0), jnp.stack(vs_l, 0), jnp.stack(rs_l, 0))
```

```python
import numpy as np
import ml_dtypes
from contextlib import ExitStack
import concourse.bass as bass
import concourse.mybir as mybir
from concourse.bass_utils import run_bass_kernel_spmd

F32 = mybir.dt.float32
BF16 = mybir.dt.bfloat16
I32 = mybir.dt.int32
U32 = mybir.dt.uint32
AF = mybir.ActivationFunctionType
ALU = mybir.AluOpType
AX = mybir.AxisListType

NCORES = 8
D = 1024
SEQ = 2048
NSEQ = 2
NS = 16
INC = 3584
NT = SEQ // 128

ENGS = ("pe", "act", "dve", "pool", "sp")
SAME_ENGINE_SYNC = True


class Buf:
    __slots__ = ("name", "w", "r", "dsem", "dcnt", "excl")

    def __init__(self, name, excl=False):
        self.name = name
        self.excl = excl
        self.w = None
        self.r = {}
        self.dsem = None
        self.dcnt = 0


class Prog:
    def __init__(self):
        self.q = {e: [] for e in ENGS}
        self.cnt = {e: 0 for e in ENGS}
        self.waited = {e: {} for e in ENGS}
        self.ndsem = 0
        self.dtotal = {}
        self.needed = {e: set() for e in ENGS}

    def _deps(self, eng, reads, writes, is_dma_sem=None):
        evs = []
        for b in reads:
            if b.w is not None:
                evs.append(b.w)
            if b.excl:
                evs.extend((k, v) for (k, v) in b.r.items() if k != eng)
        for b in writes:
            if b.w is not None:
                if not (is_dma_sem is not None and b.w[0] == is_dma_sem):
                    evs.append(b.w)
            evs.extend(b.r.items())
        waits = {}
        for (s, v) in evs:
            if s == eng and (eng == "pe" or not SAME_ENGINE_SYNC):
                continue
            if s == is_dma_sem and False:
                continue
            if self.waited[eng].get(s, 0) >= v:
                continue
            if waits.get(s, 0) < v:
                waits[s] = v
        for s, v in waits.items():
            self.waited[eng][s] = v
            if s in ENGS:
                self.needed[s].add(v)
        return list(waits.items())

    def op(self, eng, fn, reads=(), writes=()):
        waits = self._deps(eng, reads, writes)
        self.cnt[eng] += 1
        ev = (eng, self.cnt[eng])
        self.q[eng].append((waits, fn, None, self.cnt[eng]))
        for b in reads:
            b.r[ev[0]] = ev[1]
        for b in writes:
            b.w = ev
            b.r = {}

    def dma(self, eng, fn, reads=(), writes=(), sem=None):
        if sem.dsem is None:
            sem.dsem = ("d", self.ndsem)
            self.ndsem += 1
        key = sem.dsem
        waits = self._deps(eng, reads, writes, is_dma_sem=key)
        sem.dcnt += 16
        self.dtotal[key] = sem.dcnt
        ev = (key, sem.dcnt)
        self.q[eng].append((waits, fn, key, None))
        for b in reads:
            b.r[ev[0]] = ev[1]
        for b in writes:
            b.w = ev
            b.r = {}

    def barrier(self):
        for e in ENGS:
            waits = {}
            for o in ENGS:
                if o != e and self.cnt[o] > 0 and self.waited[e].get(o, 0) < self.cnt[o]:
                    waits[o] = self.cnt[o]
            for k, v in self.dtotal.items():
                if self.waited[e].get(k, 0) < v:
                    waits[k] = v
            for s, v in waits.items():
                self.waited[e][s] = v
                if s in ENGS:
                    self.needed[s].add(v)
            if waits:
                self.q[e].append((list(waits.items()), None, None, None))

    def emit(self, nc, es):
        esem = {e: es.enter_context(nc.semaphore("prog_" + e)) for e in ENGS}
        dsem = {("d", i): es.enter_context(nc.semaphore("dma%d" % i)) for i in range(self.ndsem)}
        rank = {}
        for e in ENGS:
            srt = sorted(self.needed[e])
            rank[e] = {v: i + 1 for i, v in enumerate(srt)}
        block = es.enter_context(nc.Block())

        def run(e, engobj):
            for (waits, fn, dkey, seq) in self.q[e]:
                for (s, v) in waits:
                    if s in ENGS:
                        engobj.wait_ge(esem[s], rank[s][v])
                    else:
                        engobj.wait_ge(dsem[s], v)
                if fn is None:
                    continue
                ins = fn(engobj)
                if dkey is not None:
                    ins.then_inc(dsem[dkey], 16)
                elif seq in rank[e]:
                    ins.then_inc(esem[e], 1)

        @block.tensor
        def _(eng):
            run("pe", eng)

        @block.scalar
        def _(eng):
            run("act", eng)

        @block.vector
        def _(eng):
            run("dve", eng)

        @block.gpsimd
        def _(eng):
            run("pool", eng)

        @block.sync
        def _(eng):
            run("sp", eng)


def _prod(xs):
    n = 1
    for x in xs:
        n *= int(x)
    return n


_DTSIZE = {F32: 4, BF16: 2, U32: 4, I32: 4}


class Arena:
    def __init__(self, t, n16):
        self.t = t
        self.n = n16
        self.off = 0

    def reset(self, off=0):
        self.off = off

    def alloc(self, shape, dt):
        n = _prod(shape)
        e16 = (n * _DTSIZE[dt] + 1) // 2
        a = self.off
        self.off += (e16 + 15) // 16 * 16
        assert self.off <= self.n, ("arena overflow", self.off, self.n)
        v = self.t[:, a:a + e16]
        if dt != BF16:
            v = v.bitcast(dt)
        if len(shape) == 2:
            v = v.rearrange("p (a b) -> p a b", a=shape[0])
        elif len(shape) == 3:
            v = v.rearrange("p (a b c) -> p a b c", a=shape[0], b=shape[1])
        elif len(shape) == 4:
            v = v.rearrange("p (a b c d) -> p a b c d", a=shape[0], b=shape[1], c=shape[2])
        return v


SLOPES = [float(2.0 ** (-8.0 * (h + 1) / 8)) for h in range(8)]
GELU = AF.Gelu_apprx_tanh
import os
RSTOP = int(os.environ.get('RSTOP', '99'))
LG = [float(np.log(1.0 - 2.0 ** (-5.0 - h)).astype(np.float32)) for h in range(4)]
GAM = [float(np.exp(np.float32(x))) for x in LG]


def make_tables(SEQ):
    f32 = np.float32
    TW = SEQ + 384
    NT = SEQ // 128
    r = np.arange(128)[:, None]
    x = np.arange(TW)[None, :]
    d = x - r - 384
    dpos = np.maximum(d, 0).astype(f32)
    mult = (((d >= 0) & (d <= 128)).astype(f32) + ((d >= 0) & (d % 4 == 0) & (d <= 512)).astype(f32)
            + ((d >= 0) & (d % 16 == 0) & (d <= 2048)).astype(f32))
    caus = (d >= 0).astype(f32)
    pos = np.arange(NT)[None, :] * 128 + np.arange(128)[:, None]
    kdec = np.stack([np.exp((SEQ - 1 - pos) * LG[h]) * (128.0 ** -0.5) for h in range(4)], 1)
    rr = np.arange(128)
    sbias = np.zeros((128, 3, 8), f32)
    for g, dil in enumerate((1, 4, 16)):
        for h in range(8):
            sbias[:, g, h] = -SLOPES[h] * dil * (128 - rr)
    return {
        "dtab": dpos.astype(f32),
        "multt": mult.astype(ml_dtypes.bfloat16),
        "caust": caus.astype(ml_dtypes.bfloat16),
        "kdec": kdec.reshape(128, 4 * NT).astype(f32),
        "sbias": sbias.reshape(128, 24),
        "ident": np.eye(128, dtype=f32).astype(ml_dtypes.bfloat16),
        "iota": np.tile(np.arange(128, dtype=f32)[None, :], (128, 1)),
        "selrow": np.repeat(np.eye(16, dtype=f32), 128, axis=1).reshape(16, 2048).copy(),
        "sel16": np.repeat(np.eye(16, dtype=f32), 128, axis=0).reshape(16, 128, 16).transpose(1, 0, 2).reshape(128, 256).copy(),
    }


def build_program(SEQ=2048, NSEQ=2, NS=16, do_sample=True, do_peer=True, NE=16384, stop=99):
    nc = bass.Bass("TRN2", target_bir_lowering=False)
    es = ExitStack()
    P = Prog()
    NT = SEQ // 128
    NQG = SEQ // 512
    TW = SEQ + 384
    NTOK = NSEQ * SEQ
    NI = NE // 128
    NG = NI // 4

    def din(name, shape, dt=F32):
        return nc.dram_tensor(name, list(shape), dt, kind="ExternalInput").ap()

    def dout(name, shape, dt=F32):
        return nc.dram_tensor(name, list(shape), dt, kind="ExternalOutput").ap()

    def dscr(name, shape, dt):
        return nc.dram_tensor(name, list(shape), dt).ap()

    xp = din("xp", [NTOK, D])
    xs = din("xs", [NS, D])
    ck = din("ck", [NS, 2048, 512])
    cv = din("cv", [NS, 2048, 512])
    st = din("st", [NS * 4, 128, 128])
    g1 = din("g1", [128, 8])
    g2 = din("g2", [128, 8])
    gf = din("gf", [1, D])
    ggn = din("ggn", [1, 512])
    w_in = din("w_in", [D, INC])
    w_out = din("w_out", [D, D])
    w_pq = din("w_pq", [D, 2048])
    skT = din("skT", [128, 16 * 128])
    uT = din("uT", [D, NE])
    pv = din("pv", [NE, D])
    ident_d = din("ident", [128, 128], BF16)
    iota_d = din("iota", [128, 128])
    dtab_d = din("dtab", [128, TW])
    multt_d = din("multt", [128, TW], BF16)
    caust_d = din("caust", [128, TW], BF16)
    kdec_d = din("kdec", [128, 4 * NT])
    sbias_d = din("sbias", [128, 24])
    sel16_d = din("sel16", [128, 256])
    selrow_d = din("selrow", [16, 16 * 128])

    yp = dout("yp", [NTOK, D])
    ys = dout("ys", [NS, D])
    kwp = dout("kwp", [NTOK, 512])
    vwp = dout("vwp", [NTOK, 512])
    rp = dout("rp", [NSEQ * 4, 128, 128])
    kns = dout("kns", [NS, 512])
    vns = dout("vns", [NS, 512])
    rs = dout("rs", [NS * 4, 128, 128])

    uT_bf = dscr("uT_bf", [D, NE], BF16)
    pv_bf = dscr("pv_bf", [NE, D], BF16)
    wout_bf = dscr("wout_bf", [D, D], BF16)
    wpq_bf = dscr("wpq_bf", [D, 2048], BF16)
    win_bf = dscr("win_bf", [D, INC], BF16)
    b_winbf = Buf("win_bf")
    b_uTbf = Buf("uT_bf")
    b_pvbf = Buf("pv_bf")
    b_woutbf = Buf("wout_bf")
    b_wpqbf = Buf("wpq_bf")

    def sb(name, shape, dt):
        return es.enter_context(nc.sbuf_tensor(name, list(shape), dt))

    def ps(name, shape, dt):
        return es.enter_context(nc.psum_tensor(name, list(shape), dt))

    ident = sb("ident_sb", [128, 128], BF16)
    iota = sb("iota_sb", [128, 128], F32)
    g1_sb = sb("g1_sb", [128, 8], F32)
    g2_sb = sb("g2_sb", [128, 8], F32)
    gf_bc = sb("gf_bc", [128, D], F32)
    ggn_bc = sb("ggn_bc", [128, 512], F32)
    catTok = sb("catTok", [128, NT, D], BF16)
    b_const = Buf("const")
    b_cat = [Buf("cat%d" % i) for i in range(NT)]
    for (dst, src) in ((ident[:], ident_d), (iota[:], iota_d), (g1_sb[:], g1), (g2_sb[:], g2),
                       (gf_bc[:], gf.broadcast_to([128, D])), (ggn_bc[:], ggn.broadcast_to([128, 512]))):
        P.dma("sp", lambda e, dst=dst, src=src: e.dma_start(out=dst, in_=src), writes=[b_const], sem=b_const)

    for r0 in range(0, D, 128):
        P.dma("pool", lambda e, r0=r0: e.dma_start(out=win_bf[r0:r0 + 128, :], in_=w_in[r0:r0 + 128, :]),
              writes=[b_winbf], sem=b_winbf)
    if do_peer:
        RC = 128
        for r0 in range(0, D, RC):
            P.dma("pool", lambda e, r0=r0: e.dma_start(out=wout_bf[r0:r0 + RC, :], in_=w_out[r0:r0 + RC, :]),
                  writes=[b_woutbf], sem=b_woutbf)
            P.dma("pool", lambda e, r0=r0: e.dma_start(out=wpq_bf[r0:r0 + RC, :], in_=w_pq[r0:r0 + RC, :]),
                  writes=[b_wpqbf], sem=b_wpqbf)
        EC = 1024
        for r0 in range(0, D, 128):
            for c0 in range(0, NE, 4096):
                c1 = min(NE, c0 + 4096)
                P.dma("pool", lambda e, r0=r0, c0=c0, c1=c1: e.dma_start(out=uT_bf[r0:r0 + 128, c0:c1],
                                                                       in_=uT[r0:r0 + 128, c0:c1]),
                      writes=[b_uTbf], sem=b_uTbf)
        for r0 in range(0, NE, 512):
            P.dma("pool", lambda e, r0=r0: e.dma_start(out=pv_bf[r0:r0 + 512, :], in_=pv[r0:r0 + 512, :]),
                  writes=[b_pvbf], sem=b_pvbf)

    pbank = [ps("pb%d" % i, [128, 512], F32) for i in range(8)]
    b_pb = [Buf("pb%d" % i, excl=True) for i in range(8)]

    AR16 = (nc.sbuf_bytes_remaining - 2048) // 2 // 16 * 16
    arena_t = sb("arena", [128, AR16], BF16)
    A = Arena(arena_t, AR16)

    def rmsnorm_T(src_ap, b_src, xb_ap, b_xb, junk_ap, b_junk, stat_ap, b_stat, pst_i, gain_ap, dst_ap, b_dst, tp=128):
        src_ap = src_ap[0:tp]
        xb_ap = xb_ap[0:tp]
        junk_ap = junk_ap[0:tp]
        stat_ap = stat_ap[0:tp]
        P.op("act", lambda e: e.activation(out=junk_ap, in_=src_ap, func=AF.Square, accum_out=stat_ap[:, 0:1]),
             reads=[b_src], writes=[b_junk, b_stat])
        P.op("dve", lambda e: e.tensor_scalar(out=stat_ap[:, 1:2], in0=stat_ap[:, 0:1], scalar1=1.0 / D, scalar2=1e-6,
                                              op0=ALU.mult, op1=ALU.add), reads=[b_stat], writes=[b_stat])
        P.op("act", lambda e: e.activation(out=stat_ap[:, 2:3], in_=stat_ap[:, 1:2], func=AF.Sqrt),
             reads=[b_stat], writes=[b_stat])
        P.op("dve", lambda e: e.reciprocal(out=stat_ap[:, 3:4], in_=stat_ap[:, 2:3]), reads=[b_stat], writes=[b_stat])
        P.op("dve", lambda e: e.tensor_scalar(out=xb_ap, in0=src_ap, scalar1=stat_ap[:, 3:4], scalar2=None, op0=ALU.mult),
             reads=[b_src, b_stat], writes=[b_xb])
        pst = pbank[pst_i][:].bitcast(BF16)
        for k in range(8):
            P.op("pe", lambda e, k=k: e.transpose(out=pst[:, k * 128:k * 128 + tp], in_=xb_ap[:, k * 128:(k + 1) * 128],
                                                  identity=ident[0:tp, 0:tp]),
                 reads=[b_xb, b_const], writes=[b_pb[pst_i]])
        P.op("dve", lambda e: e.tensor_tensor(out=dst_ap, in0=pst.rearrange("p (k t) -> p k t", k=8)[:, :, 0:tp],
                                              in1=gain_ap.unsqueeze(2).broadcast_to([128, 8, tp]), op=ALU.mult),
             reads=[b_pb[pst_i], b_const], writes=[b_dst])

    def groupnorm(tp, src, b_src, gain_ap, gate_ap, b_gate, out_ap, b_outs, gn_a, gn_b, gn_s, b_gn):
        ga, gb_, gs = gn_a[0:tp], gn_b[0:tp], gn_s[0:tp]
        P.op("dve", lambda e: e.tensor_reduce(out=gs[:, 0:4], in_=src, axis=AX.X, op=ALU.add), reads=[b_src], writes=[b_gn])
        P.op("dve", lambda e: e.tensor_scalar(out=gs[:, 0:4], in0=gs[:, 0:4], scalar1=1.0 / 128, scalar2=None, op0=ALU.mult),
             reads=[b_gn], writes=[b_gn])
        P.op("dve", lambda e: e.tensor_tensor(out=ga, in0=src, in1=gs[:, 0:4].unsqueeze(2).broadcast_to([tp, 4, 128]),
                                              op=ALU.subtract), reads=[b_src, b_gn], writes=[b_gn])
        P.op("dve", lambda e: e.tensor_tensor(out=gb_, in0=ga, in1=ga, op=ALU.mult), reads=[b_gn], writes=[b_gn])
        P.op("dve", lambda e: e.tensor_reduce(out=gs[:, 4:8], in_=gb_, axis=AX.X, op=ALU.add), reads=[b_gn], writes=[b_gn])
        P.op("dve", lambda e: e.tensor_scalar(out=gs[:, 4:8], in0=gs[:, 4:8], scalar1=1.0 / 128, scalar2=1e-5,
                                              op0=ALU.mult, op1=ALU.add), reads=[b_gn], writes=[b_gn])
        P.op("act", lambda e: e.activation(out=gs[:, 8:12], in_=gs[:, 4:8], func=AF.Sqrt), reads=[b_gn], writes=[b_gn])
        P.op("dve", lambda e: e.reciprocal(out=gs[:, 12:16], in_=gs[:, 8:12]), reads=[b_gn], writes=[b_gn])
        P.op("dve", lambda e: e.tensor_tensor(out=ga, in0=ga, in1=gs[:, 12:16].unsqueeze(2).broadcast_to([tp, 4, 128]),
                                              op=ALU.mult), reads=[b_gn], writes=[b_gn])
        P.op("dve", lambda e: e.tensor_tensor(out=ga, in0=ga, in1=gain_ap, op=ALU.mult), reads=[b_gn, b_const], writes=[b_gn])
        P.op("dve", lambda e: e.tensor_tensor(out=out_ap, in0=ga, in1=gate_ap, op=ALU.mult), reads=[b_gn, b_gate], writes=b_outs)

    cat_s = sb("cat_s", [128, D], BF16)
    b_cats = Buf("cat_s")

    def sample_mixer():
        tp = NS
        P.barrier()
        A.reset()
        xs_sb = A.alloc([D], F32)
        b_xs = Buf("xs")
        xsb = A.alloc([D], BF16)
        b_xsb = Buf("xsb")
        junk = A.alloc([D], BF16)
        b_junk = Buf("junk_s")
        stat_s = A.alloc([4], F32)
        b_stat_s = Buf("stat_s")
        xsT = A.alloc([8, 16], BF16)
        b_xsT = Buf("xsT")
        wblk = [A.alloc([8, 512], BF16) for _ in range(2)]
        b_wblk = [Buf("wblk%d" % i) for i in range(2)]
        z = A.alloc([INC], F32)
        b_z = Buf("z")
        selcol = A.alloc([16, 16], F32)
        selrow = A.alloc([16 * 128], F32)
        sbias_sb = A.alloc([3, 8], F32)
        identf = A.alloc([128], F32)
        b_sc = Buf("sconst")
        kg = [A.alloc([512], F32) for _ in range(2)]
        b_kg = [Buf("kg%d" % i) for i in range(2)]
        vg_ = [A.alloc([512], F32) for _ in range(2)]
        b_vg = [Buf("vg%d" % i) for i in range(2)]
        prod = A.alloc([512], F32)
        b_prod = Buf("prod")
        sc = A.alloc([8], F32)
        b_scr = Buf("sc")
        p_ = [A.alloc([8], F32) for _ in range(2)]
        b_p = [Buf("p%d" % i) for i in range(2)]
        pvt = [A.alloc([512], F32) for _ in range(2)]
        b_pvt = [Buf("pvt%d" % i) for i in range(2)]
        tmp16 = A.alloc([512], F32)
        sself = A.alloc([8], F32)
        pself = A.alloc([8], F32)
        den = A.alloc([8], F32)
        rden = A.alloc([8], F32)
        b_tm = Buf("tok_misc")
        catf = A.alloc([D], F32)
        b_catf = Buf("catf")
        rqT = A.alloc([4, 16], F32)
        b_rqT = Buf("rqT")
        qm = A.alloc([4, 16, 16], F32)
        b_qm = Buf("qm")
        S_sb = [A.alloc([128], F32) for _ in range(2)]
        b_S = [Buf("S%d" % i) for i in range(2)]
        sout = [A.alloc([128], F32) for _ in range(2)]
        b_sout = [Buf("sout%d" % i) for i in range(2)]
        kmask = [A.alloc([128], F32) for _ in range(2)]
        b_kmask = [Buf("kmask%d" % i) for i in range(2)]
        ro = A.alloc([4, 128], F32)
        b_ro = Buf("ro")
        qk = A.alloc([4], F32)
        sgs = A.alloc([4, 128], F32)
        b_sgs = Buf("sgs")
        gn_a = A.alloc([4, 128], F32)
        gn_b = A.alloc([4, 128], F32)
        gn_s = A.alloc([16], F32)
        b_gn = Buf("gn_s")

        P.dma("sp", lambda e: e.dma_start(out=selcol, in_=sel16_d.rearrange("p (a b) -> p a b", a=16)), writes=[b_sc], sem=b_sc)
        P.dma("sp", lambda e: e.dma_start(out=selrow[0:16], in_=selrow_d), writes=[b_sc], sem=b_sc)
        P.dma("sp", lambda e: e.dma_start(out=sbias_sb, in_=sbias_d.rearrange("p (a b) -> p a b", a=3)), writes=[b_sc], sem=b_sc)
        P.op("dve", lambda e: e.tensor_copy(out=identf, in_=ident[:]), reads=[b_const], writes=[b_sc])
        P.dma("sp", lambda e: e.dma_start(out=xs_sb[0:tp], in_=xs), writes=[b_xs], sem=b_xs)
        rmsnorm_T(xs_sb, b_xs, xsb, b_xsb, junk, b_junk, stat_s, b_stat_s, 0, g1_sb[:], xsT[:, :, 0:tp], b_xsT, tp=tp)
        for blk in range(7):
            jb = blk % 2
            P.dma("sp", lambda e, blk=blk, jb=jb: e.dma_start(
                out=wblk[jb], in_=win_bf[:, blk * 512:(blk + 1) * 512].rearrange("(k p) c -> p k c", p=128)),
                reads=[b_winbf], writes=[b_wblk[jb]], sem=b_wblk[jb])
            pa = 2 + jb
            for k in range(8):
                P.op("pe", lambda e, k=k, jb=jb, pa=pa: e.matmul(pbank[pa][0:tp, :], lhsT=xsT[:, k, 0:tp], rhs=wblk[jb][:, k, :],
                                                                start=(k == 0), stop=(k == 7)),
                     reads=[b_xsT, b_wblk[jb]], writes=[b_pb[pa]])
            P.op("act", lambda e, blk=blk, pa=pa: e.copy(out=z[0:tp, blk * 512:(blk + 1) * 512], in_=pbank[pa][0:tp, :]),
                 reads=[b_pb[pa]], writes=[b_z])
        P.dma("sp", lambda e: e.dma_start(out=kns, in_=z[0:tp, 512:1024]), reads=[b_z], sem=b_z)
        P.dma("sp", lambda e: e.dma_start(out=vns, in_=z[0:tp, 1024:1536]), reads=[b_z], sem=b_z)

        RT, WT_ = [b_tm], [b_tm]
        P.op("dve", lambda e: e.tensor_tensor(out=tmp16[0:tp], in0=z[0:tp, 0:512], in1=z[0:tp, 512:1024], op=ALU.mult),
             reads=[b_z], writes=WT_)
        P.op("dve", lambda e: e.tensor_reduce(out=sself[0:tp], in_=tmp16[0:tp].rearrange("p (h c) -> p h c", h=8), axis=AX.X, op=ALU.add),
             reads=RT, writes=WT_)
        P.op("act", lambda e: e.activation(out=pself[0:tp], in_=sself[0:tp], func=AF.Exp, scale=0.125), reads=RT, writes=WT_)
        P.op("dve", lambda e: e.tensor_scalar(out=pself[0:tp], in0=pself[0:tp], scalar1=3.0, scalar2=None, op0=ALU.mult), reads=RT, writes=WT_)
        NTOT = tp * 3
        for b in range(tp):
            P.op("pe", lambda e, b=b: e.matmul(pbank[4][:], lhsT=selrow[0:16, b * 128:(b + 1) * 128], rhs=z[0:16, 0:512],
                                               start=True, stop=True), reads=[b_sc, b_z], writes=[b_pb[4]])
            for g, dil in enumerate((1, 4, 16)):
                n = b * 3 + g
                jj = n % 2
                st0 = 2048 - 128 * dil
                if dil == 1:
                    ksrc = ck[b, st0:2048, :]
                    vsrc = cv[b, st0:2048, :]
                else:
                    ksrc = ck[b, st0:2048, :].rearrange("(r d) c -> r d c", d=dil)[:, 0, :]
                    vsrc = cv[b, st0:2048, :].rearrange("(r d) c -> r d c", d=dil)[:, 0, :]
                P.dma("sp", lambda e, jj=jj, ksrc=ksrc: e.dma_start(out=kg[jj], in_=ksrc), writes=[b_kg[jj]], sem=b_kg[jj])
                P.dma("sp", lambda e, jj=jj, vsrc=vsrc: e.dma_start(out=vg_[jj], in_=vsrc), writes=[b_vg[jj]], sem=b_vg[jj])
                P.op("dve", lambda e, jj=jj: e.tensor_tensor(out=prod, in0=kg[jj], in1=pbank[4][:], op=ALU.mult),
                     reads=[b_kg[jj], b_pb[4]], writes=[b_prod])
                P.op("dve", lambda e: e.tensor_reduce(out=sc, in_=prod.rearrange("p (h c) -> p h c", h=8), axis=AX.X, op=ALU.add),
                     reads=[b_prod], writes=[b_scr])
                P.op("dve", lambda e, g=g: e.scalar_tensor_tensor(out=sc, in0=sc, scalar=0.125, in1=sbias_sb[:, g, :],
                                                                 op0=ALU.mult, op1=ALU.add), reads=[b_scr, b_sc], writes=[b_scr])
                P.op("act", lambda e, jj=jj: e.activation(out=p_[jj], in_=sc, func=AF.Exp), reads=[b_scr], writes=[b_p[jj]])
                P.op("dve", lambda e, jj=jj: e.tensor_tensor(out=pvt[jj].rearrange("p (h c) -> p h c", h=8),
                                                             in0=vg_[jj].rearrange("p (h c) -> p h c", h=8),
                                                             in1=p_[jj].unsqueeze(2).broadcast_to([128, 8, 64]), op=ALU.mult),
                     reads=[b_vg[jj], b_p[jj]], writes=[b_pvt[jj]])
                P.op("pe", lambda e, jj=jj, b=b, n=n: e.matmul(pbank[5][0:16, :], lhsT=selcol[:, b, :], rhs=pvt[jj],
                                                              start=(n == 0), stop=(n == NTOT - 1)),
                     reads=[b_pvt[jj], b_sc], writes=[b_pb[5]])
                P.op("pe", lambda e, jj=jj, b=b, n=n: e.matmul(pbank[6][0:16, 0:8], lhsT=selcol[:, b, :], rhs=p_[jj],
                                                              start=(n == 0), stop=(n == NTOT - 1)),
                     reads=[b_p[jj], b_sc], writes=[b_pb[6]])
        P.op("dve", lambda e: e.tensor_tensor(out=tmp16[0:tp].rearrange("p (h c) -> p h c", h=8),
                                              in0=z[0:tp, 1024:1536].rearrange("p (h c) -> p h c", h=8),
                                              in1=pself[0:tp].unsqueeze(2).broadcast_to([tp, 8, 64]), op=ALU.mult),
             reads=[b_z, b_tm], writes=WT_)
        P.op("dve", lambda e: e.tensor_tensor(out=tmp16[0:tp], in0=tmp16[0:tp], in1=pbank[5][0:tp, :], op=ALU.add),
             reads=[b_tm, b_pb[5]], writes=WT_)
        P.op("dve", lambda e: e.tensor_tensor(out=den[0:tp], in0=pself[0:tp], in1=pbank[6][0:tp, 0:8], op=ALU.add),
             reads=[b_tm, b_pb[6]], writes=WT_)
        P.op("dve", lambda e: e.reciprocal(out=rden[0:tp], in_=den[0:tp]), reads=RT, writes=WT_)
        P.op("dve", lambda e: e.tensor_tensor(out=catf[0:tp, 0:512].rearrange("p (h c) -> p h c", h=8),
                                              in0=tmp16[0:tp].rearrange("p (h c) -> p h c", h=8),
                                              in1=rden[0:tp].unsqueeze(2).broadcast_to([tp, 8, 64]), op=ALU.mult),
             reads=RT, writes=[b_catf])

        for r in range(4):
            P.op("pe", lambda e, r=r: e.transpose(out=pbank[1][:, r * 16:(r + 1) * 16], in_=z[0:tp, 1536 + r * 128:1536 + (r + 1) * 128],
                                                  identity=identf[0:tp, 0:tp]), reads=[b_z, b_sc], writes=[b_pb[1]])
        P.op("act", lambda e: e.copy(out=rqT, in_=pbank[1][:, 0:64].rearrange("p (r b) -> p r b", r=4)), reads=[b_pb[1]], writes=[b_rqT])
        for r in range(4):
            P.op("dve", lambda e, r=r: e.tensor_tensor(out=qm[:, r], in0=rqT[:, r, :].unsqueeze(1).broadcast_to([128, 16, 16]),
                                                       in1=selcol, op=ALU.mult), reads=[b_rqT, b_sc], writes=[b_qm])
        for r in range(4):
            for b in range(tp):
                n = r * tp + b
                jj = n % 2
                P.dma("sp", lambda e, jj=jj, b=b, r=r: e.dma_start(out=S_sb[jj], in_=st[b * 4 + r]), writes=[b_S[jj]], sem=b_S[jj])
                P.op("pe", lambda e, jj=jj, b=b, r=r: e.matmul(pbank[7][0:16, r * 128:(r + 1) * 128], lhsT=qm[:, r, b, :], rhs=S_sb[jj],
                                                              start=(b == 0), stop=(b == tp - 1)),
                     reads=[b_qm, b_S[jj]], writes=[b_pb[7]])
                P.op("dve", lambda e, jj=jj, b=b, r=r: e.tensor_scalar(out=kmask[jj][0:tp], in0=z[0:tp, 2048 + r * 128:2048 + (r + 1) * 128],
                                                                      scalar1=identf[0:tp, b:b + 1], scalar2=128.0 ** -0.5,
                                                                      op0=ALU.mult, op1=ALU.mult),
                     reads=[b_z, b_sc], writes=[b_kmask[jj]])
                pa = 2 + jj
                P.op("pe", lambda e, jj=jj, r=r, pa=pa: e.matmul(pbank[pa][:, 0:128], lhsT=kmask[jj][0:tp], rhs=z[0:tp, 2560 + r * 128:2560 + (r + 1) * 128],
                                                                start=True, stop=True),
                     reads=[b_kmask[jj], b_z], writes=[b_pb[pa]])
                P.op("dve", lambda e, jj=jj, r=r, pa=pa: e.scalar_tensor_tensor(out=sout[jj], in0=S_sb[jj], scalar=GAM[r], in1=pbank[pa][:, 0:128],
                                                                               op0=ALU.mult, op1=ALU.add),
                     reads=[b_S[jj], b_pb[pa]], writes=[b_sout[jj]])
                P.dma("sp", lambda e, jj=jj, b=b, r=r: e.dma_start(out=rs[b * 4 + r], in_=sout[jj]), reads=[b_sout[jj]], sem=b_sout[jj])
        P.op("dve", lambda e: e.tensor_tensor(out=tmp16[0:tp], in0=z[0:tp, 1536:2048], in1=z[0:tp, 2048:2560], op=ALU.mult),
             reads=[b_z, b_tm], writes=WT_)
        P.op("dve", lambda e: e.tensor_reduce(out=qk[0:tp], in_=tmp16[0:tp].rearrange("p (r c) -> p r c", r=4), axis=AX.X, op=ALU.add),
             reads=RT, writes=WT_)
        P.op("dve", lambda e: e.scalar_tensor_tensor(out=ro[0:tp], in0=z[0:tp, 2560:3072].rearrange("p (r c) -> p r c", r=4),
                                                     scalar=128.0 ** -0.5, in1=qk[0:tp].unsqueeze(2).broadcast_to([tp, 4, 128]),
                                                     op0=ALU.mult, op1=ALU.mult), reads=[b_z, b_tm], writes=[b_ro])
        for r in range(4):
            P.op("dve", lambda e, r=r: e.scalar_tensor_tensor(out=ro[0:tp, r, :], in0=pbank[7][0:tp, r * 128:(r + 1) * 128], scalar=GAM[r],
                                                             in1=ro[0:tp, r, :], op0=ALU.mult, op1=ALU.add),
                 reads=[b_pb[7], b_ro], writes=[b_ro])
        P.op("act", lambda e: e.activation(out=sgs[0:tp], in_=z[0:tp, 3072:3584].rearrange("p (r c) -> p r c", r=4), func=AF.Silu),
             reads=[b_z], writes=[b_sgs])
        groupnorm(tp, ro[0:tp], b_ro, ggn_bc[0:tp, :].rearrange("p (r c) -> p r c", r=4), sgs[0:tp], b_sgs,
                  catf[0:tp, 512:1024].rearrange("p (r c) -> p r c", r=4), [b_catf], gn_a, gn_b, gn_s, b_gn)
        P.op("act", lambda e: e.copy(out=cat_s[0:tp, :], in_=catf[0:tp, :]), reads=[b_catf], writes=[b_cats])

    def prompt_mixer(s):
        P.barrier()
        A.reset()
        dtab = A.alloc([TW], F32)
        multt = A.alloc([TW], BF16)
        caust = A.alloc([TW], BF16)
        kdec = A.alloc([4 * NT], F32)
        b_tab = Buf("tab")
        for (dst, src) in ((dtab, dtab_d), (multt, multt_d), (caust, caust_d), (kdec, kdec_d)):
            P.dma("sp", lambda e, dst=dst, src=src: e.dma_start(out=dst, in_=src), writes=[b_tab], sem=b_tab)
        xt = [A.alloc([D], F32) for i in range(2)]
        b_xt = [Buf("xt%d" % i) for i in range(2)]
        xb = [A.alloc([D], BF16) for i in range(2)]
        b_xb = [Buf("xb%d" % i) for i in range(2)]
        junk = A.alloc([D], BF16)
        b_junk = Buf("junk")
        stat = A.alloc([NT, 4], F32)
        b_stat = [Buf("stat%d" % i) for i in range(NT)]
        xT = A.alloc([8, SEQ], BF16)
        b_xT = [Buf("xT%d" % i) for i in range(NT)]
        wkv = A.alloc([8, 1024], BF16)
        b_wkv = Buf("wkv")
        kvst = [A.alloc([1024], F32) for i in range(2)]
        b_kvst = [Buf("kvst%d" % i) for i in range(2)]
        wu = [A.alloc([8, 4, 128], BF16) for i in range(2)]
        b_wu = [Buf("wu%d" % i) for i in range(2)]
        QT = [A.alloc([SEQ], BF16) for i in range(2)]
        KT = [A.alloc([SEQ], BF16) for i in range(2)]
        b_QT = [Buf("QT%d" % i) for i in range(2)]
        b_KT = [Buf("KT%d" % i) for i in range(2)]
        Vb = [A.alloc([NT, 130], BF16) for i in range(2)]
        b_Vb = [Buf("Vb%d" % i) for i in range(2)]
        Kd = A.alloc([NT, 128], BF16)
        b_Kd = Buf("Kd")
        sg = A.alloc([NT, 128], BF16)
        b_sg = Buf("sg")
        Et = [A.alloc([TW], BF16) for i in range(2)]
        b_Et = [Buf("Et%d" % i) for i in range(2)]
        Etmp = A.alloc([TW], BF16)
        b_Etmp = Buf("Etmp")
        NPB = 3
        Pe = [A.alloc([512], BF16) for i in range(NPB)]
        b_Pe = [Buf("Pe%d" % i) for i in range(NPB)]
        Pm = [A.alloc([512], BF16) for i in range(NPB)]
        b_Pm = [Buf("Pm%d" % i) for i in range(NPB)]
        gn_a = A.alloc([4, 128], F32)
        gn_b = A.alloc([4, 128], F32)
        gn_s = A.alloc([16], F32)
        b_gn = Buf("gn")
        rden = A.alloc([4], F32)
        b_rden = Buf("rden")
        sfin = A.alloc([128], F32)
        b_sfin = Buf("sfin")

        for k in range(8):
            P.dma("sp", lambda e, k=k: e.dma_start(out=wkv[:, k, :], in_=win_bf[k * 128:(k + 1) * 128, 512:1536]),
                  reads=[b_winbf], writes=[b_wkv], sem=b_wkv)

        for i in range(NT):
            j = i % 2
            r0 = s * SEQ + i * 128
            P.dma("sp", lambda e, j=j, r0=r0: e.dma_start(out=xt[j], in_=xp[r0:r0 + 128, :]),
                  writes=[b_xt[j]], sem=b_xt[j])
            rmsnorm_T(xt[j], b_xt[j], xb[j], b_xb[j], junk, b_junk, stat[:, i, :], b_stat[i], j, g1_sb[:],
                      xT[:, :, i * 128:(i + 1) * 128], b_xT[i])

        for i in range(NT):
            j = i % 2
            r0 = s * SEQ + i * 128
            for half in range(2):
                pa = 2 + 2 * j + half
                for k in range(8):
                    P.op("pe", lambda e, pa=pa, k=k, i=i, half=half: e.matmul(
                        pbank[pa][:], lhsT=xT[:, k, i * 128:(i + 1) * 128], rhs=wkv[:, k, half * 512:(half + 1) * 512],
                        start=(k == 0), stop=(k == 7)),
                        reads=[b_xT[i], b_wkv], writes=[b_pb[pa]])
                P.op("act", lambda e, pa=pa, j=j, half=half: e.copy(out=kvst[j][:, half * 512:(half + 1) * 512],
                                                                   in_=pbank[pa][:]),
                     reads=[b_pb[pa]], writes=[b_kvst[j]])
            P.dma("sp", lambda e, j=j, r0=r0: e.dma_start(out=kwp[r0:r0 + 128, :], in_=kvst[j][:, 0:512]),
                  reads=[b_kvst[j]], sem=b_kvst[j])
            P.dma("sp", lambda e, j=j, r0=r0: e.dma_start(out=vwp[r0:r0 + 128, :], in_=kvst[j][:, 512:1024]),
                  reads=[b_kvst[j]], sem=b_kvst[j])

        if stop <= 1:
            return
        for j in range(2):
            P.op("pool", lambda e, j=j: e.memset(Vb[j][:, :, 64:65], 1.0), writes=[b_Vb[j]])
            P.op("pool", lambda e, j=j: e.memset(Vb[j][:, :, 129:130], 1.0), writes=[b_Vb[j]])

        for u in range(8):
            if u >= stop - 10:
                return
            j = u % 2
            is_att = u < 4
            if is_att:
                cols = [u * 128, 512 + u * 128, 1024 + u * 128]
            else:
                r = u - 4
                cols = [1536 + r * 128, 2048 + r * 128, 2560 + r * 128, 3072 + r * 128]
            for m, c0 in enumerate(cols):
                P.dma("sp", lambda e, j=j, m=m, c0=c0: e.dma_start(
                    out=wu[j][:, :, m, :], in_=win_bf[:, c0:c0 + 128].rearrange("(k p) c -> p k c", p=128)),
                    reads=[b_winbf], writes=[b_wu[j]], sem=b_wu[j])
            for tg in range(NQG):
                for m in range(2):
                    pa = 2 + (2 * tg + m) % 4
                    for k in range(8):
                        P.op("pe", lambda e, pa=pa, k=k, m=m, tg=tg, j=j: e.matmul(
                            pbank[pa][:], lhsT=wu[j][:, k, m, :], rhs=xT[:, k, tg * 512:(tg + 1) * 512],
                            start=(k == 0), stop=(k == 7)),
                            reads=[b_wu[j]] + b_xT[4 * tg:4 * tg + 4], writes=[b_pb[pa]])
                    dst = (QT if m == 0 else KT)[j]
                    bd = (b_QT if m == 0 else b_KT)[j]
                    sc = 1.0 if (is_att or m == 0) else 128.0 ** -0.5
                    P.op("act", lambda e, pa=pa, dst=dst, tg=tg, sc=sc: e.activation(
                        out=dst[:, tg * 512:(tg + 1) * 512], in_=pbank[pa][:], func=AF.Copy, scale=sc),
                        reads=[b_pb[pa]], writes=[bd])
            if u == 4 and RSTOP <= 1:
                return
            for i in range(NT):
                pa = 2 + i % 4
                ncol = 128 if is_att else 384
                m0 = 2 if is_att else 1
                for k in range(8):
                    P.op("pe", lambda e, pa=pa, k=k, i=i, j=j, m0=m0, ncol=ncol: e.matmul(
                        pbank[pa][:, 0:ncol], lhsT=xT[:, k, i * 128:(i + 1) * 128],
                        rhs=wu[j][:, k, m0:m0 + ncol // 128, :].rearrange("p m c -> p (m c)"),
                        start=(k == 0), stop=(k == 7)),
                        reads=[b_xT[i], b_wu[j]], writes=[b_pb[pa]])
                if is_att:
                    P.op("act", lambda e, pa=pa, i=i, j=j: e.copy(
                        out=Vb[j][:, i, :].rearrange("p (h c) -> p h c", h=2)[:, :, 0:64],
                        in_=pbank[pa][:, 0:128].rearrange("p (h c) -> p h c", h=2)),
                        reads=[b_pb[pa]], writes=[b_Vb[j]])
                else:
                    r = u - 4
                    P.op("dve", lambda e, pa=pa, i=i, r=r: e.tensor_scalar(
                        out=Kd[:, i, :], in0=pbank[pa][:, 0:128], scalar1=(2.0 if os.environ.get('RTEST') == 'a' else kdec[:, r * NT + i:r * NT + i + 1]),
                        scalar2=None, op0=ALU.mult), reads=[b_pb[pa], b_tab], writes=[b_Kd])
                    P.op("act", lambda e, pa=pa, i=i, j=j: e.copy(out=Vb[j][:, i, 0:128], in_=pbank[pa][:, 128:256]),
                         reads=[b_pb[pa]], writes=[b_Vb[j]])
                    P.op("act", lambda e, pa=pa, i=i: e.activation(out=sg[:, i, :], in_=pbank[pa][:, 256:384], func=(AF.Copy if os.environ.get('RTEST') == 'b' else AF.Silu)),
                         reads=[b_pb[pa]], writes=[b_sg])

            if stop <= 2 or (u == 4 and RSTOP <= 2):
                return
            heads = [0, 1] if is_att else [0]
            for hl in heads:
                if is_att:
                    h = 2 * u + hl
                    prow = slice(hl * 64, hl * 64 + 64)
                    tabsc = -SLOPES[h]
                    mt = multt
                    vw = 65
                else:
                    r = u - 4
                    prow = slice(0, 128)
                    tabsc = LG[r]
                    mt = caust
                    vw = 128
                ej = (2 * u + hl) % 2
                P.op("act", lambda e, tabsc=tabsc: e.activation(out=Etmp, in_=dtab, func=AF.Exp, scale=tabsc),
                     reads=[b_tab], writes=[b_Etmp])
                P.op("dve", lambda e, ej=ej, mt=mt: e.tensor_tensor(out=Et[ej], in0=Etmp, in1=mt, op=ALU.mult),
                     reads=[b_Etmp, b_tab], writes=[b_Et[ej]])
                items = [(qg, kb) for qg in range(NQG) for kb in range(4 * qg + 4)]
                LA = 2

                def front(n, qg, kb, j=j, prow=prow, ej=ej):
                    pa = 2 + n % 4
                    pb_ = n % NPB
                    off = qg * 512 - kb * 128 + 384
                    P.op("pe", lambda e: e.matmul(pbank[pa][:], lhsT=KT[j][prow, kb * 128:(kb + 1) * 128],
                                                  rhs=QT[j][prow, qg * 512:(qg + 1) * 512], start=True, stop=True),
                         reads=[b_KT[j], b_QT[j]], writes=[b_pb[pa]])
                    if is_att:
                        P.op("act", lambda e: e.activation(out=Pe[pb_], in_=pbank[pa][:], func=AF.Exp, scale=0.125),
                             reads=[b_pb[pa]], writes=[b_Pe[pb_]])
                        P.op("dve", lambda e: e.tensor_tensor(out=Pm[pb_], in0=Pe[pb_], in1=Et[ej][:, off:off + 512],
                                                              op=ALU.mult),
                             reads=[b_Pe[pb_], b_Et[ej]], writes=[b_Pm[pb_]])
                    else:
                        P.op("dve", lambda e: e.tensor_tensor(out=Pm[pb_], in0=pbank[pa][:], in1=Et[ej][:, off:off + 512],
                                                              op=ALU.mult),
                             reads=[b_pb[pa], b_Et[ej]], writes=[b_Pm[pb_]])

                def back(n, qg, kb, j=j, hl=hl, vw=vw, u=u):
                    pb_ = n % NPB
                    po = qg % 2
                    ov = pbank[po][:].rearrange("p (q c) -> p q c", q=4)
                    for qb in range(4):
                        QB = 4 * qg + qb
                        if kb > QB:
                            continue
                        if is_att:
                            rhs = Vb[j][:, kb, hl * 65:hl * 65 + 65]
                        else:
                            rhs = Vb[j][:, kb, 0:128]
                        P.op("pe", lambda e, qb=qb, QB=QB, rhs=rhs: e.matmul(
                            ov[:, qb, 0:vw], lhsT=Pm[pb_][:, qb * 128:(qb + 1) * 128], rhs=rhs,
                            start=(kb == 0 and qb == 0), stop=(kb == QB), skip_group_check=True),
                            reads=[b_Pm[pb_], b_Vb[j]], writes=[b_pb[po]])
                    if kb == 4 * qg + 3:
                        bc = b_cat[4 * qg:4 * qg + 4]
                        if is_att:
                            h = 2 * u + hl
                            P.op("dve", lambda e: e.reciprocal(out=rden, in_=ov[:, :, 64]), reads=[b_pb[po]], writes=[b_rden])
                            P.op("dve", lambda e: e.tensor_tensor(
                                out=catTok[:, 4 * qg:4 * qg + 4, h * 64:(h + 1) * 64], in0=ov[:, :, 0:64],
                                in1=rden.unsqueeze(2).broadcast_to([128, 4, 64]), op=ALU.mult),
                                reads=[b_pb[po], b_rden], writes=bc)
                        else:
                            r = u - 4
                            groupnorm(128, ov, b_pb[po],
                                      ggn_bc[:, r * 128:(r + 1) * 128].unsqueeze(1).broadcast_to([128, 4, 128]),
                                      sg[:, 4 * qg:4 * qg + 4, :], b_sg,
                                      catTok[:, 4 * qg:4 * qg + 4, 512 + r * 128:512 + (r + 1) * 128], bc,
                                      gn_a, gn_b, gn_s, b_gn)

                for n in range(len(items) + LA):
                    if n < len(items):
                        front(n, *items[n])
                    if n >= LA:
                        back(n - LA, *items[n - LA])

            if stop <= 3 or (u == 4 and RSTOP <= 3):
                return
            if not is_att:
                r = u - 4
                pa = 2
                for kb in range(NT):
                    P.op("pe", lambda e, kb=kb, j=j: e.matmul(pbank[pa][:, 0:128], lhsT=Kd[:, kb, :], rhs=Vb[j][:, kb, 0:128],
                                                              start=(kb == 0), stop=(kb == NT - 1)),
                         reads=[b_Kd, b_Vb[j]], writes=[b_pb[pa]])
                P.op("act", lambda e: e.copy(out=sfin, in_=pbank[pa][:, 0:128]), reads=[b_pb[pa]], writes=[b_sfin])
                P.dma("sp", lambda e, r=r: e.dma_start(out=rp[s * 4 + r], in_=sfin), reads=[b_sfin], sem=b_sfin)


    def peer_phase(tp, nsub, n_ptiles, cat_src, x_src, y_dst):
        P.barrier()
        A.reset()
        slot = [A.alloc([4096], BF16) for _ in range(4)]
        b_slot = [Buf("slot%d" % i) for i in range(4)]
        su = [sl.rearrange("p (k e) -> p k e", k=8) for sl in slot]
        sv_ = [sl.rearrange("p (c d) -> p c d", c=4) for sl in slot]
        WT = A.alloc([128, 256], BF16)
        b_WT = Buf("WT")
        skT_sb = A.alloc([16, 128], BF16)
        b_skT = Buf("skT")
        P.dma("pool", lambda e: e.dma_start(out=skT_sb, in_=skT.rearrange("p (a b) -> p a b", a=16)),
              writes=[b_skT], sem=b_skT)
        xn2T = A.alloc([8, 256], BF16)
        b_xn2T = [Buf("xn2T%d" % a) for a in range(2)]
        qT = A.alloc([16, 256], BF16)
        b_qT = Buf("qT")
        s_sb = A.alloc([8, 256], F32)
        b_ssb = Buf("s_sb")
        h_sb = [A.alloc([D], F32) for a in range(2)]
        b_h = [Buf("h%d" % a) for a in range(2)]
        xt2 = [A.alloc([D], F32) for a in range(2)]
        b_xt2 = [Buf("xt2%d" % a) for a in range(2)]
        hb = A.alloc([D], BF16)
        b_hb = Buf("hb")
        junk2 = A.alloc([D], BF16)
        b_junk2 = Buf("junk2")
        catT_t = A.alloc([8, 128], BF16)
        b_catT = Buf("catT")
        NGB = 3
        Gt = [A.alloc([256], BF16) for _ in range(NGB)]
        b_Gt = [Buf("Gt%d" % i) for i in range(NGB)]
        Ht = [A.alloc([256], BF16) for _ in range(NGB)]
        b_Ht = [Buf("Ht%d" % i) for i in range(NGB)]
        pk = [A.alloc([256], F32) for _ in range(3)]
        b_pk = Buf("pk")
        tk = [A.alloc([128], F32) for _ in range(3)]
        b_tk = Buf("tk")
        sv = A.alloc([8, 2, 16], F32)
        si = A.alloc([8, 2, 16], U32)
        si_f = A.alloc([8, 2, 16], F32)
        swork = A.alloc([128], F32)
        cand = A.alloc([16, 16], F32)
        cwork = A.alloc([16, 16], F32)
        best = A.alloc([8, 16], F32)
        pos = A.alloc([8, 16], U32)
        pa_u = A.alloc([8, 16], U32)
        pb_u = A.alloc([8, 16], U32)
        pa_f = A.alloc([8, 16], F32)
        pb_f = A.alloc([8, 16], F32)
        oh = A.alloc([16, 16], F32)
        ebest = A.alloc([8, 16], F32)
        esum = A.alloc([16], F32)
        b_tkw = Buf("topk_work")
        NOB = 4
        OI = [A.alloc([128], BF16) for _ in range(NOB)]
        OJ = [A.alloc([128], BF16) for _ in range(NOB)]
        b_OI = [Buf("OI%d" % i) for i in range(NOB)]
        b_OJ = [Buf("OJ%d" % i) for i in range(NOB)]
        stat2 = A.alloc([2, 4], F32)
        b_stat2 = [Buf("stat2%d" % a) for a in range(2)]
        fstat = A.alloc([2, 4], F32)
        b_fstat = [Buf("fstat%d" % a) for a in range(2)]
        identf = A.alloc([128], F32)
        b_identf = Buf("identf")
        P.op("dve", lambda e: e.tensor_copy(out=identf, in_=ident[:]), reads=[b_const], writes=[b_identf])

        for tile in range(n_ptiles):
            T = tp * nsub
            for half in range(2):
                P.dma("sp", lambda e, half=half: e.dma_start(
                    out=su[half], in_=wout_bf[:, half * 512:(half + 1) * 512].rearrange("(k p) c -> p k c", p=128)),
                    reads=[b_woutbf], writes=[b_slot[half]], sem=b_slot[half])
            for a in range(nsub):
                cat_ap, cat_b = cat_src(tile, a)
                x_dr = x_src(tile, a)
                P.dma("sp", lambda e, a=a, x_dr=x_dr: e.dma_start(out=xt2[a][0:tp], in_=x_dr),
                      writes=[b_xt2[a]], sem=b_xt2[a])
                pst = pbank[0][:].bitcast(BF16)
                for k in range(8):
                    P.op("pe", lambda e, k=k, cat_ap=cat_ap: e.transpose(out=pst[:, k * 128:k * 128 + tp],
                                                                        in_=cat_ap[0:tp, k * 128:(k + 1) * 128],
                                                                        identity=ident[0:tp, 0:tp]),
                         reads=[cat_b, b_const], writes=[b_pb[0]])
                P.op("act", lambda e: e.copy(out=catT_t[:, :, 0:tp], in_=pst.rearrange("p (k t) -> p k t", k=8)[:, :, 0:tp]),
                     reads=[b_pb[0]], writes=[b_catT])
                for half in range(2):
                    pa = 2 + half
                    for k in range(8):
                        P.op("pe", lambda e, k=k, half=half, pa=pa: e.matmul(
                            pbank[pa][0:tp, :], lhsT=catT_t[:, k, 0:tp], rhs=su[half][:, k, :], start=(k == 0), stop=(k == 7)),
                            reads=[b_catT, b_slot[half]], writes=[b_pb[pa]])
                    P.op("dve", lambda e, a=a, half=half, pa=pa: e.tensor_tensor(
                        out=h_sb[a][0:tp, half * 512:(half + 1) * 512], in0=pbank[pa][0:tp, :],
                        in1=xt2[a][0:tp, half * 512:(half + 1) * 512], op=ALU.add),
                        reads=[b_pb[pa], b_xt2[a]], writes=[b_h[a]])
                rmsnorm_T(h_sb[a], b_h[a], hb, b_hb, junk2, b_junk2, stat2[:, a, :], b_stat2[a], 1, g2_sb[:],
                          xn2T[:, :, a * tp:(a + 1) * tp], b_xn2T[a], tp=tp)

            for qq in range(4):
                sl = (2 + qq) % 4
                P.dma("sp", lambda e, qq=qq, sl=sl: e.dma_start(
                    out=su[sl], in_=wpq_bf[:, qq * 512:(qq + 1) * 512].rearrange("(k p) c -> p k c", p=128)),
                    reads=[b_wpqbf], writes=[b_slot[sl]], sem=b_slot[sl])
                for cc in range(4):
                    hc = qq * 4 + cc
                    pa = 2 + hc % 4
                    for k in range(8):
                        P.op("pe", lambda e, k=k, cc=cc, sl=sl, pa=pa: e.matmul(
                            pbank[pa][:, 0:T], lhsT=su[sl][:, k, cc * 128:(cc + 1) * 128], rhs=xn2T[:, k, 0:T],
                            start=(k == 0), stop=(k == 7)),
                            reads=[b_slot[sl]] + b_xn2T, writes=[b_pb[pa]])
                    P.op("act", lambda e, hc=hc, pa=pa: e.copy(out=qT[:, hc, 0:T], in_=pbank[pa][:, 0:T]),
                         reads=[b_pb[pa]], writes=[b_qT])

            for a in range(nsub):
                for h in range(8):
                    pa = 2 + (h // 2) % 4
                    for c in range(2):
                        P.op("pe", lambda e, h=h, c=c, a=a, pa=pa: e.matmul(
                            pbank[pa][0:tp, (h % 2) * 256 + c * 128:(h % 2) * 256 + (c + 1) * 128],
                            lhsT=qT[:, 2 * h + c, a * tp:(a + 1) * tp], rhs=skT_sb[:, 2 * h + c, :],
                            start=True, stop=True), reads=[b_qT, b_skT], writes=[b_pb[pa]])
                    if h % 2 == 1:
                        P.op("act", lambda e, h=h, pa=pa: e.copy(
                            out=s_sb[0:tp, h - 1:h + 1, :], in_=pbank[pa][0:tp, :].rearrange("p (h n) -> p h n", h=2)),
                            reads=[b_pb[pa]], writes=[b_ssb])
                W_ = [b_tkw]
                R_ = [b_tkw]
                for h in range(8):
                    for c in range(2):
                        src = s_sb[0:tp, h, c * 128:(c + 1) * 128]
                        P.op("dve", lambda e, h=h, c=c, src=src: e.max(out=sv[0:tp, h, c, 0:8], in_=src), reads=[b_ssb, b_tkw], writes=W_)
                        P.op("dve", lambda e, h=h, c=c, src=src: e.max_index(out=si[0:tp, h, c, 0:8], in_max=sv[0:tp, h, c, 0:8], in_values=src),
                             reads=[b_ssb, b_tkw], writes=W_)
                        P.op("dve", lambda e, h=h, c=c, src=src: e.match_replace(out=swork[0:tp], in_to_replace=sv[0:tp, h, c, 0:8],
                                                                               in_values=src, imm_value=-1e30),
                             reads=[b_ssb, b_tkw], writes=W_)
                        P.op("dve", lambda e, h=h, c=c: e.max(out=sv[0:tp, h, c, 8:16], in_=swork[0:tp]), reads=R_, writes=W_)
                        P.op("dve", lambda e, h=h, c=c: e.max_index(out=si[0:tp, h, c, 8:16], in_max=sv[0:tp, h, c, 8:16], in_values=swork[0:tp]),
                             reads=R_, writes=W_)
                    P.op("dve", lambda e, h=h: e.tensor_tensor(
                        out=cand[0:tp], in0=sv[0:tp, h, 0, :].unsqueeze(2).broadcast_to([tp, 16, 16]),
                        in1=sv[0:tp, h, 1, :].unsqueeze(1).broadcast_to([tp, 16, 16]), op=ALU.add), reads=R_, writes=W_)
                    cf = cand[0:tp].rearrange("p a b -> p (a b)")
                    cwf = cwork[0:tp].rearrange("p a b -> p (a b)")
                    P.op("dve", lambda e, h=h, cf=cf: e.max(out=best[0:tp, h, 0:8], in_=cf), reads=R_, writes=W_)
                    P.op("dve", lambda e, h=h, cf=cf: e.max_index(out=pos[0:tp, h, 0:8], in_max=best[0:tp, h, 0:8], in_values=cf), reads=R_, writes=W_)
                    P.op("dve", lambda e, h=h, cf=cf, cwf=cwf: e.match_replace(out=cwf, in_to_replace=best[0:tp, h, 0:8], in_values=cf, imm_value=-1e30),
                         reads=R_, writes=W_)
                    P.op("dve", lambda e, h=h, cwf=cwf: e.max(out=best[0:tp, h, 8:16], in_=cwf), reads=R_, writes=W_)
                    P.op("dve", lambda e, h=h, cwf=cwf: e.max_index(out=pos[0:tp, h, 8:16], in_max=best[0:tp, h, 8:16], in_values=cwf), reads=R_, writes=W_)
                P.op("dve", lambda e: e.tensor_copy(out=si_f[0:tp], in_=si[0:tp]), reads=R_, writes=W_)
                P.op("dve", lambda e: e.tensor_single_scalar(out=pa_u[0:tp], in_=pos[0:tp], scalar=4, op=ALU.logical_shift_right), reads=R_, writes=W_)
                P.op("dve", lambda e: e.tensor_single_scalar(out=pb_u[0:tp], in_=pos[0:tp], scalar=15, op=ALU.bitwise_and), reads=R_, writes=W_)
                P.op("dve", lambda e: e.tensor_copy(out=pa_f[0:tp], in_=pa_u[0:tp]), reads=R_, writes=W_)
                P.op("dve", lambda e: e.tensor_copy(out=pb_f[0:tp], in_=pb_u[0:tp]), reads=R_, writes=W_)
                for which, pf in ((0, pa_f), (1, pb_f)):
                    for h in range(8):
                        P.op("dve", lambda e, h=h, pf=pf: e.tensor_tensor(
                            out=oh[0:tp], in0=iota[0:tp, 0:16].unsqueeze(1).broadcast_to([tp, 16, 16]),
                            in1=pf[0:tp, h, :].unsqueeze(2).broadcast_to([tp, 16, 16]), op=ALU.is_equal),
                            reads=[b_tkw, b_const], writes=W_)
                        P.op("dve", lambda e, h=h, which=which: e.tensor_tensor(
                            out=oh[0:tp], in0=oh[0:tp],
                            in1=si_f[0:tp, h, which, :].unsqueeze(1).broadcast_to([tp, 16, 16]), op=ALU.mult),
                            reads=R_, writes=W_)
                        P.op("dve", lambda e, which=which, h=h: e.tensor_reduce(
                            out=tk[which][0:tp, h * 16:(h + 1) * 16], in_=oh[0:tp], axis=AX.X, op=ALU.add),
                            reads=R_, writes=[b_tk])
                P.op("dve", lambda e: e.tensor_tensor(out=ebest[0:tp], in0=best[0:tp], in1=best[0:tp, :, 0:1].broadcast_to([tp, 8, 16]),
                                                      op=ALU.subtract), reads=R_, writes=W_)
                P.op("act", lambda e: e.activation(out=ebest[0:tp], in_=ebest[0:tp], func=AF.Exp), reads=R_, writes=W_)
                P.op("dve", lambda e: e.tensor_reduce(out=esum[0:tp, 0:8], in_=ebest[0:tp], axis=AX.X, op=ALU.add), reads=R_, writes=W_)
                P.op("dve", lambda e: e.reciprocal(out=esum[0:tp, 8:16], in_=esum[0:tp, 0:8]), reads=R_, writes=W_)
                P.op("dve", lambda e: e.tensor_tensor(out=tk[2][0:tp].rearrange("p (h k) -> p h k", h=8), in0=ebest[0:tp],
                                                      in1=esum[0:tp, 8:16].unsqueeze(2).broadcast_to([tp, 8, 16]), op=ALU.mult),
                     reads=R_, writes=[b_tk])
                pt = pbank[1][:]
                for w3 in range(3):
                    P.op("pe", lambda e, w3=w3: e.transpose(out=pt[:, w3 * 128:w3 * 128 + tp], in_=tk[w3][0:tp], identity=identf[0:tp, 0:tp]),
                         reads=[b_tk, b_identf], writes=[b_pb[1]])
                for w3 in range(3):
                    P.op("act", lambda e, w3=w3, a=a: e.copy(out=pk[w3][:, a * tp:(a + 1) * tp], in_=pt[:, w3 * 128:w3 * 128 + tp]),
                         reads=[b_pb[1]], writes=[b_pk])

            for t in range(T):
                ob = t % NOB
                pw = 6 + (t // 4) % 2
                P.op("pool", lambda e, t=t, ob=ob: e.tensor_scalar(out=OI[ob], in0=iota[:], scalar1=pk[0][:, t:t + 1], scalar2=None,
                                                                  op0=ALU.is_equal), reads=[b_pk, b_const], writes=[b_OI[ob]])
                P.op("dve", lambda e, t=t, ob=ob: e.tensor_scalar(out=OJ[ob], in0=iota[:], scalar1=pk[1][:, t:t + 1],
                                                                 scalar2=pk[2][:, t:t + 1], op0=ALU.is_equal, op1=ALU.mult),
                     reads=[b_pk, b_const], writes=[b_OJ[ob]])
                P.op("pe", lambda e, t=t, ob=ob, pw=pw: e.matmul(pbank[pw][:, (t % 4) * 128:(t % 4 + 1) * 128], lhsT=OJ[ob], rhs=OI[ob],
                                                                start=True, stop=True),
                     reads=[b_OI[ob], b_OJ[ob]], writes=[b_pb[pw]])
                if t % 4 == 3:
                    t0 = t - 3
                    P.op("act", lambda e, t0=t0, pw=pw: e.copy(
                        out=WT[:, :, t0:t0 + 4], in_=pbank[pw][:].rearrange("p (t i) -> p i t", t=4)),
                        reads=[b_pb[pw]], writes=[b_WT])

            items = [(g, c) for g in range(NG) for c in range(4)]
            LA = 2

            def pfront(n, g, c):
                i = g * 4 + c
                us, vs = g % 2, 2 + g % 2
                if c == 0:
                    P.dma("sp", lambda e: e.dma_start(
                        out=su[us], in_=uT_bf[:, g * 512:(g + 1) * 512].rearrange("(k p) e -> p k e", p=128)),
                        reads=[b_uTbf], writes=[b_slot[us]], sem=b_slot[us])
                    P.dma("sp", lambda e: e.dma_start(
                        out=sv_[vs], in_=pv_bf[g * 512:(g + 1) * 512, :].rearrange("(c j) d -> j c d", j=128)),
                        reads=[b_pvbf], writes=[b_slot[vs]], sem=b_slot[vs])
                pa = 6 + n % 2
                gb = n % NGB
                for k in range(8):
                    P.op("pe", lambda e, k=k: e.matmul(pbank[pa][:, 0:T], lhsT=su[us][:, k, c * 128:(c + 1) * 128], rhs=xn2T[:, k, 0:T],
                                                       start=(k == 0), stop=(k == 7)),
                         reads=[b_slot[us]] + b_xn2T, writes=[b_pb[pa]])
                P.op("act", lambda e: e.activation(out=Gt[gb][:, 0:T], in_=pbank[pa][:, 0:T], func=GELU), reads=[b_pb[pa]], writes=[b_Gt[gb]])
                P.op("dve", lambda e: e.tensor_tensor(out=Ht[gb][:, 0:T], in0=Gt[gb][:, 0:T], in1=WT[:, i, 0:T], op=ALU.mult),
                     reads=[b_Gt[gb], b_WT], writes=[b_Ht[gb]])

            def pback(n, g, c):
                i = g * 4 + c
                vs = 2 + g % 2
                gb = n % NGB
                for a in range(nsub):
                    for half in range(2):
                        pa = 2 + a * 2 + half
                        P.op("pe", lambda e, a=a, half=half, pa=pa: e.matmul(
                            pbank[pa][0:tp, :], lhsT=Ht[gb][:, a * tp:(a + 1) * tp], rhs=sv_[vs][:, c, half * 512:(half + 1) * 512],
                            start=(i == 0), stop=(i == NI - 1)),
                            reads=[b_Ht[gb], b_slot[vs]], writes=[b_pb[pa]])

            for n in range(len(items) + LA):
                if n < len(items):
                    pfront(n, *items[n])
                if n >= LA:
                    pback(n - LA, *items[n - LA])

            for a in range(nsub):
                y_dr = y_dst(tile, a)
                for half in range(2):
                    pa = 2 + a * 2 + half
                    P.op("dve", lambda e, a=a, half=half, pa=pa: e.tensor_tensor(
                        out=h_sb[a][0:tp, half * 512:(half + 1) * 512], in0=pbank[pa][0:tp, :],
                        in1=h_sb[a][0:tp, half * 512:(half + 1) * 512], op=ALU.add),
                        reads=[b_pb[pa], b_h[a]], writes=[b_h[a]])
                fs = fstat[0:tp, a, :]
                P.op("act", lambda e, a=a, fs=fs: e.activation(out=junk2[0:tp], in_=h_sb[a][0:tp], func=AF.Square, accum_out=fs[:, 0:1]),
                     reads=[b_h[a]], writes=[b_junk2, b_fstat[a]])
                P.op("dve", lambda e, fs=fs: e.tensor_scalar(out=fs[:, 1:2], in0=fs[:, 0:1], scalar1=1.0 / D, scalar2=1e-6,
                                                            op0=ALU.mult, op1=ALU.add), reads=[b_fstat[a]], writes=[b_fstat[a]])
                P.op("act", lambda e, fs=fs: e.activation(out=fs[:, 2:3], in_=fs[:, 1:2], func=AF.Sqrt),
                     reads=[b_fstat[a]], writes=[b_fstat[a]])
                P.op("dve", lambda e, fs=fs: e.reciprocal(out=fs[:, 3:4], in_=fs[:, 2:3]), reads=[b_fstat[a]], writes=[b_fstat[a]])
                P.op("dve", lambda e, a=a, fs=fs: e.scalar_tensor_tensor(out=xt2[a][0:tp], in0=h_sb[a][0:tp], scalar=fs[:, 3:4], in1=gf_bc[0:tp, :],
                                                                       op0=ALU.mult, op1=ALU.mult),
                     reads=[b_h[a], b_fstat[a], b_const], writes=[b_xt2[a]])
                P.dma("sp", lambda e, a=a, y_dr=y_dr: e.dma_start(out=y_dr, in_=xt2[a][0:tp]),
                      reads=[b_xt2[a]], sem=b_xt2[a])

    for s in range(NSEQ):
        prompt_mixer(s)
        if do_peer:
            def rows(tile, a, s=s):
                r0 = s * SEQ + (tile * 2 + a) * 128
                return slice(r0, r0 + 128)
            peer_phase(128, 2, SEQ // 256,
                       lambda tile, a: (catTok[:, tile * 2 + a, :], b_cat[tile * 2 + a]),
                       lambda tile, a, rows=rows: xp[rows(tile, a), :],
                       lambda tile, a, rows=rows: yp[rows(tile, a), :])
    if do_sample:
        sample_mixer()
        if do_peer:
            peer_phase(NS, 1, 1, lambda tile, a: (cat_s[:], b_cats), lambda tile, a: xs, lambda tile, a: ys)

    P.barrier()
    P.emit(nc, es)
    es.close()
    return nc


_NC_CACHE = {}


def kernel(x_prompt, x_sample, cache_k_win, cache_v_win, state_ret, norm1_g, w_in, ret_gn_g, w_out,
           norm2_g, w_pq, peer_sub_keys, peer_u, peer_v, norm_f_g):
    f32 = np.float32
    if "nc" not in _NC_CACHE:
        _NC_CACHE["nc"] = build_program()
    nc = _NC_CACHE["nc"]
    x_prompt = np.asarray(x_prompt, f32)
    x_sample = np.asarray(x_sample, f32)
    ck = np.asarray(cache_k_win, f32)[0].reshape(128, 2048, 512)
    cv = np.asarray(cache_v_win, f32)[0].reshape(128, 2048, 512)
    stt = np.asarray(state_ret, f32)[0].reshape(128 * 4, 128, 128)
    shared = {
        "g1": np.ascontiguousarray(np.asarray(norm1_g, f32)[0].reshape(8, 128).T),
        "g2": np.ascontiguousarray(np.asarray(norm2_g, f32)[0].reshape(8, 128).T),
        "gf": np.asarray(norm_f_g, f32).reshape(1, D),
        "ggn": np.asarray(ret_gn_g, f32).reshape(1, 512),
        "w_in": np.asarray(w_in, f32)[0],
        "w_out": np.asarray(w_out, f32)[0],
        "w_pq": np.asarray(w_pq, f32)[0],
        "skT": np.ascontiguousarray(np.asarray(peer_sub_keys, f32)[0].reshape(16, 128, 128).transpose(2, 0, 1)).reshape(128, 2048),
        "uT": np.ascontiguousarray(np.asarray(peer_u, f32)[0].T),
        "pv": np.asarray(peer_v, f32)[0],
    }
    shared.update(make_tables(SEQ))
    in_maps = []
    for c in range(NCORES):
        m = dict(shared)
        m["xp"] = x_prompt[2 * c:2 * c + 2].reshape(NSEQ * SEQ, D)
        m["xs"] = x_sample[16 * c:16 * c + 16, 0, :]
        m["ck"] = ck[16 * c:16 * c + 16]
        m["cv"] = cv[16 * c:16 * c + 16]
        m["st"] = stt[64 * c:64 * c + 64]
        in_maps.append(m)
    res = run_bass_kernel_spmd(nc, in_maps, core_ids=list(range(NCORES)))
    R = res.results
    cat = lambda k: np.concatenate([np.asarray(R[c][k], f32) for c in range(NCORES)], axis=0)
    y_prompt = cat("yp").reshape(16, SEQ, D)
    y_sample = cat("ys").reshape(128, 1, D)
    k_win = cat("kwp").reshape(1, 16, SEQ, 8, 64)
    v_win = cat("vwp").reshape(1, 16, SEQ, 8, 64)
    ret_p = cat("rp").reshape(1, 16, 4, 128, 128)
    k_new = cat("kns").reshape(1, 128, 1, 8, 64)
    v_new = cat("vns").reshape(1, 128, 1, 8, 64)
    ret_s = cat("rs").reshape(1, 128, 4, 128, 128)
    return (y_prompt, y_sample, k_win, v_win, ret_p, k_new, v_new, ret_s)
```

```python
import numpy as np
import ml_dtypes
from contextlib import ExitStack
import concourse.bass as bass
import concourse.mybir as mybir
from concourse.bass_utils import run_bass_kernel_spmd

F32 = mybir.dt.float32
BF16 = mybir.dt.bfloat16
I32 = mybir.dt.int32
U32 = mybir.dt.uint32
AF = mybir.ActivationFunctionType
ALU = mybir.AluOpType
AX = mybir.AxisListType

NCORES = 8
D = 1024
SEQ = 2048
NSEQ = 2
NS = 16
INC = 3584
NT = SEQ // 128

ENGS = ("pe", "act", "dve", "pool", "sp")
SAME_ENGINE_SYNC = True


class Buf:
    __slots__ = ("name", "w", "r", "dsem", "dcnt", "excl")

    def __init__(self, name, excl=False):
        self.name = name
        self.excl = excl
        self.w = None
        self.r = {}
        self.dsem = None
        self.dcnt = 0


class Prog:
    def __init__(self):
        self.q = {e: [] for e in ENGS}
        self.cnt = {e: 0 for e in ENGS}
        self.waited = {e: {} for e in ENGS}
        self.ndsem = 0
        self.dtotal = {}
        self.needed = {e: set() for e in ENGS}

    def _deps(self, eng, reads, writes, is_dma_sem=None):
        evs = []
        for b in reads:
            if b.w is not None:
                evs.append(b.w)
            if b.excl:
                evs.extend((k, v) for (k, v) in b.r.items() if k != eng)
        for b in writes:
            if b.w is not None:
                if not (is_dma_sem is not None and b.w[0] == is_dma_sem):
                    evs.append(b.w)
            evs.extend(b.r.items())
        waits = {}
        for (s, v) in evs:
            if s == eng and (eng == "pe" or not SAME_ENGINE_SYNC):
                continue
            if s == is_dma_sem and False:
                continue
            if self.waited[eng].get(s, 0) >= v:
                continue
            if waits.get(s, 0) < v:
                waits[s] = v
        for s, v in waits.items():
            self.waited[eng][s] = v
            if s in ENGS:
                self.needed[s].add(v)
        return list(waits.items())

    def op(self, eng, fn, reads=(), writes=()):
        waits = self._deps(eng, reads, writes)
        self.cnt[eng] += 1
        ev = (eng, self.cnt[eng])
        self.q[eng].append((waits, fn, None, self.cnt[eng]))
        for b in reads:
            b.r[ev[0]] = ev[1]
        for b in writes:
            b.w = ev
            b.r = {}

    def dma(self, eng, fn, reads=(), writes=(), sem=None):
        if sem.dsem is None:
            sem.dsem = ("d", self.ndsem)
            self.ndsem += 1
        key = sem.dsem
        waits = self._deps(eng, reads, writes, is_dma_sem=key)
        sem.dcnt += 16
        self.dtotal[key] = sem.dcnt
        ev = (key, sem.dcnt)
        self.q[eng].append((waits, fn, key, None))
        for b in reads:
            b.r[ev[0]] = ev[1]
        for b in writes:
            b.w = ev
            b.r = {}

    def barrier(self):
        for e in ENGS:
            waits = {}
            for o in ENGS:
                if o != e and self.cnt[o] > 0 and self.waited[e].get(o, 0) < self.cnt[o]:
                    waits[o] = self.cnt[o]
            for k, v in self.dtotal.items():
                if self.waited[e].get(k, 0) < v:
                    waits[k] = v
            for s, v in waits.items():
                self.waited[e][s] = v
                if s in ENGS:
                    self.needed[s].add(v)
            if waits:
                self.q[e].append((list(waits.items()), None, None, None))

    def emit(self, nc, es):
        esem = {e: es.enter_context(nc.semaphore("prog_" + e)) for e in ENGS}
        dsem = {("d", i): es.enter_context(nc.semaphore("dma%d" % i)) for i in range(self.ndsem)}
        rank = {}
        for e in ENGS:
            srt = sorted(self.needed[e])
            rank[e] = {v: i + 1 for i, v in enumerate(srt)}
        block = es.enter_context(nc.Block())

        def run(e, engobj):
            for (waits, fn, dkey, seq) in self.q[e]:
                for (s, v) in waits:
                    if s in ENGS:
                        engobj.wait_ge(esem[s], rank[s][v])
                    else:
                        engobj.wait_ge(dsem[s], v)
                if fn is None:
                    continue
                ins = fn(engobj)
                if dkey is not None:
                    ins.then_inc(dsem[dkey], 16)
                elif seq in rank[e]:
                    ins.then_inc(esem[e], 1)

        @block.tensor
        def _(eng):
            run("pe", eng)

        @block.scalar
        def _(eng):
            run("act", eng)

        @block.vector
        def _(eng):
            run("dve", eng)

        @block.gpsimd
        def _(eng):
            run("pool", eng)

        @block.sync
        def _(eng):
            run("sp", eng)


def _prod(xs):
    n = 1
    for x in xs:
        n *= int(x)
    return n


_DTSIZE = {F32: 4, BF16: 2, U32: 4, I32: 4}


class Arena:
    def __init__(self, t, n16):
        self.t = t
        self.n = n16
        self.off = 0

    def reset(self, off=0):
        self.off = off

    def alloc(self, shape, dt):
        n = _prod(shape)
        e16 = (n * _DTSIZE[dt] + 1) // 2
        a = self.off
        self.off += (e16 + 15) // 16 * 16
        assert self.off <= self.n, ("arena overflow", self.off, self.n)
        v = self.t[:, a:a + e16]
        if dt != BF16:
            v = v.bitcast(dt)
        if len(shape) == 2:
            v = v.rearrange("p (a b) -> p a b", a=shape[0])
        elif len(shape) == 3:
            v = v.rearrange("p (a b c) -> p a b c", a=shape[0], b=shape[1])
        elif len(shape) == 4:
            v = v.rearrange("p (a b c d) -> p a b c d", a=shape[0], b=shape[1], c=shape[2])
        return v


SLOPES = [float(2.0 ** (-8.0 * (h + 1) / 8)) for h in range(8)]
GELU = AF.Gelu_apprx_tanh
RSTOP = 99
LG = [float(np.log(1.0 - 2.0 ** (-5.0 - h)).astype(np.float32)) for h in range(4)]
GAM = [float(np.exp(np.float32(x))) for x in LG]


def make_tables(SEQ):
    f32 = np.float32
    TW = SEQ + 384
    NT = SEQ // 128
    r = np.arange(128)[:, None]
    x = np.arange(TW)[None, :]
    d = x - r - 384
    dpos = np.maximum(d, 0).astype(f32)
    mult = (((d >= 0) & (d <= 128)).astype(f32) + ((d >= 0) & (d % 4 == 0) & (d <= 512)).astype(f32)
            + ((d >= 0) & (d % 16 == 0) & (d <= 2048)).astype(f32))
    caus = (d >= 0).astype(f32)
    pos = np.arange(NT)[None, :] * 128 + np.arange(128)[:, None]
    kdec = np.stack([np.exp((SEQ - 1 - pos) * LG[h]) * (128.0 ** -0.5) for h in range(4)], 1)
    rr = np.arange(128)
    sbias = np.zeros((128, 3, 8), f32)
    for g, dil in enumerate((1, 4, 16)):
        for h in range(8):
            sbias[:, g, h] = -SLOPES[h] * dil * (128 - rr)
    return {
        "dtab": dpos.astype(f32),
        "multt": mult.astype(ml_dtypes.bfloat16),
        "caust": caus.astype(ml_dtypes.bfloat16),
        "kdec": kdec.reshape(128, 4 * NT).astype(f32),
        "sbias": sbias.reshape(128, 24),
        "ident": np.eye(128, dtype=f32).astype(ml_dtypes.bfloat16),
        "iota": np.tile(np.arange(128, dtype=f32)[None, :], (128, 1)),
        "selrow": np.repeat(np.eye(16, dtype=f32), 128, axis=1).reshape(16, 2048).copy(),
        "sel16": np.repeat(np.eye(16, dtype=f32), 128, axis=0).reshape(16, 128, 16).transpose(1, 0, 2).reshape(128, 256).copy(),
    }


def build_program(SEQ=2048, NSEQ=2, NS=16, do_sample=True, do_peer=True, NE=16384, stop=99):
    nc = bass.Bass("TRN2", target_bir_lowering=False)
    es = ExitStack()
    P = Prog()
    NT = SEQ // 128
    NQG = SEQ // 512
    TW = SEQ + 384
    NTOK = NSEQ * SEQ
    NI = NE // 128
    NG = NI // 4

    def din(name, shape, dt=F32):
        return nc.dram_tensor(name, list(shape), dt, kind="ExternalInput").ap()

    def dout(name, shape, dt=F32):
        return nc.dram_tensor(name, list(shape), dt, kind="ExternalOutput").ap()

    def dscr(name, shape, dt):
        return nc.dram_tensor(name, list(shape), dt).ap()

    xp = din("xp", [NTOK, D])
    xs = din("xs", [NS, D])
    ck = din("ck", [NS, 2048, 512])
    cv = din("cv", [NS, 2048, 512])
    st = din("st", [NS * 4, 128, 128])
    g1 = din("g1", [128, 8])
    g2 = din("g2", [128, 8])
    gf = din("gf", [1, D])
    ggn = din("ggn", [1, 512])
    w_in = din("w_in", [D, INC])
    w_out = din("w_out", [D, D])
    w_pq = din("w_pq", [D, 2048])
    skT = din("skT", [128, 16 * 128])
    uT = din("uT", [D, NE])
    pv = din("pv", [NE, D])
    ident_d = din("ident", [128, 128], BF16)
    iota_d = din("iota", [128, 128])
    dtab_d = din("dtab", [128, TW])
    multt_d = din("multt", [128, TW], BF16)
    caust_d = din("caust", [128, TW], BF16)
    kdec_d = din("kdec", [128, 4 * NT])
    sbias_d = din("sbias", [128, 24])
    sel16_d = din("sel16", [128, 256])
    selrow_d = din("selrow", [16, 16 * 128])

    yp = dout("yp", [NTOK, D])
    ys = dout("ys", [NS, D])
    kwp = dout("kwp", [NTOK, 512])
    vwp = dout("vwp", [NTOK, 512])
    rp = dout("rp", [NSEQ * 4, 128, 128])
    kns = dout("kns", [NS, 512])
    vns = dout("vns", [NS, 512])
    rs = dout("rs", [NS * 4, 128, 128])

    uT_bf = dscr("uT_bf", [NE // 512, 128, 8 * 512], BF16)
    pv_bf = dscr("pv_bf", [NE // 512, 128, 4 * 1024], BF16)
    wout_bf = dscr("wout_bf", [D, D], BF16)
    wpq_bf = dscr("wpq_bf", [D, 2048], BF16)
    win_bf = dscr("win_bf", [D, INC], BF16)
    b_winbf = Buf("win_bf")
    b_uTbf = Buf("uT_bf")
    b_pvbf = Buf("pv_bf")
    b_woutbf = Buf("wout_bf")
    b_wpqbf = Buf("wpq_bf")

    def sb(name, shape, dt):
        return es.enter_context(nc.sbuf_tensor(name, list(shape), dt))

    def ps(name, shape, dt):
        return es.enter_context(nc.psum_tensor(name, list(shape), dt))

    ident = sb("ident_sb", [128, 128], BF16)
    iota = sb("iota_sb", [128, 128], F32)
    g1_sb = sb("g1_sb", [128, 8], F32)
    g2_sb = sb("g2_sb", [128, 8], F32)
    gf_bc = sb("gf_bc", [128, D], F32)
    ggn_bc = sb("ggn_bc", [128, 512], F32)
    catTok = sb("catTok", [128, NT, D], BF16)
    b_const = Buf("const")
    b_cat = [Buf("cat%d" % i) for i in range(NT)]
    for (dst, src) in ((ident[:], ident_d), (iota[:], iota_d), (g1_sb[:], g1), (g2_sb[:], g2),
                       (gf_bc[:], gf.broadcast_to([128, D])), (ggn_bc[:], ggn.broadcast_to([128, 512]))):
        P.dma("sp", lambda e, dst=dst, src=src: e.dma_start(out=dst, in_=src), writes=[b_const], sem=b_const)

    for r0 in range(0, D, 128):
        P.dma("pool", lambda e, r0=r0: e.dma_start(out=win_bf[r0:r0 + 128, :], in_=w_in[r0:r0 + 128, :]),
              writes=[b_winbf], sem=b_winbf)
    if do_peer:
        RC = 128
        for r0 in range(0, D, RC):
            P.dma("pool", lambda e, r0=r0: e.dma_start(out=wout_bf[r0:r0 + RC, :], in_=w_out[r0:r0 + RC, :]),
                  writes=[b_woutbf], sem=b_woutbf)
            P.dma("pool", lambda e, r0=r0: e.dma_start(out=wpq_bf[r0:r0 + RC, :], in_=w_pq[r0:r0 + RC, :]),
                  writes=[b_wpqbf], sem=b_wpqbf)
        for g in range(NE // 512):
            for k in range(8):
                P.dma("pool", lambda e, g=g, k=k: e.dma_start(out=uT_bf[g, :, k * 512:(k + 1) * 512],
                                                             in_=uT[k * 128:(k + 1) * 128, g * 512:(g + 1) * 512]),
                      writes=[b_uTbf], sem=b_uTbf)
            for c in range(4):
                P.dma("pool", lambda e, g=g, c=c: e.dma_start(out=pv_bf[g, :, c * 1024:(c + 1) * 1024],
                                                             in_=pv[(g * 4 + c) * 128:(g * 4 + c + 1) * 128, :]),
                      writes=[b_pvbf], sem=b_pvbf)

    pbank = [ps("pb%d" % i, [128, 512], F32) for i in range(8)]
    b_pb = [Buf("pb%d" % i, excl=True) for i in range(8)]

    AR16 = (nc.sbuf_bytes_remaining - 2048) // 2 // 16 * 16
    arena_t = sb("arena", [128, AR16], BF16)
    A = Arena(arena_t, AR16)

    def rmsnorm_T(src_ap, b_src, xb_ap, b_xb, junk_ap, b_junk, stat_ap, b_stat, pst_i, gain_ap, dst_ap, b_dst, tp=128):
        src_ap = src_ap[0:tp]
        xb_ap = xb_ap[0:tp]
        junk_ap = junk_ap[0:tp]
        stat_ap = stat_ap[0:tp]
        P.op("act", lambda e: e.activation(out=junk_ap, in_=src_ap, func=AF.Square, accum_out=stat_ap[:, 0:1]),
             reads=[b_src], writes=[b_junk, b_stat])
        P.op("dve", lambda e: e.tensor_scalar(out=stat_ap[:, 1:2], in0=stat_ap[:, 0:1], scalar1=1.0 / D, scalar2=1e-6,
                                              op0=ALU.mult, op1=ALU.add), reads=[b_stat], writes=[b_stat])
        P.op("act", lambda e: e.activation(out=stat_ap[:, 2:3], in_=stat_ap[:, 1:2], func=AF.Sqrt),
             reads=[b_stat], writes=[b_stat])
        P.op("dve", lambda e: e.reciprocal(out=stat_ap[:, 3:4], in_=stat_ap[:, 2:3]), reads=[b_stat], writes=[b_stat])
        P.op("dve", lambda e: e.tensor_scalar(out=xb_ap, in0=src_ap, scalar1=stat_ap[:, 3:4], scalar2=None, op0=ALU.mult),
             reads=[b_src, b_stat], writes=[b_xb])
        pst = pbank[pst_i][:].bitcast(BF16)
        for k in range(8):
            P.op("pe", lambda e, k=k: e.transpose(out=pst[:, k * 128:k * 128 + tp], in_=xb_ap[:, k * 128:(k + 1) * 128],
                                                  identity=ident[0:tp, 0:tp]),
                 reads=[b_xb, b_const], writes=[b_pb[pst_i]])
        P.op("dve", lambda e: e.tensor_tensor(out=dst_ap, in0=pst.rearrange("p (k t) -> p k t", k=8)[:, :, 0:tp],
                                              in1=gain_ap.unsqueeze(2).broadcast_to([128, 8, tp]), op=ALU.mult),
             reads=[b_pb[pst_i], b_const], writes=[b_dst])

    def groupnorm(tp, src, b_src, gain_ap, gate_ap, b_gate, out_ap, b_outs, gn_a, gn_b, gn_s, b_gn):
        ga, gb_, gs = gn_a[0:tp], gn_b[0:tp], gn_s[0:tp]
        P.op("dve", lambda e: e.tensor_reduce(out=gs[:, 0:4], in_=src, axis=AX.X, op=ALU.add), reads=[b_src], writes=[b_gn])
        P.op("dve", lambda e: e.tensor_scalar(out=gs[:, 0:4], in0=gs[:, 0:4], scalar1=1.0 / 128, scalar2=None, op0=ALU.mult),
             reads=[b_gn], writes=[b_gn])
        P.op("dve", lambda e: e.tensor_tensor(out=ga, in0=src, in1=gs[:, 0:4].unsqueeze(2).broadcast_to([tp, 4, 128]),
                                              op=ALU.subtract), reads=[b_src, b_gn], writes=[b_gn])
        P.op("dve", lambda e: e.tensor_tensor(out=gb_, in0=ga, in1=ga, op=ALU.mult), reads=[b_gn], writes=[b_gn])
        P.op("dve", lambda e: e.tensor_reduce(out=gs[:, 4:8], in_=gb_, axis=AX.X, op=ALU.add), reads=[b_gn], writes=[b_gn])
        P.op("dve", lambda e: e.tensor_scalar(out=gs[:, 4:8], in0=gs[:, 4:8], scalar1=1.0 / 128, scalar2=1e-5,
                                              op0=ALU.mult, op1=ALU.add), reads=[b_gn], writes=[b_gn])
        P.op("act", lambda e: e.activation(out=gs[:, 8:12], in_=gs[:, 4:8], func=AF.Sqrt), reads=[b_gn], writes=[b_gn])
        P.op("dve", lambda e: e.reciprocal(out=gs[:, 12:16], in_=gs[:, 8:12]), reads=[b_gn], writes=[b_gn])
        P.op("dve", lambda e: e.tensor_tensor(out=ga, in0=ga, in1=gs[:, 12:16].unsqueeze(2).broadcast_to([tp, 4, 128]),
                                              op=ALU.mult), reads=[b_gn], writes=[b_gn])
        P.op("dve", lambda e: e.tensor_tensor(out=ga, in0=ga, in1=gain_ap, op=ALU.mult), reads=[b_gn, b_const], writes=[b_gn])
        P.op("dve", lambda e: e.tensor_tensor(out=out_ap, in0=ga, in1=gate_ap, op=ALU.mult), reads=[b_gn, b_gate], writes=b_outs)

    cat_s = catTok[:, 0, :]
    b_cats = Buf("cat_s")

    def sample_mixer():
        tp = NS
        P.barrier()
        A.reset()
        xs_sb = A.alloc([D], F32)
        b_xs = Buf("xs")
        xsb = A.alloc([D], BF16)
        b_xsb = Buf("xsb")
        junk = A.alloc([D], BF16)
        b_junk = Buf("junk_s")
        stat_s = A.alloc([4], F32)
        b_stat_s = Buf("stat_s")
        xsT = A.alloc([8, 16], BF16)
        b_xsT = Buf("xsT")
        wblk = [A.alloc([8, 512], BF16) for _ in range(2)]
        b_wblk = [Buf("wblk%d" % i) for i in range(2)]
        z = A.alloc([INC], F32)
        b_z = Buf("z")
        selcol = A.alloc([16, 16], F32)
        selrow = A.alloc([16 * 128], F32)
        sbias_sb = A.alloc([3, 8], F32)
        identf = A.alloc([128], F32)
        b_sc = Buf("sconst")
        kg = [A.alloc([512], F32) for _ in range(2)]
        b_kg = [Buf("kg%d" % i) for i in range(2)]
        vg_ = [A.alloc([512], F32) for _ in range(2)]
        b_vg = [Buf("vg%d" % i) for i in range(2)]
        prod = A.alloc([512], F32)
        b_prod = Buf("prod")
        sc = A.alloc([8], F32)
        b_scr = Buf("sc")
        p_ = [A.alloc([8], F32) for _ in range(2)]
        b_p = [Buf("p%d" % i) for i in range(2)]
        pvt = [A.alloc([512], F32) for _ in range(2)]
        b_pvt = [Buf("pvt%d" % i) for i in range(2)]
        tmp16 = A.alloc([512], F32)
        sself = A.alloc([8], F32)
        pself = A.alloc([8], F32)
        den = A.alloc([8], F32)
        rden = A.alloc([8], F32)
        b_tm = Buf("tok_misc")
        catf = A.alloc([D], F32)
        b_catf = Buf("catf")
        rqT = A.alloc([4, 16], F32)
        b_rqT = Buf("rqT")
        qm = A.alloc([4, 16, 16], F32)
        b_qm = Buf("qm")
        S_sb = [A.alloc([128], F32) for _ in range(2)]
        b_S = [Buf("S%d" % i) for i in range(2)]
        sout = [A.alloc([128], F32) for _ in range(2)]
        b_sout = [Buf("sout%d" % i) for i in range(2)]
        kmask = [A.alloc([128], F32) for _ in range(2)]
        b_kmask = [Buf("kmask%d" % i) for i in range(2)]
        ro = A.alloc([4, 128], F32)
        b_ro = Buf("ro")
        qk = A.alloc([4], F32)
        sgs = A.alloc([4, 128], F32)
        b_sgs = Buf("sgs")
        gn_a = A.alloc([4, 128], F32)
        gn_b = A.alloc([4, 128], F32)
        gn_s = A.alloc([16], F32)
        b_gn = Buf("gn_s")

        P.dma("sp", lambda e: e.dma_start(out=selcol, in_=sel16_d.rearrange("p (a b) -> p a b", a=16)), writes=[b_sc], sem=b_sc)
        P.dma("sp", lambda e: e.dma_start(out=selrow[0:16], in_=selrow_d), writes=[b_sc], sem=b_sc)
        P.dma("sp", lambda e: e.dma_start(out=sbias_sb, in_=sbias_d.rearrange("p (a b) -> p a b", a=3)), writes=[b_sc], sem=b_sc)
        P.op("dve", lambda e: e.tensor_copy(out=identf, in_=ident[:]), reads=[b_const], writes=[b_sc])
        P.dma("sp", lambda e: e.dma_start(out=xs_sb[0:tp], in_=xs), writes=[b_xs], sem=b_xs)
        rmsnorm_T(xs_sb, b_xs, xsb, b_xsb, junk, b_junk, stat_s, b_stat_s, 0, g1_sb[:], xsT[:, :, 0:tp], b_xsT, tp=tp)
        for blk in range(7):
            jb = blk % 2
            P.dma("sp", lambda e, blk=blk, jb=jb: e.dma_start(
                out=wblk[jb], in_=win_bf[:, blk * 512:(blk + 1) * 512].rearrange("(k p) c -> p k c", p=128)),
                reads=[b_winbf], writes=[b_wblk[jb]], sem=b_wblk[jb])
            pa = 2 + jb
            for k in range(8):
                P.op("pe", lambda e, k=k, jb=jb, pa=pa: e.matmul(pbank[pa][0:tp, :], lhsT=xsT[:, k, 0:tp], rhs=wblk[jb][:, k, :],
                                                                start=(k == 0), stop=(k == 7)),
                     reads=[b_xsT, b_wblk[jb]], writes=[b_pb[pa]])
            P.op("act", lambda e, blk=blk, pa=pa: e.copy(out=z[0:tp, blk * 512:(blk + 1) * 512], in_=pbank[pa][0:tp, :]),
                 reads=[b_pb[pa]], writes=[b_z])
        P.dma("sp", lambda e: e.dma_start(out=kns, in_=z[0:tp, 512:1024]), reads=[b_z], sem=b_z)
        P.dma("sp", lambda e: e.dma_start(out=vns, in_=z[0:tp, 1024:1536]), reads=[b_z], sem=b_z)

        RT, WT_ = [b_tm], [b_tm]
        P.op("dve", lambda e: e.tensor_tensor(out=tmp16[0:tp], in0=z[0:tp, 0:512], in1=z[0:tp, 512:1024], op=ALU.mult),
             reads=[b_z], writes=WT_)
        P.op("dve", lambda e: e.tensor_reduce(out=sself[0:tp], in_=tmp16[0:tp].rearrange("p (h c) -> p h c", h=8), axis=AX.X, op=ALU.add),
             reads=RT, writes=WT_)
        P.op("act", lambda e: e.activation(out=pself[0:tp], in_=sself[0:tp], func=AF.Exp, scale=0.125), reads=RT, writes=WT_)
        P.op("dve", lambda e: e.tensor_scalar(out=pself[0:tp], in0=pself[0:tp], scalar1=3.0, scalar2=None, op0=ALU.mult), reads=RT, writes=WT_)
        NTOT = tp * 3
        for b in range(tp):
            P.op("pe", lambda e, b=b: e.matmul(pbank[4][:], lhsT=selrow[0:16, b * 128:(b + 1) * 128], rhs=z[0:16, 0:512],
                                               start=True, stop=True), reads=[b_sc, b_z], writes=[b_pb[4]])
            for g, dil in enumerate((1, 4, 16)):
                n = b * 3 + g
                jj = n % 2
                st0 = 2048 - 128 * dil
                if dil == 1:
                    ksrc = ck[b, st0:2048, :]
                    vsrc = cv[b, st0:2048, :]
                else:
                    ksrc = ck[b, st0:2048, :].rearrange("(r d) c -> r d c", d=dil)[:, 0, :]
                    vsrc = cv[b, st0:2048, :].rearrange("(r d) c -> r d c", d=dil)[:, 0, :]
                P.dma("sp", lambda e, jj=jj, ksrc=ksrc: e.dma_start(out=kg[jj], in_=ksrc), writes=[b_kg[jj]], sem=b_kg[jj])
                P.dma("sp", lambda e, jj=jj, vsrc=vsrc: e.dma_start(out=vg_[jj], in_=vsrc), writes=[b_vg[jj]], sem=b_vg[jj])
                P.op("dve", lambda e, jj=jj: e.tensor_tensor(out=prod, in0=kg[jj], in1=pbank[4][:], op=ALU.mult),
                     reads=[b_kg[jj], b_pb[4]], writes=[b_prod])
                P.op("dve", lambda e: e.tensor_reduce(out=sc, in_=prod.rearrange("p (h c) -> p h c", h=8), axis=AX.X, op=ALU.add),
                     reads=[b_prod], writes=[b_scr])
                P.op("dve", lambda e, g=g: e.scalar_tensor_tensor(out=sc, in0=sc, scalar=0.125, in1=sbias_sb[:, g, :],
                                                                 op0=ALU.mult, op1=ALU.add), reads=[b_scr, b_sc], writes=[b_scr])
                P.op("act", lambda e, jj=jj: e.activation(out=p_[jj], in_=sc, func=AF.Exp), reads=[b_scr], writes=[b_p[jj]])
                P.op("dve", lambda e, jj=jj: e.tensor_tensor(out=pvt[jj].rearrange("p (h c) -> p h c", h=8),
                                                             in0=vg_[jj].rearrange("p (h c) -> p h c", h=8),
                                                             in1=p_[jj].unsqueeze(2).broadcast_to([128, 8, 64]), op=ALU.mult),
                     reads=[b_vg[jj], b_p[jj]], writes=[b_pvt[jj]])
                P.op("pe", lambda e, jj=jj, b=b, n=n: e.matmul(pbank[5][0:16, :], lhsT=selcol[:, b, :], rhs=pvt[jj],
                                                              start=(n == 0), stop=(n == NTOT - 1)),
                     reads=[b_pvt[jj], b_sc], writes=[b_pb[5]])
                P.op("pe", lambda e, jj=jj, b=b, n=n: e.matmul(pbank[6][0:16, 0:8], lhsT=selcol[:, b, :], rhs=p_[jj],
                                                              start=(n == 0), stop=(n == NTOT - 1)),
                     reads=[b_p[jj], b_sc], writes=[b_pb[6]])
        P.op("dve", lambda e: e.tensor_tensor(out=tmp16[0:tp].rearrange("p (h c) -> p h c", h=8),
                                              in0=z[0:tp, 1024:1536].rearrange("p (h c) -> p h c", h=8),
                                              in1=pself[0:tp].unsqueeze(2).broadcast_to([tp, 8, 64]), op=ALU.mult),
             reads=[b_z, b_tm], writes=WT_)
        P.op("dve", lambda e: e.tensor_tensor(out=tmp16[0:tp], in0=tmp16[0:tp], in1=pbank[5][0:tp, :], op=ALU.add),
             reads=[b_tm, b_pb[5]], writes=WT_)
        P.op("dve", lambda e: e.tensor_tensor(out=den[0:tp], in0=pself[0:tp], in1=pbank[6][0:tp, 0:8], op=ALU.add),
             reads=[b_tm, b_pb[6]], writes=WT_)
        P.op("dve", lambda e: e.reciprocal(out=rden[0:tp], in_=den[0:tp]), reads=RT, writes=WT_)
        P.op("dve", lambda e: e.tensor_tensor(out=catf[0:tp, 0:512].rearrange("p (h c) -> p h c", h=8),
                                              in0=tmp16[0:tp].rearrange("p (h c) -> p h c", h=8),
                                              in1=rden[0:tp].unsqueeze(2).broadcast_to([tp, 8, 64]), op=ALU.mult),
             reads=RT, writes=[b_catf])

        for r in range(4):
            P.op("pe", lambda e, r=r: e.transpose(out=pbank[1][:, r * 16:(r + 1) * 16], in_=z[0:tp, 1536 + r * 128:1536 + (r + 1) * 128],
                                                  identity=identf[0:tp, 0:tp]), reads=[b_z, b_sc], writes=[b_pb[1]])
        P.op("act", lambda e: e.copy(out=rqT, in_=pbank[1][:, 0:64].rearrange("p (r b) -> p r b", r=4)), reads=[b_pb[1]], writes=[b_rqT])
        for r in range(4):
            P.op("dve", lambda e, r=r: e.tensor_tensor(out=qm[:, r], in0=rqT[:, r, :].unsqueeze(1).broadcast_to([128, 16, 16]),
                                                       in1=selcol, op=ALU.mult), reads=[b_rqT, b_sc], writes=[b_qm])
        for r in range(4):
            for b in range(tp):
                n = r * tp + b
                jj = n % 2
                P.dma("sp", lambda e, jj=jj, b=b, r=r: e.dma_start(out=S_sb[jj], in_=st[b * 4 + r]), writes=[b_S[jj]], sem=b_S[jj])
                P.op("pe", lambda e, jj=jj, b=b, r=r: e.matmul(pbank[7][0:16, r * 128:(r + 1) * 128], lhsT=qm[:, r, b, :], rhs=S_sb[jj],
                                                              start=(b == 0), stop=(b == tp - 1)),
                     reads=[b_qm, b_S[jj]], writes=[b_pb[7]])
                P.op("dve", lambda e, jj=jj, b=b, r=r: e.tensor_scalar(out=kmask[jj][0:tp], in0=z[0:tp, 2048 + r * 128:2048 + (r + 1) * 128],
                                                                      scalar1=identf[0:tp, b:b + 1], scalar2=128.0 ** -0.5,
                                                                      op0=ALU.mult, op1=ALU.mult),
                     reads=[b_z, b_sc], writes=[b_kmask[jj]])
                pa = 2 + jj
                P.op("pe", lambda e, jj=jj, r=r, pa=pa: e.matmul(pbank[pa][:, 0:128], lhsT=kmask[jj][0:tp], rhs=z[0:tp, 2560 + r * 128:2560 + (r + 1) * 128],
                                                                start=True, stop=True),
                     reads=[b_kmask[jj], b_z], writes=[b_pb[pa]])
                P.op("dve", lambda e, jj=jj, r=r, pa=pa: e.scalar_tensor_tensor(out=sout[jj], in0=S_sb[jj], scalar=GAM[r], in1=pbank[pa][:, 0:128],
                                                                               op0=ALU.mult, op1=ALU.add),
                     reads=[b_S[jj], b_pb[pa]], writes=[b_sout[jj]])
                P.dma("sp", lambda e, jj=jj, b=b, r=r: e.dma_start(out=rs[b * 4 + r], in_=sout[jj]), reads=[b_sout[jj]], sem=b_sout[jj])
        P.op("dve", lambda e: e.tensor_tensor(out=tmp16[0:tp], in0=z[0:tp, 1536:2048], in1=z[0:tp, 2048:2560], op=ALU.mult),
             reads=[b_z, b_tm], writes=WT_)
        P.op("dve", lambda e: e.tensor_reduce(out=qk[0:tp], in_=tmp16[0:tp].rearrange("p (r c) -> p r c", r=4), axis=AX.X, op=ALU.add),
             reads=RT, writes=WT_)
        P.op("dve", lambda e: e.scalar_tensor_tensor(out=ro[0:tp], in0=z[0:tp, 2560:3072].rearrange("p (r c) -> p r c", r=4),
                                                     scalar=128.0 ** -0.5, in1=qk[0:tp].unsqueeze(2).broadcast_to([tp, 4, 128]),
                                                     op0=ALU.mult, op1=ALU.mult), reads=[b_z, b_tm], writes=[b_ro])
        for r in range(4):
            P.op("dve", lambda e, r=r: e.scalar_tensor_tensor(out=ro[0:tp, r, :], in0=pbank[7][0:tp, r * 128:(r + 1) * 128], scalar=GAM[r],
                                                             in1=ro[0:tp, r, :], op0=ALU.mult, op1=ALU.add),
                 reads=[b_pb[7], b_ro], writes=[b_ro])
        P.op("act", lambda e: e.activation(out=sgs[0:tp], in_=z[0:tp, 3072:3584].rearrange("p (r c) -> p r c", r=4), func=AF.Silu),
             reads=[b_z], writes=[b_sgs])
        groupnorm(tp, ro[0:tp], b_ro, ggn_bc[0:tp, :].rearrange("p (r c) -> p r c", r=4), sgs[0:tp], b_sgs,
                  catf[0:tp, 512:1024].rearrange("p (r c) -> p r c", r=4), [b_catf], gn_a, gn_b, gn_s, b_gn)
        P.op("act", lambda e: e.copy(out=cat_s[0:tp, :], in_=catf[0:tp, :]), reads=[b_catf], writes=[b_cats])

    def prompt_mixer(s):
        P.barrier()
        A.reset()
        dtab = A.alloc([TW], F32)
        multt = A.alloc([TW], BF16)
        caust = A.alloc([TW], BF16)
        kdec = A.alloc([4 * NT], F32)
        b_tab = Buf("tab")
        for (dst, src) in ((dtab, dtab_d), (multt, multt_d), (caust, caust_d), (kdec, kdec_d)):
            P.dma("sp", lambda e, dst=dst, src=src: e.dma_start(out=dst, in_=src), writes=[b_tab], sem=b_tab)
        xt = [A.alloc([D], F32) for i in range(2)]
        b_xt = [Buf("xt%d" % i) for i in range(2)]
        xb = [A.alloc([D], BF16) for i in range(2)]
        b_xb = [Buf("xb%d" % i) for i in range(2)]
        junk = A.alloc([D], BF16)
        b_junk = Buf("junk")
        stat = A.alloc([NT, 4], F32)
        b_stat = [Buf("stat%d" % i) for i in range(NT)]
        xT = A.alloc([8, SEQ], BF16)
        b_xT = [Buf("xT%d" % i) for i in range(NT)]
        wkv = A.alloc([8, 1024], BF16)
        b_wkv = Buf("wkv")
        kvst = [A.alloc([1024], F32) for i in range(2)]
        b_kvst = [Buf("kvst%d" % i) for i in range(2)]
        wu = [A.alloc([8, 4, 128], BF16) for i in range(2)]
        b_wu = [Buf("wu%d" % i) for i in range(2)]
        QT = [A.alloc([SEQ], BF16) for i in range(2)]
        KT = [A.alloc([SEQ], BF16) for i in range(2)]
        b_QT = [Buf("QT%d" % i) for i in range(2)]
        b_KT = [Buf("KT%d" % i) for i in range(2)]
        Vb = [A.alloc([NT, 130], BF16) for i in range(2)]
        b_Vb = [Buf("Vb%d" % i) for i in range(2)]
        Kd = A.alloc([NT, 128], BF16)
        b_Kd = Buf("Kd")
        sg = A.alloc([NT, 128], BF16)
        b_sg = Buf("sg")
        Et = [A.alloc([TW], BF16) for i in range(2)]
        b_Et = [Buf("Et%d" % i) for i in range(2)]
        Etmp = A.alloc([TW], BF16)
        b_Etmp = Buf("Etmp")
        NPB = 4
        Pe = [A.alloc([512], BF16) for i in range(NPB)]
        b_Pe = [Buf("Pe%d" % i) for i in range(NPB)]
        Pm = [A.alloc([512], BF16) for i in range(NPB)]
        b_Pm = [Buf("Pm%d" % i) for i in range(NPB)]
        gn_a = A.alloc([4, 128], F32)
        gn_b = A.alloc([4, 128], F32)
        gn_s = A.alloc([16], F32)
        b_gn = Buf("gn")
        rden = A.alloc([4], F32)
        b_rden = Buf("rden")
        sfin = A.alloc([128], F32)
        b_sfin = Buf("sfin")

        for k in range(8):
            P.dma("sp", lambda e, k=k: e.dma_start(out=wkv[:, k, :], in_=win_bf[k * 128:(k + 1) * 128, 512:1536]),
                  reads=[b_winbf], writes=[b_wkv], sem=b_wkv)

        for i in range(NT):
            j = i % 2
            r0 = s * SEQ + i * 128
            P.dma("sp", lambda e, j=j, r0=r0: e.dma_start(out=xt[j], in_=xp[r0:r0 + 128, :]),
                  writes=[b_xt[j]], sem=b_xt[j])
            rmsnorm_T(xt[j], b_xt[j], xb[j], b_xb[j], junk, b_junk, stat[:, i, :], b_stat[i], j, g1_sb[:],
                      xT[:, :, i * 128:(i + 1) * 128], b_xT[i])

        for i in range(NT):
            j = i % 2
            r0 = s * SEQ + i * 128
            for half in range(2):
                pa = 2 + 2 * j + half
                for k in range(8):
                    P.op("pe", lambda e, pa=pa, k=k, i=i, half=half: e.matmul(
                        pbank[pa][:], lhsT=xT[:, k, i * 128:(i + 1) * 128], rhs=wkv[:, k, half * 512:(half + 1) * 512],
                        start=(k == 0), stop=(k == 7)),
                        reads=[b_xT[i], b_wkv], writes=[b_pb[pa]])
                P.op("act", lambda e, pa=pa, j=j, half=half: e.copy(out=kvst[j][:, half * 512:(half + 1) * 512],
                                                                   in_=pbank[pa][:]),
                     reads=[b_pb[pa]], writes=[b_kvst[j]])
            P.dma("sp", lambda e, j=j, r0=r0: e.dma_start(out=kwp[r0:r0 + 128, :], in_=kvst[j][:, 0:512]),
                  reads=[b_kvst[j]], sem=b_kvst[j])
            P.dma("sp", lambda e, j=j, r0=r0: e.dma_start(out=vwp[r0:r0 + 128, :], in_=kvst[j][:, 512:1024]),
                  reads=[b_kvst[j]], sem=b_kvst[j])

        if stop <= 1:
            return
        for j in range(2):
            P.op("pool", lambda e, j=j: e.memset(Vb[j][:, :, 64:65], 1.0), writes=[b_Vb[j]])
            P.op("pool", lambda e, j=j: e.memset(Vb[j][:, :, 129:130], 1.0), writes=[b_Vb[j]])

        for u in range(8):
            if u >= stop - 10:
                return
            j = u % 2
            is_att = u < 4
            if is_att:
                cols = [u * 128, 512 + u * 128, 1024 + u * 128]
            else:
                r = u - 4
                cols = [1536 + r * 128, 2048 + r * 128, 2560 + r * 128, 3072 + r * 128]
            for m, c0 in enumerate(cols):
                P.dma("sp", lambda e, j=j, m=m, c0=c0: e.dma_start(
                    out=wu[j][:, :, m, :], in_=win_bf[:, c0:c0 + 128].rearrange("(k p) c -> p k c", p=128)),
                    reads=[b_winbf], writes=[b_wu[j]], sem=b_wu[j])
            for tg in range(NQG):
                for m in range(2):
                    pa = 2 + (2 * tg + m) % 4
                    for k in range(8):
                        P.op("pe", lambda e, pa=pa, k=k, m=m, tg=tg, j=j: e.matmul(
                            pbank[pa][:], lhsT=wu[j][:, k, m, :], rhs=xT[:, k, tg * 512:(tg + 1) * 512],
                            start=(k == 0), stop=(k == 7)),
                            reads=[b_wu[j]] + b_xT[4 * tg:4 * tg + 4], writes=[b_pb[pa]])
                    dst = (QT if m == 0 else KT)[j]
                    bd = (b_QT if m == 0 else b_KT)[j]
                    sc = 1.0 if (is_att or m == 0) else 128.0 ** -0.5
                    P.op("act", lambda e, pa=pa, dst=dst, tg=tg, sc=sc: e.activation(
                        out=dst[:, tg * 512:(tg + 1) * 512], in_=pbank[pa][:], func=AF.Copy, scale=sc),
                        reads=[b_pb[pa]], writes=[bd])
            if u == 4 and RSTOP <= 1:
                return
            for i in range(NT):
                pa = 2 + i % 4
                ncol = 128 if is_att else 384
                m0 = 2 if is_att else 1
                for k in range(8):
                    P.op("pe", lambda e, pa=pa, k=k, i=i, j=j, m0=m0, ncol=ncol: e.matmul(
                        pbank[pa][:, 0:ncol], lhsT=xT[:, k, i * 128:(i + 1) * 128],
                        rhs=wu[j][:, k, m0:m0 + ncol // 128, :].rearrange("p m c -> p (m c)"),
                        start=(k == 0), stop=(k == 7)),
                        reads=[b_xT[i], b_wu[j]], writes=[b_pb[pa]])
                if is_att:
                    P.op("act", lambda e, pa=pa, i=i, j=j: e.copy(
                        out=Vb[j][:, i, :].rearrange("p (h c) -> p h c", h=2)[:, :, 0:64],
                        in_=pbank[pa][:, 0:128].rearrange("p (h c) -> p h c", h=2)),
                        reads=[b_pb[pa]], writes=[b_Vb[j]])
                else:
                    r = u - 4
                    P.op("dve", lambda e, pa=pa, i=i, r=r: e.tensor_scalar(
                        out=Kd[:, i, :], in0=pbank[pa][:, 0:128], scalar1=kdec[:, r * NT + i:r * NT + i + 1],
                        scalar2=None, op0=ALU.mult), reads=[b_pb[pa], b_tab], writes=[b_Kd])
                    P.op("act", lambda e, pa=pa, i=i, j=j: e.copy(out=Vb[j][:, i, 0:128], in_=pbank[pa][:, 128:256]),
                         reads=[b_pb[pa]], writes=[b_Vb[j]])
                    P.op("act", lambda e, pa=pa, i=i: e.activation(out=sg[:, i, :], in_=pbank[pa][:, 256:384], func=AF.Silu),
                         reads=[b_pb[pa]], writes=[b_sg])

            if stop <= 2 or (u == 4 and RSTOP <= 2):
                return
            heads = [0, 1] if is_att else [0]
            for hl in heads:
                if is_att:
                    h = 2 * u + hl
                    prow = slice(hl * 64, hl * 64 + 64)
                    tabsc = -SLOPES[h]
                    mt = multt
                    vw = 65
                else:
                    r = u - 4
                    prow = slice(0, 128)
                    tabsc = LG[r]
                    mt = caust
                    vw = 128
                ej = (2 * u + hl) % 2
                P.op("act", lambda e, tabsc=tabsc: e.activation(out=Etmp, in_=dtab, func=AF.Exp, scale=tabsc),
                     reads=[b_tab], writes=[b_Etmp])
                P.op("dve", lambda e, ej=ej, mt=mt: e.tensor_tensor(out=Et[ej], in0=Etmp, in1=mt, op=ALU.mult),
                     reads=[b_Etmp, b_tab], writes=[b_Et[ej]])
                items = [(qg, kb) for qg in range(NQG) for kb in range(4 * qg + 4)]
                LA = 3

                def front(n, qg, kb, j=j, prow=prow, ej=ej):
                    pa = 2 + n % 4
                    pb_ = n % NPB
                    off = qg * 512 - kb * 128 + 384
                    P.op("pe", lambda e: e.matmul(pbank[pa][:], lhsT=KT[j][prow, kb * 128:(kb + 1) * 128],
                                                  rhs=QT[j][prow, qg * 512:(qg + 1) * 512], start=True, stop=True),
                         reads=[b_KT[j], b_QT[j]], writes=[b_pb[pa]])
                    if is_att:
                        P.op("act", lambda e: e.activation(out=Pe[pb_], in_=pbank[pa][:], func=AF.Exp, scale=0.125),
                             reads=[b_pb[pa]], writes=[b_Pe[pb_]])
                        P.op("dve", lambda e: e.tensor_tensor(out=Pm[pb_], in0=Pe[pb_], in1=Et[ej][:, off:off + 512],
                                                              op=ALU.mult),
                             reads=[b_Pe[pb_], b_Et[ej]], writes=[b_Pm[pb_]])
                    else:
                        P.op("dve", lambda e: e.tensor_tensor(out=Pm[pb_], in0=pbank[pa][:], in1=Et[ej][:, off:off + 512],
                                                              op=ALU.mult),
                             reads=[b_pb[pa], b_Et[ej]], writes=[b_Pm[pb_]])

                def back(n, qg, kb, j=j, hl=hl, vw=vw, u=u):
                    pb_ = n % NPB
                    po = qg % 2
                    ov = pbank[po][:].rearrange("p (q c) -> p q c", q=4)
                    for qb in range(4):
                        QB = 4 * qg + qb
                        if kb > QB:
                            continue
                        if is_att:
                            rhs = Vb[j][:, kb, hl * 65:hl * 65 + 65]
                        else:
                            rhs = Vb[j][:, kb, 0:128]
                        P.op("pe", lambda e, qb=qb, QB=QB, rhs=rhs: e.matmul(
                            ov[:, qb, 0:vw], lhsT=Pm[pb_][:, qb * 128:(qb + 1) * 128], rhs=rhs,
                            start=(kb == 0 and qb == 0), stop=(kb == QB), skip_group_check=True),
                            reads=[b_Pm[pb_], b_Vb[j]], writes=[b_pb[po]])
                    if kb == 4 * qg + 3:
                        bc = b_cat[4 * qg:4 * qg + 4]
                        if is_att:
                            h = 2 * u + hl
                            P.op("dve", lambda e: e.reciprocal(out=rden, in_=ov[:, :, 64]), reads=[b_pb[po]], writes=[b_rden])
                            P.op("dve", lambda e: e.tensor_tensor(
                                out=catTok[:, 4 * qg:4 * qg + 4, h * 64:(h + 1) * 64], in0=ov[:, :, 0:64],
                                in1=rden.unsqueeze(2).broadcast_to([128, 4, 64]), op=ALU.mult),
                                reads=[b_pb[po], b_rden], writes=bc)
                        else:
                            r = u - 4
                            groupnorm(128, ov, b_pb[po],
                                      ggn_bc[:, r * 128:(r + 1) * 128].unsqueeze(1).broadcast_to([128, 4, 128]),
                                      sg[:, 4 * qg:4 * qg + 4, :], b_sg,
                                      catTok[:, 4 * qg:4 * qg + 4, 512 + r * 128:512 + (r + 1) * 128], bc,
                                      gn_a, gn_b, gn_s, b_gn)

                for n in range(len(items) + LA):
                    if n < len(items):
                        front(n, *items[n])
                    if n >= LA:
                        back(n - LA, *items[n - LA])

            if stop <= 3 or (u == 4 and RSTOP <= 3):
                return
            if not is_att:
                r = u - 4
                pa = 2
                for kb in range(NT):
                    P.op("pe", lambda e, kb=kb, j=j: e.matmul(pbank[pa][:, 0:128], lhsT=Kd[:, kb, :], rhs=Vb[j][:, kb, 0:128],
                                                              start=(kb == 0), stop=(kb == NT - 1)),
                         reads=[b_Kd, b_Vb[j]], writes=[b_pb[pa]])
                P.op("act", lambda e: e.copy(out=sfin, in_=pbank[pa][:, 0:128]), reads=[b_pb[pa]], writes=[b_sfin])
                P.dma("sp", lambda e, r=r: e.dma_start(out=rp[s * 4 + r], in_=sfin), reads=[b_sfin], sem=b_sfin)


    def peer_phase(tp, nsub, n_ptiles, cat_src, x_src, y_dst):
        P.barrier()
        A.reset()
        slot = [A.alloc([4096], BF16) for _ in range(4)]
        b_slot = [Buf("slot%d" % i) for i in range(4)]
        su = [sl.rearrange("p (k e) -> p k e", k=8) for sl in slot]
        sv_ = [sl.rearrange("p (c d) -> p c d", c=4) for sl in slot]
        WT = A.alloc([128, 256], BF16)
        b_WT = Buf("WT")
        skT_sb = A.alloc([16, 128], BF16)
        b_skT = Buf("skT")
        P.dma("pool", lambda e: e.dma_start(out=skT_sb, in_=skT.rearrange("p (a b) -> p a b", a=16)),
              writes=[b_skT], sem=b_skT)
        xn2T = A.alloc([8, 256], BF16)
        b_xn2T = [Buf("xn2T%d" % a) for a in range(2)]
        qT = A.alloc([16, 256], BF16)
        b_qT = Buf("qT")
        h_sb = [A.alloc([D], F32) for a in range(2)]
        b_h = [Buf("h%d" % a) for a in range(2)]
        xt2 = [A.alloc([D], F32)] * 2
        b_xt2 = [Buf("xt2")] * 2
        hb = A.alloc([D], BF16)
        b_hb = Buf("hb")
        catT_t = A.alloc([8, 128], BF16)
        b_catT = Buf("catT")
        NGB = 4
        Gt = [A.alloc([256], BF16) for _ in range(NGB)]
        b_Gt = [Buf("Gt%d" % i) for i in range(NGB)]
        Ht = [A.alloc([256], BF16) for _ in range(NGB)]
        b_Ht = [Buf("Ht%d" % i) for i in range(NGB)]
        pk = [A.alloc([256], F32) for _ in range(3)]
        b_pk = Buf("pk")
        tk = [A.alloc([128], F32) for _ in range(3)]
        b_tk = Buf("tk")
        sv = A.alloc([8, 2, 16], F32)
        si = A.alloc([8, 2, 16], U32)
        si_f = A.alloc([8, 2, 16], F32)
        swork = [A.alloc([128], F32) for _ in range(8)]
        b_sw = [Buf("sw%d" % i) for i in range(8)]
        cand = [A.alloc([16, 16], F32) for _ in range(4)]
        cwork = [A.alloc([16, 16], F32) for _ in range(4)]
        b_cand = [Buf("cand%d" % i) for i in range(4)]
        b_cw = [Buf("cw%d" % i) for i in range(4)]
        b_svc = [[Buf("sv%d_%d" % (h, c)) for c in range(2)] for h in range(8)]
        b_sic = [[Buf("si%d_%d" % (h, c)) for c in range(2)] for h in range(8)]
        b_best = [Buf("best%d" % h) for h in range(8)]
        b_pos = [Buf("pos%d" % h) for h in range(8)]
        best = A.alloc([8, 16], F32)
        pos = A.alloc([8, 16], U32)
        pa_u = A.alloc([8, 16], U32)
        pb_u = A.alloc([8, 16], U32)
        pa_f = A.alloc([8, 16], F32)
        pb_f = A.alloc([8, 16], F32)
        oh = [A.alloc([16, 16], F32) for _ in range(2)]
        b_oh = [Buf("oh%d" % i) for i in range(2)]
        ebest = A.alloc([8, 16], F32)
        esum = A.alloc([16], F32)
        b_tkw = Buf("topk_work")
        NOB = 2
        OIb = [A.alloc([8, 128], BF16) for _ in range(NOB)]
        OJb = [A.alloc([8, 128], BF16) for _ in range(NOB)]
        b_OI = [Buf("OI%d" % i) for i in range(NOB)]
        b_OJ = [Buf("OJ%d" % i) for i in range(NOB)]
        stat2 = A.alloc([2, 4], F32)
        b_stat2 = [Buf("stat2%d" % a) for a in range(2)]
        fstat = A.alloc([2, 4], F32)
        b_fstat = [Buf("fstat%d" % a) for a in range(2)]
        identf = A.alloc([128], F32)
        b_identf = Buf("identf")
        P.op("dve", lambda e: e.tensor_copy(out=identf, in_=ident[:]), reads=[b_const], writes=[b_identf])

        for tile in range(n_ptiles):
            T = tp * nsub
            for half in range(2):
                P.dma("sp", lambda e, half=half: e.dma_start(
                    out=su[half], in_=wout_bf[:, half * 512:(half + 1) * 512].rearrange("(k p) c -> p k c", p=128)),
                    reads=[b_woutbf], writes=[b_slot[half]], sem=b_slot[half])
            for a in range(nsub):
                cat_ap, cat_b = cat_src(tile, a)
                x_dr = x_src(tile, a)
                P.dma("sp", lambda e, a=a, x_dr=x_dr: e.dma_start(out=xt2[a][0:tp], in_=x_dr),
                      writes=[b_xt2[a]], sem=b_xt2[a])
                pst = pbank[0][:].bitcast(BF16)
                for k in range(8):
                    P.op("pe", lambda e, k=k, cat_ap=cat_ap: e.transpose(out=pst[:, k * 128:k * 128 + tp],
                                                                        in_=cat_ap[0:tp, k * 128:(k + 1) * 128],
                                                                        identity=ident[0:tp, 0:tp]),
                         reads=[cat_b, b_const], writes=[b_pb[0]])
                P.op("act", lambda e: e.copy(out=catT_t[:, :, 0:tp], in_=pst.rearrange("p (k t) -> p k t", k=8)[:, :, 0:tp]),
                     reads=[b_pb[0]], writes=[b_catT])
                for half in range(2):
                    pa = 2 + half
                    for k in range(8):
                        P.op("pe", lambda e, k=k, half=half, pa=pa: e.matmul(
                            pbank[pa][0:tp, :], lhsT=catT_t[:, k, 0:tp], rhs=su[half][:, k, :], start=(k == 0), stop=(k == 7)),
                            reads=[b_catT, b_slot[half]], writes=[b_pb[pa]])
                    P.op("dve", lambda e, a=a, half=half, pa=pa: e.tensor_tensor(
                        out=h_sb[a][0:tp, half * 512:(half + 1) * 512], in0=pbank[pa][0:tp, :],
                        in1=xt2[a][0:tp, half * 512:(half + 1) * 512], op=ALU.add),
                        reads=[b_pb[pa], b_xt2[a]], writes=[b_h[a]])
                rmsnorm_T(h_sb[a], b_h[a], hb, b_hb, hb, b_hb, stat2[:, a, :], b_stat2[a], 1, g2_sb[:],
                          xn2T[:, :, a * tp:(a + 1) * tp], b_xn2T[a], tp=tp)

            for qq in range(4):
                sl = (2 + qq) % 4
                P.dma("sp", lambda e, qq=qq, sl=sl: e.dma_start(
                    out=su[sl], in_=wpq_bf[:, qq * 512:(qq + 1) * 512].rearrange("(k p) c -> p k c", p=128)),
                    reads=[b_wpqbf], writes=[b_slot[sl]], sem=b_slot[sl])
                for cc in range(4):
                    hc = qq * 4 + cc
                    pa = 2 + hc % 4
                    for k in range(8):
                        P.op("pe", lambda e, k=k, cc=cc, sl=sl, pa=pa: e.matmul(
                            pbank[pa][:, 0:T], lhsT=su[sl][:, k, cc * 128:(cc + 1) * 128], rhs=xn2T[:, k, 0:T],
                            start=(k == 0), stop=(k == 7)),
                            reads=[b_slot[sl]] + b_xn2T, writes=[b_pb[pa]])
                    P.op("act", lambda e, hc=hc, pa=pa: e.copy(out=qT[:, hc, 0:T], in_=pbank[pa][:, 0:T]),
                         reads=[b_pb[pa]], writes=[b_qT])

            for a in range(nsub):
                for h in range(8):
                    pa = 2 + (h // 2) % 4
                    for c in range(2):
                        P.op("pe", lambda e, h=h, c=c, a=a, pa=pa: e.matmul(
                            pbank[pa][0:tp, (h % 2) * 256 + c * 128:(h % 2) * 256 + (c + 1) * 128],
                            lhsT=qT[:, 2 * h + c, a * tp:(a + 1) * tp], rhs=skT_sb[:, 2 * h + c, :],
                            start=True, stop=True), reads=[b_qT, b_skT], writes=[b_pb[pa]])
                W_ = [b_tkw]
                R_ = [b_tkw]
                for hg in range(2):
                    chains = [(h, c) for h in range(4 * hg, 4 * hg + 4) for c in range(2)]

                    def srcof(h, c):
                        return pbank[2 + (h // 2) % 4][0:tp, (h % 2) * 256 + c * 128:(h % 2) * 256 + (c + 1) * 128], b_pb[2 + (h // 2) % 4]

                    for ci, (h, c) in enumerate(chains):
                        src, bsrc = srcof(h, c)
                        P.op("dve", lambda e, h=h, c=c, src=src: e.max(out=sv[0:tp, h, c, 0:8], in_=src), reads=[bsrc], writes=[b_svc[h][c]])
                    for ci, (h, c) in enumerate(chains):
                        src, bsrc = srcof(h, c)
                        P.op("dve", lambda e, h=h, c=c, src=src: e.max_index(out=si[0:tp, h, c, 0:8], in_max=sv[0:tp, h, c, 0:8], in_values=src),
                             reads=[bsrc, b_svc[h][c]], writes=[b_sic[h][c]])
                    for ci, (h, c) in enumerate(chains):
                        src, bsrc = srcof(h, c)
                        P.op("dve", lambda e, h=h, c=c, src=src, ci=ci: e.match_replace(out=swork[ci][0:tp], in_to_replace=sv[0:tp, h, c, 0:8],
                                                                                      in_values=src, imm_value=-1e30),
                             reads=[bsrc, b_svc[h][c]], writes=[b_sw[ci]])
                    for ci, (h, c) in enumerate(chains):
                        P.op("dve", lambda e, h=h, c=c, ci=ci: e.max(out=sv[0:tp, h, c, 8:16], in_=swork[ci][0:tp]), reads=[b_sw[ci]], writes=[b_svc[h][c]])
                    for ci, (h, c) in enumerate(chains):
                        P.op("dve", lambda e, h=h, c=c, ci=ci: e.max_index(out=si[0:tp, h, c, 8:16], in_max=sv[0:tp, h, c, 8:16], in_values=swork[ci][0:tp]),
                             reads=[b_sw[ci], b_svc[h][c]], writes=[b_sic[h][c]])
                    hs = list(range(4 * hg, 4 * hg + 4))
                    for hi, h in enumerate(hs):
                        P.op("dve", lambda e, h=h, hi=hi: e.tensor_tensor(
                            out=cand[hi][0:tp], in0=sv[0:tp, h, 0, :].unsqueeze(2).broadcast_to([tp, 16, 16]),
                            in1=sv[0:tp, h, 1, :].unsqueeze(1).broadcast_to([tp, 16, 16]), op=ALU.add),
                            reads=[b_svc[h][0], b_svc[h][1]], writes=[b_cand[hi]])
                    cf = [cand[hi][0:tp].rearrange("p a b -> p (a b)") for hi in range(4)]
                    cwf = [cwork[hi][0:tp].rearrange("p a b -> p (a b)") for hi in range(4)]
                    for hi, h in enumerate(hs):
                        P.op("dve", lambda e, h=h, hi=hi: e.max(out=best[0:tp, h, 0:8], in_=cf[hi]), reads=[b_cand[hi]], writes=[b_best[h]])
                    for hi, h in enumerate(hs):
                        P.op("dve", lambda e, h=h, hi=hi: e.max_index(out=pos[0:tp, h, 0:8], in_max=best[0:tp, h, 0:8], in_values=cf[hi]),
                             reads=[b_cand[hi], b_best[h]], writes=[b_pos[h]])
                    for hi, h in enumerate(hs):
                        P.op("dve", lambda e, h=h, hi=hi: e.match_replace(out=cwf[hi], in_to_replace=best[0:tp, h, 0:8], in_values=cf[hi], imm_value=-1e30),
                             reads=[b_cand[hi], b_best[h]], writes=[b_cw[hi]])
                    for hi, h in enumerate(hs):
                        P.op("dve", lambda e, h=h, hi=hi: e.max(out=best[0:tp, h, 8:16], in_=cwf[hi]), reads=[b_cw[hi]], writes=[b_best[h]])
                    for hi, h in enumerate(hs):
                        P.op("dve", lambda e, h=h, hi=hi: e.max_index(out=pos[0:tp, h, 8:16], in_max=best[0:tp, h, 8:16], in_values=cwf[hi]),
                             reads=[b_cw[hi], b_best[h]], writes=[b_pos[h]])
                allsv = [b_svc[h][c] for h in range(8) for c in range(2)] + [b_sic[h][c] for h in range(8) for c in range(2)]
                allbp = b_best + b_pos
                P.op("dve", lambda e: e.tensor_copy(out=si_f[0:tp], in_=si[0:tp]), reads=R_ + allsv, writes=W_)
                P.op("dve", lambda e: e.tensor_single_scalar(out=pa_u[0:tp], in_=pos[0:tp], scalar=4, op=ALU.logical_shift_right), reads=R_ + allbp, writes=W_)
                P.op("dve", lambda e: e.tensor_single_scalar(out=pb_u[0:tp], in_=pos[0:tp], scalar=15, op=ALU.bitwise_and), reads=R_ + allbp, writes=W_)
                P.op("dve", lambda e: e.tensor_copy(out=pa_f[0:tp], in_=pa_u[0:tp]), reads=R_, writes=W_)
                P.op("dve", lambda e: e.tensor_copy(out=pb_f[0:tp], in_=pb_u[0:tp]), reads=R_, writes=W_)
                combos = [(which, h) for which in range(2) for h in range(8)]
                for g0 in range(0, 16, 2):
                    grp = combos[g0:g0 + 2]
                    for oi, (which, h) in enumerate(grp):
                        pf = pa_f if which == 0 else pb_f
                        P.op("dve", lambda e, h=h, pf=pf, oi=oi: e.tensor_tensor(
                            out=oh[oi][0:tp], in0=iota[0:tp, 0:16].unsqueeze(1).broadcast_to([tp, 16, 16]),
                            in1=pf[0:tp, h, :].unsqueeze(2).broadcast_to([tp, 16, 16]), op=ALU.is_equal),
                            reads=[b_tkw, b_const], writes=[b_oh[oi]])
                    for oi, (which, h) in enumerate(grp):
                        P.op("dve", lambda e, h=h, which=which, oi=oi: e.tensor_tensor(
                            out=oh[oi][0:tp], in0=oh[oi][0:tp],
                            in1=si_f[0:tp, h, which, :].unsqueeze(1).broadcast_to([tp, 16, 16]), op=ALU.mult),
                            reads=[b_tkw, b_oh[oi]], writes=[b_oh[oi]])
                    for oi, (which, h) in enumerate(grp):
                        P.op("dve", lambda e, which=which, h=h, oi=oi: e.tensor_reduce(
                            out=tk[which][0:tp, h * 16:(h + 1) * 16], in_=oh[oi][0:tp], axis=AX.X, op=ALU.add),
                            reads=[b_oh[oi]], writes=[b_tk])
                P.op("dve", lambda e: e.tensor_tensor(out=ebest[0:tp], in0=best[0:tp], in1=best[0:tp, :, 0:1].broadcast_to([tp, 8, 16]),
                                                      op=ALU.subtract), reads=R_ + allbp, writes=W_)
                P.op("act", lambda e: e.activation(out=ebest[0:tp], in_=ebest[0:tp], func=AF.Exp), reads=R_, writes=W_)
                P.op("dve", lambda e: e.tensor_reduce(out=esum[0:tp, 0:8], in_=ebest[0:tp], axis=AX.X, op=ALU.add), reads=R_, writes=W_)
                P.op("dve", lambda e: e.reciprocal(out=esum[0:tp, 8:16], in_=esum[0:tp, 0:8]), reads=R_, writes=W_)
                P.op("dve", lambda e: e.tensor_tensor(out=tk[2][0:tp].rearrange("p (h k) -> p h k", h=8), in0=ebest[0:tp],
                                                      in1=esum[0:tp, 8:16].unsqueeze(2).broadcast_to([tp, 8, 16]), op=ALU.mult),
                     reads=R_, writes=[b_tk])
                pt = pbank[1][:]
                for w3 in range(3):
                    P.op("pe", lambda e, w3=w3: e.transpose(out=pt[:, w3 * 128:w3 * 128 + tp], in_=tk[w3][0:tp], identity=identf[0:tp, 0:tp]),
                         reads=[b_tk, b_identf], writes=[b_pb[1]])
                for w3 in range(3):
                    P.op("act", lambda e, w3=w3, a=a: e.copy(out=pk[w3][:, a * tp:(a + 1) * tp], in_=pt[:, w3 * 128:w3 * 128 + tp]),
                         reads=[b_pb[1]], writes=[b_pk])

            TB = 8
            for t0 in range(0, T, TB):
                ob = (t0 // TB) % 2
                P.op("dve", lambda e, t0=t0, ob=ob: e.tensor_tensor(
                    out=OIb[ob], in0=iota[:].unsqueeze(1).broadcast_to([128, TB, 128]),
                    in1=pk[0][:, t0:t0 + TB].unsqueeze(2).broadcast_to([128, TB, 128]), op=ALU.is_equal),
                    reads=[b_pk, b_const], writes=[b_OI[ob]])
                P.op("dve", lambda e, t0=t0, ob=ob: e.tensor_tensor(
                    out=OJb[ob], in0=iota[:].unsqueeze(1).broadcast_to([128, TB, 128]),
                    in1=pk[1][:, t0:t0 + TB].unsqueeze(2).broadcast_to([128, TB, 128]), op=ALU.is_equal),
                    reads=[b_pk, b_const], writes=[b_OJ[ob]])
                P.op("pool", lambda e, t0=t0, ob=ob: e.tensor_tensor(
                    out=OJb[ob], in0=OJb[ob],
                    in1=pk[2][:, t0:t0 + TB].unsqueeze(2).broadcast_to([128, TB, 128]), op=ALU.mult),
                    reads=[b_pk, b_OJ[ob]], writes=[b_OJ[ob]])
                for tt in range(TB):
                    t = t0 + tt
                    pw = 6 + (t // 4) % 2
                    P.op("pe", lambda e, t=t, tt=tt, ob=ob, pw=pw: e.matmul(pbank[pw][:, (t % 4) * 128:(t % 4 + 1) * 128],
                                                                           lhsT=OJb[ob][:, tt, :], rhs=OIb[ob][:, tt, :],
                                                                           start=True, stop=True),
                         reads=[b_OI[ob], b_OJ[ob]], writes=[b_pb[pw]])
                    if t % 4 == 3:
                        t4 = t - 3
                        P.op("act", lambda e, t4=t4, pw=pw: e.copy(
                            out=WT[:, :, t4:t4 + 4], in_=pbank[pw][:].rearrange("p (t i) -> p i t", t=4)),
                            reads=[b_pb[pw]], writes=[b_WT])

            items = [(g, c) for g in range(NG) for c in range(4)]
            LA = 3

            def pfront(n, g, c):
                i = g * 4 + c
                us, vs = g % 2, 2 + g % 2
                if c == 0:
                    P.dma("sp", lambda e: e.dma_start(out=slot[us], in_=uT_bf[g]),
                          reads=[b_uTbf], writes=[b_slot[us]], sem=b_slot[us])
                    P.dma("sp", lambda e: e.dma_start(out=slot[vs], in_=pv_bf[g]),
                          reads=[b_pvbf], writes=[b_slot[vs]], sem=b_slot[vs])
                pa = 6 + n % 2
                gb = n % NGB
                for k in range(8):
                    P.op("pe", lambda e, k=k: e.matmul(pbank[pa][:, 0:T], lhsT=su[us][:, k, c * 128:(c + 1) * 128], rhs=xn2T[:, k, 0:T],
                                                       start=(k == 0), stop=(k == 7)),
                         reads=[b_slot[us]] + b_xn2T, writes=[b_pb[pa]])
                P.op("act", lambda e: e.activation(out=Gt[gb][:, 0:T], in_=pbank[pa][:, 0:T], func=GELU), reads=[b_pb[pa]], writes=[b_Gt[gb]])
                P.op("dve", lambda e: e.tensor_tensor(out=Ht[gb][:, 0:T], in0=Gt[gb][:, 0:T], in1=WT[:, i, 0:T], op=ALU.mult),
                     reads=[b_Gt[gb], b_WT], writes=[b_Ht[gb]])

            def pback(n, g, c):
                i = g * 4 + c
                vs = 2 + g % 2
                gb = n % NGB
                for a in range(nsub):
                    for half in range(2):
                        pa = 2 + a * 2 + half
                        P.op("pe", lambda e, a=a, half=half, pa=pa: e.matmul(
                            pbank[pa][0:tp, :], lhsT=Ht[gb][:, a * tp:(a + 1) * tp], rhs=sv_[vs][:, c, half * 512:(half + 1) * 512],
                            start=(i == 0), stop=(i == NI - 1)),
                            reads=[b_Ht[gb], b_slot[vs]], writes=[b_pb[pa]])

            for n in range(len(items) + LA):
                if n < len(items):
                    pfront(n, *items[n])
                if n >= LA:
                    pback(n - LA, *items[n - LA])

            for a in range(nsub):
                y_dr = y_dst(tile, a)
                for half in range(2):
                    pa = 2 + a * 2 + half
                    P.op("dve", lambda e, a=a, half=half, pa=pa: e.tensor_tensor(
                        out=h_sb[a][0:tp, half * 512:(half + 1) * 512], in0=pbank[pa][0:tp, :],
                        in1=h_sb[a][0:tp, half * 512:(half + 1) * 512], op=ALU.add),
                        reads=[b_pb[pa], b_h[a]], writes=[b_h[a]])
                fs = fstat[0:tp, a, :]
                P.op("act", lambda e, a=a, fs=fs: e.activation(out=hb[0:tp], in_=h_sb[a][0:tp], func=AF.Square, accum_out=fs[:, 0:1]),
                     reads=[b_h[a]], writes=[b_hb, b_fstat[a]])
                P.op("dve", lambda e, fs=fs: e.tensor_scalar(out=fs[:, 1:2], in0=fs[:, 0:1], scalar1=1.0 / D, scalar2=1e-6,
                                                            op0=ALU.mult, op1=ALU.add), reads=[b_fstat[a]], writes=[b_fstat[a]])
                P.op("act", lambda e, fs=fs: e.activation(out=fs[:, 2:3], in_=fs[:, 1:2], func=AF.Sqrt),
                     reads=[b_fstat[a]], writes=[b_fstat[a]])
                P.op("dve", lambda e, fs=fs: e.reciprocal(out=fs[:, 3:4], in_=fs[:, 2:3]), reads=[b_fstat[a]], writes=[b_fstat[a]])
                P.op("dve", lambda e, a=a, fs=fs: e.scalar_tensor_tensor(out=h_sb[a][0:tp], in0=h_sb[a][0:tp], scalar=fs[:, 3:4], in1=gf_bc[0:tp, :],
                                                                       op0=ALU.mult, op1=ALU.mult),
                     reads=[b_h[a], b_fstat[a], b_const], writes=[b_h[a]])
                P.dma("sp", lambda e, a=a, y_dr=y_dr: e.dma_start(out=y_dr, in_=h_sb[a][0:tp]),
                      reads=[b_h[a]], sem=b_h[a])

    for s in range(NSEQ):
        prompt_mixer(s)
        if do_peer:
            def rows(tile, a, s=s):
                r0 = s * SEQ + (tile * 2 + a) * 128
                return slice(r0, r0 + 128)
            peer_phase(128, 2, SEQ // 256,
                       lambda tile, a: (catTok[:, tile * 2 + a, :], b_cat[tile * 2 + a]),
                       lambda tile, a, rows=rows: xp[rows(tile, a), :],
                       lambda tile, a, rows=rows: yp[rows(tile, a), :])
    if do_sample:
        sample_mixer()
        if do_peer:
            peer_phase(NS, 1, 1, lambda tile, a: (cat_s[:], b_cats), lambda tile, a: xs, lambda tile, a: ys)

    P.barrier()
    P.emit(nc, es)
    es.close()
    return nc


_NC_CACHE = {}


def kernel(x_prompt, x_sample, cache_k_win, cache_v_win, state_ret, norm1_g, w_in, ret_gn_g, w_out,
           norm2_g, w_pq, peer_sub_keys, peer_u, peer_v, norm_f_g):
    f32 = np.float32
    if "nc" not in _NC_CACHE:
        _NC_CACHE["nc"] = build_program()
    nc = _NC_CACHE["nc"]
    x_prompt = np.asarray(x_prompt, f32)
    x_sample = np.asarray(x_sample, f32)
    ck = np.asarray(cache_k_win, f32)[0].reshape(128, 2048, 512)
    cv = np.asarray(cache_v_win, f32)[0].reshape(128, 2048, 512)
    stt = np.asarray(state_ret, f32)[0].reshape(128 * 4, 128, 128)
    shared = {
        "g1": np.ascontiguousarray(np.asarray(norm1_g, f32)[0].reshape(8, 128).T),
        "g2": np.ascontiguousarray(np.asarray(norm2_g, f32)[0].reshape(8, 128).T),
        "gf": np.asarray(norm_f_g, f32).reshape(1, D),
        "ggn": np.asarray(ret_gn_g, f32).reshape(1, 512),
        "w_in": np.asarray(w_in, f32)[0],
        "w_out": np.asarray(w_out, f32)[0],
        "w_pq": np.asarray(w_pq, f32)[0],
        "skT": np.ascontiguousarray(np.asarray(peer_sub_keys, f32)[0].reshape(16, 128, 128).transpose(2, 0, 1)).reshape(128, 2048),
        "uT": np.ascontiguousarray(np.asarray(peer_u, f32)[0].T),
        "pv": np.asarray(peer_v, f32)[0],
    }
    shared.update(make_tables(SEQ))
    in_maps = []
    for c in range(NCORES):
        m = dict(shared)
        m["xp"] = x_prompt[2 * c:2 * c + 2].reshape(NSEQ * SEQ, D)
        m["xs"] = x_sample[16 * c:16 * c + 16, 0, :]
        m["ck"] = ck[16 * c:16 * c + 16]
        m["cv"] = cv[16 * c:16 * c + 16]
        m["st"] = stt[64 * c:64 * c + 64]
        in_maps.append(m)
    res = run_bass_kernel_spmd(nc, in_maps, core_ids=list(range(NCORES)))
    R = res.results
    cat = lambda k: np.concatenate([np.asarray(R[c][k], f32) for c in range(NCORES)], axis=0)
    y_prompt = cat("yp").reshape(16, SEQ, D)
    y_sample = cat("ys").reshape(128, 1, D)
    k_win = cat("kwp").reshape(1, 16, SEQ, 8, 64)
    v_win = cat("vwp").reshape(1, 16, SEQ, 8, 64)
    ret_p = cat("rp").reshape(1, 16, 4, 128, 128)
    k_new = cat("kns").reshape(1, 128, 1, 8, 64)
    v_new = cat("vns").reshape(1, 128, 1, 8, 64)
    ret_s = cat("rs").reshape(1, 128, 4, 128, 128)
    return (y_prompt, y_sample, k_win, v_win, ret_p, k_new, v_new, ret_s)
```

```python
import numpy as np
import ml_dtypes
from contextlib import ExitStack
import concourse.bass as bass
import concourse.mybir as mybir
from concourse.bass_utils import run_bass_kernel_spmd

F32 = mybir.dt.float32
BF16 = mybir.dt.bfloat16
I32 = mybir.dt.int32
U32 = mybir.dt.uint32
AF = mybir.ActivationFunctionType
ALU = mybir.AluOpType
AX = mybir.AxisListType

NCORES = 8
D = 1024
SEQ = 2048
NSEQ = 2
NS = 16
INC = 3584
NT = SEQ // 128

ENGS = ("pe", "act", "dve", "pool", "sp")
SAME_ENGINE_SYNC = True


class Buf:
    __slots__ = ("name", "w", "r", "dsem", "dcnt", "excl")

    def __init__(self, name, excl=False):
        self.name = name
        self.excl = excl
        self.w = None
        self.r = {}
        self.dsem = None
        self.dcnt = 0


class Prog:
    def __init__(self):
        self.q = {e: [] for e in ENGS}
        self.cnt = {e: 0 for e in ENGS}
        self.waited = {e: {} for e in ENGS}
        self.ndsem = 0
        self.dtotal = {}
        self.needed = {e: set() for e in ENGS}

    def _deps(self, eng, reads, writes, is_dma_sem=None):
        evs = []
        for b in reads:
            if b.w is not None:
                evs.append(b.w)
            if b.excl:
                evs.extend((k, v) for (k, v) in b.r.items() if k != eng)
        for b in writes:
            if b.w is not None:
                if not (is_dma_sem is not None and b.w[0] == is_dma_sem):
                    evs.append(b.w)
            evs.extend(b.r.items())
        waits = {}
        for (s, v) in evs:
            if s == eng and (eng == "pe" or not SAME_ENGINE_SYNC):
                continue
            if s == is_dma_sem and False:
                continue
            if self.waited[eng].get(s, 0) >= v:
                continue
            if waits.get(s, 0) < v:
                waits[s] = v
        for s, v in waits.items():
            self.waited[eng][s] = v
            if s in ENGS:
                self.needed[s].add(v)
        return list(waits.items())

    def op(self, eng, fn, reads=(), writes=()):
        waits = self._deps(eng, reads, writes)
        self.cnt[eng] += 1
        ev = (eng, self.cnt[eng])
        self.q[eng].append((waits, fn, None, self.cnt[eng]))
        for b in reads:
            b.r[ev[0]] = ev[1]
        for b in writes:
            b.w = ev
            b.r = {}

    def dma(self, eng, fn, reads=(), writes=(), sem=None):
        if sem.dsem is None:
            sem.dsem = ("d", self.ndsem)
            self.ndsem += 1
        key = sem.dsem
        waits = self._deps(eng, reads, writes, is_dma_sem=key)
        sem.dcnt += 16
        self.dtotal[key] = sem.dcnt
        ev = (key, sem.dcnt)
        self.q[eng].append((waits, fn, key, None))
        for b in reads:
            b.r[ev[0]] = ev[1]
        for b in writes:
            b.w = ev
            b.r = {}

    def barrier(self):
        for e in ENGS:
            waits = {}
            for o in ENGS:
                if o != e and self.cnt[o] > 0 and self.waited[e].get(o, 0) < self.cnt[o]:
                    waits[o] = self.cnt[o]
            for k, v in self.dtotal.items():
                if self.waited[e].get(k, 0) < v:
                    waits[k] = v
            for s, v in waits.items():
                self.waited[e][s] = v
                if s in ENGS:
                    self.needed[s].add(v)
            if waits:
                self.q[e].append((list(waits.items()), None, None, None))

    def emit(self, nc, es):
        esem = {e: es.enter_context(nc.semaphore("prog_" + e)) for e in ENGS}
        dsem = {("d", i): es.enter_context(nc.semaphore("dma%d" % i)) for i in range(self.ndsem)}
        rank = {}
        for e in ENGS:
            srt = sorted(self.needed[e])
            rank[e] = {v: i + 1 for i, v in enumerate(srt)}
        block = es.enter_context(nc.Block())

        def run(e, engobj):
            for (waits, fn, dkey, seq) in self.q[e]:
                for (s, v) in waits:
                    if s in ENGS:
                        engobj.wait_ge(esem[s], rank[s][v])
                    else:
                        engobj.wait_ge(dsem[s], v)
                if fn is None:
                    continue
                ins = fn(engobj)
                if dkey is not None:
                    ins.then_inc(dsem[dkey], 16)
                elif seq in rank[e]:
                    ins.then_inc(esem[e], 1)

        @block.tensor
        def _(eng):
            run("pe", eng)

        @block.scalar
        def _(eng):
            run("act", eng)

        @block.vector
        def _(eng):
            run("dve", eng)

        @block.gpsimd
        def _(eng):
            run("pool", eng)

        @block.sync
        def _(eng):
            run("sp", eng)


def _prod(xs):
    n = 1
    for x in xs:
        n *= int(x)
    return n


_DTSIZE = {F32: 4, BF16: 2, U32: 4, I32: 4}


class Arena:
    def __init__(self, t, n16):
        self.t = t
        self.n = n16
        self.off = 0

    def reset(self, off=0):
        self.off = off

    def alloc(self, shape, dt):
        n = _prod(shape)
        e16 = (n * _DTSIZE[dt] + 1) // 2
        a = self.off
        self.off += (e16 + 15) // 16 * 16
        assert self.off <= self.n, ("arena overflow", self.off, self.n)
        v = self.t[:, a:a + e16]
        if dt != BF16:
            v = v.bitcast(dt)
        if len(shape) == 2:
            v = v.rearrange("p (a b) -> p a b", a=shape[0])
        elif len(shape) == 3:
            v = v.rearrange("p (a b c) -> p a b c", a=shape[0], b=shape[1])
        elif len(shape) == 4:
            v = v.rearrange("p (a b c d) -> p a b c d", a=shape[0], b=shape[1], c=shape[2])
        return v


SLOPES = [float(2.0 ** (-8.0 * (h + 1) / 8)) for h in range(8)]
GELU = AF.Gelu_apprx_tanh
RSTOP = 99
LG = [float(np.log(1.0 - 2.0 ** (-5.0 - h)).astype(np.float32)) for h in range(4)]
GAM = [float(np.exp(np.float32(x))) for x in LG]


def make_tables(SEQ):
    f32 = np.float32
    TW = SEQ + 384
    NT = SEQ // 128
    r = np.arange(128)[:, None]
    x = np.arange(TW)[None, :]
    d = x - r - 384
    dpos = np.maximum(d, 0).astype(f32)
    mult = (((d >= 0) & (d <= 128)).astype(f32) + ((d >= 0) & (d % 4 == 0) & (d <= 512)).astype(f32)
            + ((d >= 0) & (d % 16 == 0) & (d <= 2048)).astype(f32))
    caus = (d >= 0).astype(f32)
    pos = np.arange(NT)[None, :] * 128 + np.arange(128)[:, None]
    kdec = np.stack([np.exp((SEQ - 1 - pos) * LG[h]) * (128.0 ** -0.5) for h in range(4)], 1)
    rr = np.arange(128)
    sbias = np.zeros((128, 3, 8), f32)
    for g, dil in enumerate((1, 4, 16)):
        for h in range(8):
            sbias[:, g, h] = -SLOPES[h] * dil * (128 - rr)
    return {
        "dtab": dpos.astype(f32),
        "multt": mult.astype(ml_dtypes.bfloat16),
        "caust": caus.astype(ml_dtypes.bfloat16),
        "kdec": kdec.reshape(128, 4 * NT).astype(f32),
        "sbias": sbias.reshape(128, 24),
        "ident": np.eye(128, dtype=f32).astype(ml_dtypes.bfloat16),
        "iota": np.tile(np.arange(128, dtype=f32)[None, :], (128, 1)),
        "selrow": np.repeat(np.eye(16, dtype=f32), 128, axis=1).reshape(16, 2048).copy(),
        "sel16": np.repeat(np.eye(16, dtype=f32), 128, axis=0).reshape(16, 128, 16).transpose(1, 0, 2).reshape(128, 256).copy(),
    }


def build_program(SEQ=2048, NSEQ=2, NS=16, do_sample=True, do_peer=True, NE=16384, stop=99):
    nc = bass.Bass("TRN2", target_bir_lowering=False)
    es = ExitStack()
    P = Prog()
    NT = SEQ // 128
    NQG = SEQ // 512
    TW = SEQ + 384
    NTOK = NSEQ * SEQ
    NI = NE // 128
    NG = NI // 4

    def din(name, shape, dt=F32):
        return nc.dram_tensor(name, list(shape), dt, kind="ExternalInput").ap()

    def dout(name, shape, dt=F32):
        return nc.dram_tensor(name, list(shape), dt, kind="ExternalOutput").ap()

    def dscr(name, shape, dt):
        return nc.dram_tensor(name, list(shape), dt).ap()

    xp = din("xp", [NTOK, D])
    xs = din("xs", [NS, D])
    ck = din("ck", [NS, 2048, 512])
    cv = din("cv", [NS, 2048, 512])
    st = din("st", [NS * 4, 128, 128])
    g1 = din("g1", [128, 8])
    g2 = din("g2", [128, 8])
    gf = din("gf", [1, D])
    ggn = din("ggn", [1, 512])
    w_in = din("w_in", [D, INC])
    w_out = din("w_out", [D, D])
    w_pq = din("w_pq", [D, 2048])
    skT = din("skT", [128, 16 * 128])
    uT = din("uT", [D, NE])
    pv = din("pv", [NE, D])
    ident_d = din("ident", [128, 128], BF16)
    iota_d = din("iota", [128, 128])
    dtab_d = din("dtab", [128, TW])
    multt_d = din("multt", [128, TW], BF16)
    caust_d = din("caust", [128, TW], BF16)
    kdec_d = din("kdec", [128, 4 * NT])
    sbias_d = din("sbias", [128, 24])
    sel16_d = din("sel16", [128, 256])
    selrow_d = din("selrow", [16, 16 * 128])

    yp = dout("yp", [NTOK, D])
    ys = dout("ys", [NS, D])
    kwp = dout("kwp", [NTOK, 512])
    vwp = dout("vwp", [NTOK, 512])
    rp = dout("rp", [NSEQ * 4, 128, 128])
    kns = dout("kns", [NS, 512])
    vns = dout("vns", [NS, 512])
    rs = dout("rs", [NS * 4, 128, 128])

    uT_bf = dscr("uT_bf", [NE // 512, 128, 8 * 512], BF16)
    pv_bf = dscr("pv_bf", [NE // 512, 128, 4 * 1024], BF16)
    wout_bf = dscr("wout_bf", [D, D], BF16)
    wpq_bf = dscr("wpq_bf", [D, 2048], BF16)
    win_bf = dscr("win_bf", [D, INC], BF16)
    b_winbf = Buf("win_bf")
    b_uTbf = Buf("uT_bf")
    b_pvbf = Buf("pv_bf")
    b_woutbf = Buf("wout_bf")
    b_wpqbf = Buf("wpq_bf")

    def sb(name, shape, dt):
        return es.enter_context(nc.sbuf_tensor(name, list(shape), dt))

    def ps(name, shape, dt):
        return es.enter_context(nc.psum_tensor(name, list(shape), dt))

    ident = sb("ident_sb", [128, 128], BF16)
    iota = sb("iota_sb", [128, 128], F32)
    g1_sb = sb("g1_sb", [128, 8], F32)
    g2_sb = sb("g2_sb", [128, 8], F32)
    gf_bc = sb("gf_bc", [128, D], F32)
    ggn_bc = sb("ggn_bc", [128, 512], F32)
    catTok = sb("catTok", [128, NT, D], BF16)
    b_const = Buf("const")
    b_cat = [Buf("cat%d" % i) for i in range(NT)]
    for (dst, src) in ((ident[:], ident_d), (iota[:], iota_d), (g1_sb[:], g1), (g2_sb[:], g2),
                       (gf_bc[:], gf.broadcast_to([128, D])), (ggn_bc[:], ggn.broadcast_to([128, 512]))):
        P.dma("sp", lambda e, dst=dst, src=src: e.dma_start(out=dst, in_=src), writes=[b_const], sem=b_const)

    for r0 in range(0, D, 128):
        P.dma("pool", lambda e, r0=r0: e.dma_start(out=win_bf[r0:r0 + 128, :], in_=w_in[r0:r0 + 128, :]),
              writes=[b_winbf], sem=b_winbf)
    if do_peer:
        RC = 128
        for r0 in range(0, D, RC):
            P.dma("pool", lambda e, r0=r0: e.dma_start(out=wout_bf[r0:r0 + RC, :], in_=w_out[r0:r0 + RC, :]),
                  writes=[b_woutbf], sem=b_woutbf)
            P.dma("pool", lambda e, r0=r0: e.dma_start(out=wpq_bf[r0:r0 + RC, :], in_=w_pq[r0:r0 + RC, :]),
                  writes=[b_wpqbf], sem=b_wpqbf)
        for g in range(NE // 512):
            for k in range(8):
                P.dma("pool", lambda e, g=g, k=k: e.dma_start(out=uT_bf[g, :, k * 512:(k + 1) * 512],
                                                             in_=uT[k * 128:(k + 1) * 128, g * 512:(g + 1) * 512]),
                      writes=[b_uTbf], sem=b_uTbf)
            for c in range(4):
                P.dma("pool", lambda e, g=g, c=c: e.dma_start(out=pv_bf[g, :, c * 1024:(c + 1) * 1024],
                                                             in_=pv[(g * 4 + c) * 128:(g * 4 + c + 1) * 128, :]),
                      writes=[b_pvbf], sem=b_pvbf)

    pbank = [ps("pb%d" % i, [128, 512], F32) for i in range(8)]
    b_pb = [Buf("pb%d" % i, excl=True) for i in range(8)]

    AR16 = (nc.sbuf_bytes_remaining - 2048) // 2 // 16 * 16
    arena_t = sb("arena", [128, AR16], BF16)
    A = Arena(arena_t, AR16)

    def rmsnorm_T(src_ap, b_src, xb_ap, b_xb, junk_ap, b_junk, stat_ap, b_stat, pst_i, gain_ap, dst_ap, b_dst, tp=128):
        src_ap = src_ap[0:tp]
        xb_ap = xb_ap[0:tp]
        junk_ap = junk_ap[0:tp]
        stat_ap = stat_ap[0:tp]
        P.op("act", lambda e: e.activation(out=junk_ap, in_=src_ap, func=AF.Square, accum_out=stat_ap[:, 0:1]),
             reads=[b_src], writes=[b_junk, b_stat])
        P.op("dve", lambda e: e.tensor_scalar(out=stat_ap[:, 1:2], in0=stat_ap[:, 0:1], scalar1=1.0 / D, scalar2=1e-6,
                                              op0=ALU.mult, op1=ALU.add), reads=[b_stat], writes=[b_stat])
        P.op("act", lambda e: e.activation(out=stat_ap[:, 2:3], in_=stat_ap[:, 1:2], func=AF.Sqrt),
             reads=[b_stat], writes=[b_stat])
        P.op("dve", lambda e: e.reciprocal(out=stat_ap[:, 3:4], in_=stat_ap[:, 2:3]), reads=[b_stat], writes=[b_stat])
        P.op("dve", lambda e: e.tensor_scalar(out=xb_ap, in0=src_ap, scalar1=stat_ap[:, 3:4], scalar2=None, op0=ALU.mult),
             reads=[b_src, b_stat], writes=[b_xb])
        pst = pbank[pst_i][:].bitcast(BF16)
        for k in range(8):
            P.op("pe", lambda e, k=k: e.transpose(out=pst[:, k * 128:k * 128 + tp], in_=xb_ap[:, k * 128:(k + 1) * 128],
                                                  identity=ident[0:tp, 0:tp]),
                 reads=[b_xb, b_const], writes=[b_pb[pst_i]])
        P.op("dve", lambda e: e.tensor_tensor(out=dst_ap, in0=pst.rearrange("p (k t) -> p k t", k=8)[:, :, 0:tp],
                                              in1=gain_ap.unsqueeze(2).broadcast_to([128, 8, tp]), op=ALU.mult),
             reads=[b_pb[pst_i], b_const], writes=[b_dst])

    def groupnorm(tp, src, b_src, gain_ap, gate_ap, b_gate, out_ap, b_outs, gn_a, gn_b, gn_s, b_gn):
        ga, gb_, gs = gn_a[0:tp], gn_b[0:tp], gn_s[0:tp]
        P.op("dve", lambda e: e.tensor_reduce(out=gs[:, 0:4], in_=src, axis=AX.X, op=ALU.add), reads=[b_src], writes=[b_gn])
        P.op("dve", lambda e: e.tensor_scalar(out=gs[:, 0:4], in0=gs[:, 0:4], scalar1=1.0 / 128, scalar2=None, op0=ALU.mult),
             reads=[b_gn], writes=[b_gn])
        P.op("dve", lambda e: e.tensor_tensor(out=ga, in0=src, in1=gs[:, 0:4].unsqueeze(2).broadcast_to([tp, 4, 128]),
                                              op=ALU.subtract), reads=[b_src, b_gn], writes=[b_gn])
        P.op("dve", lambda e: e.tensor_tensor(out=gb_, in0=ga, in1=ga, op=ALU.mult), reads=[b_gn], writes=[b_gn])
        P.op("dve", lambda e: e.tensor_reduce(out=gs[:, 4:8], in_=gb_, axis=AX.X, op=ALU.add), reads=[b_gn], writes=[b_gn])
        P.op("dve", lambda e: e.tensor_scalar(out=gs[:, 4:8], in0=gs[:, 4:8], scalar1=1.0 / 128, scalar2=1e-5,
                                              op0=ALU.mult, op1=ALU.add), reads=[b_gn], writes=[b_gn])
        P.op("act", lambda e: e.activation(out=gs[:, 8:12], in_=gs[:, 4:8], func=AF.Sqrt), reads=[b_gn], writes=[b_gn])
        P.op("dve", lambda e: e.reciprocal(out=gs[:, 12:16], in_=gs[:, 8:12]), reads=[b_gn], writes=[b_gn])
        P.op("dve", lambda e: e.tensor_tensor(out=ga, in0=ga, in1=gs[:, 12:16].unsqueeze(2).broadcast_to([tp, 4, 128]),
                                              op=ALU.mult), reads=[b_gn], writes=[b_gn])
        P.op("dve", lambda e: e.tensor_tensor(out=ga, in0=ga, in1=gain_ap, op=ALU.mult), reads=[b_gn, b_const], writes=[b_gn])
        P.op("dve", lambda e: e.tensor_tensor(out=out_ap, in0=ga, in1=gate_ap, op=ALU.mult), reads=[b_gn, b_gate], writes=b_outs)

    cat_s = catTok[:, 0, :]
    b_cats = Buf("cat_s")

    def sample_mixer():
        tp = NS
        P.barrier()
        A.reset()
        xs_sb = A.alloc([D], F32)
        b_xs = Buf("xs")
        xsb = A.alloc([D], BF16)
        b_xsb = Buf("xsb")
        junk = A.alloc([D], BF16)
        b_junk = Buf("junk_s")
        stat_s = A.alloc([4], F32)
        b_stat_s = Buf("stat_s")
        xsT = A.alloc([8, 16], BF16)
        b_xsT = Buf("xsT")
        wblk = [A.alloc([8, 512], BF16) for _ in range(2)]
        b_wblk = [Buf("wblk%d" % i) for i in range(2)]
        z = A.alloc([INC], F32)
        b_z = Buf("z")
        selcol = A.alloc([16, 16], F32)
        selrow = A.alloc([16 * 128], F32)
        sbias_sb = A.alloc([3, 8], F32)
        identf = A.alloc([128], F32)
        b_sc = Buf("sconst")
        kg = [A.alloc([512], F32) for _ in range(2)]
        b_kg = [Buf("kg%d" % i) for i in range(2)]
        vg_ = [A.alloc([512], F32) for _ in range(2)]
        b_vg = [Buf("vg%d" % i) for i in range(2)]
        prod = A.alloc([512], F32)
        b_prod = Buf("prod")
        sc = A.alloc([8], F32)
        b_scr = Buf("sc")
        p_ = [A.alloc([8], F32) for _ in range(2)]
        b_p = [Buf("p%d" % i) for i in range(2)]
        pvt = [A.alloc([512], F32) for _ in range(2)]
        b_pvt = [Buf("pvt%d" % i) for i in range(2)]
        tmp16 = A.alloc([512], F32)
        sself = A.alloc([8], F32)
        pself = A.alloc([8], F32)
        den = A.alloc([8], F32)
        rden = A.alloc([8], F32)
        b_tm = Buf("tok_misc")
        catf = A.alloc([D], F32)
        b_catf = Buf("catf")
        rqT = A.alloc([4, 16], F32)
        b_rqT = Buf("rqT")
        qm = A.alloc([4, 16, 16], F32)
        b_qm = Buf("qm")
        S_sb = [A.alloc([128], F32) for _ in range(2)]
        b_S = [Buf("S%d" % i) for i in range(2)]
        sout = [A.alloc([128], F32) for _ in range(2)]
        b_sout = [Buf("sout%d" % i) for i in range(2)]
        kmask = [A.alloc([128], F32) for _ in range(2)]
        b_kmask = [Buf("kmask%d" % i) for i in range(2)]
        ro = A.alloc([4, 128], F32)
        b_ro = Buf("ro")
        qk = A.alloc([4], F32)
        sgs = A.alloc([4, 128], F32)
        b_sgs = Buf("sgs")
        gn_a = A.alloc([4, 128], F32)
        gn_b = A.alloc([4, 128], F32)
        gn_s = A.alloc([16], F32)
        b_gn = Buf("gn_s")

        P.dma("sp", lambda e: e.dma_start(out=selcol, in_=sel16_d.rearrange("p (a b) -> p a b", a=16)), writes=[b_sc], sem=b_sc)
        P.dma("sp", lambda e: e.dma_start(out=selrow[0:16], in_=selrow_d), writes=[b_sc], sem=b_sc)
        P.dma("sp", lambda e: e.dma_start(out=sbias_sb, in_=sbias_d.rearrange("p (a b) -> p a b", a=3)), writes=[b_sc], sem=b_sc)
        P.op("dve", lambda e: e.tensor_copy(out=identf, in_=ident[:]), reads=[b_const], writes=[b_sc])
        P.dma("sp", lambda e: e.dma_start(out=xs_sb[0:tp], in_=xs), writes=[b_xs], sem=b_xs)
        rmsnorm_T(xs_sb, b_xs, xsb, b_xsb, junk, b_junk, stat_s, b_stat_s, 0, g1_sb[:], xsT[:, :, 0:tp], b_xsT, tp=tp)
        for blk in range(7):
            jb = blk % 2
            P.dma("sp", lambda e, blk=blk, jb=jb: e.dma_start(
                out=wblk[jb], in_=win_bf[:, blk * 512:(blk + 1) * 512].rearrange("(k p) c -> p k c", p=128)),
                reads=[b_winbf], writes=[b_wblk[jb]], sem=b_wblk[jb])
            pa = 2 + jb
            for k in range(8):
                P.op("pe", lambda e, k=k, jb=jb, pa=pa: e.matmul(pbank[pa][0:tp, :], lhsT=xsT[:, k, 0:tp], rhs=wblk[jb][:, k, :],
                                                                start=(k == 0), stop=(k == 7)),
                     reads=[b_xsT, b_wblk[jb]], writes=[b_pb[pa]])
            P.op("act", lambda e, blk=blk, pa=pa: e.copy(out=z[0:tp, blk * 512:(blk + 1) * 512], in_=pbank[pa][0:tp, :]),
                 reads=[b_pb[pa]], writes=[b_z])
        P.dma("sp", lambda e: e.dma_start(out=kns, in_=z[0:tp, 512:1024]), reads=[b_z], sem=b_z)
        P.dma("sp", lambda e: e.dma_start(out=vns, in_=z[0:tp, 1024:1536]), reads=[b_z], sem=b_z)

        RT, WT_ = [b_tm], [b_tm]
        P.op("dve", lambda e: e.tensor_tensor(out=tmp16[0:tp], in0=z[0:tp, 0:512], in1=z[0:tp, 512:1024], op=ALU.mult),
             reads=[b_z], writes=WT_)
        P.op("dve", lambda e: e.tensor_reduce(out=sself[0:tp], in_=tmp16[0:tp].rearrange("p (h c) -> p h c", h=8), axis=AX.X, op=ALU.add),
             reads=RT, writes=WT_)
        P.op("act", lambda e: e.activation(out=pself[0:tp], in_=sself[0:tp], func=AF.Exp, scale=0.125), reads=RT, writes=WT_)
        P.op("dve", lambda e: e.tensor_scalar(out=pself[0:tp], in0=pself[0:tp], scalar1=3.0, scalar2=None, op0=ALU.mult), reads=RT, writes=WT_)
        NTOT = tp * 3
        for b in range(tp):
            P.op("pe", lambda e, b=b: e.matmul(pbank[4][:], lhsT=selrow[0:16, b * 128:(b + 1) * 128], rhs=z[0:16, 0:512],
                                               start=True, stop=True), reads=[b_sc, b_z], writes=[b_pb[4]])
            for g, dil in enumerate((1, 4, 16)):
                n = b * 3 + g
                jj = n % 2
                st0 = 2048 - 128 * dil
                if dil == 1:
                    ksrc = ck[b, st0:2048, :]
                    vsrc = cv[b, st0:2048, :]
                else:
                    ksrc = ck[b, st0:2048, :].rearrange("(r d) c -> r d c", d=dil)[:, 0, :]
                    vsrc = cv[b, st0:2048, :].rearrange("(r d) c -> r d c", d=dil)[:, 0, :]
                P.dma("sp", lambda e, jj=jj, ksrc=ksrc: e.dma_start(out=kg[jj], in_=ksrc), writes=[b_kg[jj]], sem=b_kg[jj])
                P.dma("sp", lambda e, jj=jj, vsrc=vsrc: e.dma_start(out=vg_[jj], in_=vsrc), writes=[b_vg[jj]], sem=b_vg[jj])
                P.op("dve", lambda e, jj=jj: e.tensor_tensor(out=prod, in0=kg[jj], in1=pbank[4][:], op=ALU.mult),
                     reads=[b_kg[jj], b_pb[4]], writes=[b_prod])
                P.op("dve", lambda e: e.tensor_reduce(out=sc, in_=prod.rearrange("p (h c) -> p h c", h=8), axis=AX.X, op=ALU.add),
                     reads=[b_prod], writes=[b_scr])
                P.op("dve", lambda e, g=g: e.scalar_tensor_tensor(out=sc, in0=sc, scalar=0.125, in1=sbias_sb[:, g, :],
                                                                 op0=ALU.mult, op1=ALU.add), reads=[b_scr, b_sc], writes=[b_scr])
                P.op("act", lambda e, jj=jj: e.activation(out=p_[jj], in_=sc, func=AF.Exp), reads=[b_scr], writes=[b_p[jj]])
                P.op("dve", lambda e, jj=jj: e.tensor_tensor(out=pvt[jj].rearrange("p (h c) -> p h c", h=8),
                                                             in0=vg_[jj].rearrange("p (h c) -> p h c", h=8),
                                                             in1=p_[jj].unsqueeze(2).broadcast_to([128, 8, 64]), op=ALU.mult),
                     reads=[b_vg[jj], b_p[jj]], writes=[b_pvt[jj]])
                P.op("pe", lambda e, jj=jj, b=b, n=n: e.matmul(pbank[5][0:16, :], lhsT=selcol[:, b, :], rhs=pvt[jj],
                                                              start=(n == 0), stop=(n == NTOT - 1)),
                     reads=[b_pvt[jj], b_sc], writes=[b_pb[5]])
                P.op("pe", lambda e, jj=jj, b=b, n=n: e.matmul(pbank[6][0:16, 0:8], lhsT=selcol[:, b, :], rhs=p_[jj],
                                                              start=(n == 0), stop=(n == NTOT - 1)),
                     reads=[b_p[jj], b_sc], writes=[b_pb[6]])
        P.op("dve", lambda e: e.tensor_tensor(out=tmp16[0:tp].rearrange("p (h c) -> p h c", h=8),
                                              in0=z[0:tp, 1024:1536].rearrange("p (h c) -> p h c", h=8),
                                              in1=pself[0:tp].unsqueeze(2).broadcast_to([tp, 8, 64]), op=ALU.mult),
             reads=[b_z, b_tm], writes=WT_)
        P.op("dve", lambda e: e.tensor_tensor(out=tmp16[0:tp], in0=tmp16[0:tp], in1=pbank[5][0:tp, :], op=ALU.add),
             reads=[b_tm, b_pb[5]], writes=WT_)
        P.op("dve", lambda e: e.tensor_tensor(out=den[0:tp], in0=pself[0:tp], in1=pbank[6][0:tp, 0:8], op=ALU.add),
             reads=[b_tm, b_pb[6]], writes=WT_)
        P.op("dve", lambda e: e.reciprocal(out=rden[0:tp], in_=den[0:tp]), reads=RT, writes=WT_)
        P.op("dve", lambda e: e.tensor_tensor(out=catf[0:tp, 0:512].rearrange("p (h c) -> p h c", h=8),
                                              in0=tmp16[0:tp].rearrange("p (h c) -> p h c", h=8),
                                              in1=rden[0:tp].unsqueeze(2).broadcast_to([tp, 8, 64]), op=ALU.mult),
             reads=RT, writes=[b_catf])

        for r in range(4):
            P.op("pe", lambda e, r=r: e.transpose(out=pbank[1][:, r * 16:(r + 1) * 16], in_=z[0:tp, 1536 + r * 128:1536 + (r + 1) * 128],
                                                  identity=identf[0:tp, 0:tp]), reads=[b_z, b_sc], writes=[b_pb[1]])
        P.op("act", lambda e: e.copy(out=rqT, in_=pbank[1][:, 0:64].rearrange("p (r b) -> p r b", r=4)), reads=[b_pb[1]], writes=[b_rqT])
        for r in range(4):
            P.op("dve", lambda e, r=r: e.tensor_tensor(out=qm[:, r], in0=rqT[:, r, :].unsqueeze(1).broadcast_to([128, 16, 16]),
                                                       in1=selcol, op=ALU.mult), reads=[b_rqT, b_sc], writes=[b_qm])
        for r in range(4):
            for b in range(tp):
                n = r * tp + b
                jj = n % 2
                P.dma("sp", lambda e, jj=jj, b=b, r=r: e.dma_start(out=S_sb[jj], in_=st[b * 4 + r]), writes=[b_S[jj]], sem=b_S[jj])
                P.op("pe", lambda e, jj=jj, b=b, r=r: e.matmul(pbank[7][0:16, r * 128:(r + 1) * 128], lhsT=qm[:, r, b, :], rhs=S_sb[jj],
                                                              start=(b == 0), stop=(b == tp - 1)),
                     reads=[b_qm, b_S[jj]], writes=[b_pb[7]])
                P.op("dve", lambda e, jj=jj, b=b, r=r: e.tensor_scalar(out=kmask[jj][0:tp], in0=z[0:tp, 2048 + r * 128:2048 + (r + 1) * 128],
                                                                      scalar1=identf[0:tp, b:b + 1], scalar2=128.0 ** -0.5,
                                                                      op0=ALU.mult, op1=ALU.mult),
                     reads=[b_z, b_sc], writes=[b_kmask[jj]])
                pa = 2 + jj
                P.op("pe", lambda e, jj=jj, r=r, pa=pa: e.matmul(pbank[pa][:, 0:128], lhsT=kmask[jj][0:tp], rhs=z[0:tp, 2560 + r * 128:2560 + (r + 1) * 128],
                                                                start=True, stop=True),
                     reads=[b_kmask[jj], b_z], writes=[b_pb[pa]])
                P.op("dve", lambda e, jj=jj, r=r, pa=pa: e.scalar_tensor_tensor(out=sout[jj], in0=S_sb[jj], scalar=GAM[r], in1=pbank[pa][:, 0:128],
                                                                               op0=ALU.mult, op1=ALU.add),
                     reads=[b_S[jj], b_pb[pa]], writes=[b_sout[jj]])
                P.dma("sp", lambda e, jj=jj, b=b, r=r: e.dma_start(out=rs[b * 4 + r], in_=sout[jj]), reads=[b_sout[jj]], sem=b_sout[jj])
        P.op("dve", lambda e: e.tensor_tensor(out=tmp16[0:tp], in0=z[0:tp, 1536:2048], in1=z[0:tp, 2048:2560], op=ALU.mult),
             reads=[b_z, b_tm], writes=WT_)
        P.op("dve", lambda e: e.tensor_reduce(out=qk[0:tp], in_=tmp16[0:tp].rearrange("p (r c) -> p r c", r=4), axis=AX.X, op=ALU.add),
             reads=RT, writes=WT_)
        P.op("dve", lambda e: e.scalar_tensor_tensor(out=ro[0:tp], in0=z[0:tp, 2560:3072].rearrange("p (r c) -> p r c", r=4),
                                                     scalar=128.0 ** -0.5, in1=qk[0:tp].unsqueeze(2).broadcast_to([tp, 4, 128]),
                                                     op0=ALU.mult, op1=ALU.mult), reads=[b_z, b_tm], writes=[b_ro])
        for r in range(4):
            P.op("dve", lambda e, r=r: e.scalar_tensor_tensor(out=ro[0:tp, r, :], in0=pbank[7][0:tp, r * 128:(r + 1) * 128], scalar=GAM[r],
                                                             in1=ro[0:tp, r, :], op0=ALU.mult, op1=ALU.add),
                 reads=[b_pb[7], b_ro], writes=[b_ro])
        P.op("act", lambda e: e.activation(out=sgs[0:tp], in_=z[0:tp, 3072:3584].rearrange("p (r c) -> p r c", r=4), func=AF.Silu),
             reads=[b_z], writes=[b_sgs])
        groupnorm(tp, ro[0:tp], b_ro, ggn_bc[0:tp, :].rearrange("p (r c) -> p r c", r=4), sgs[0:tp], b_sgs,
                  catf[0:tp, 512:1024].rearrange("p (r c) -> p r c", r=4), [b_catf], gn_a, gn_b, gn_s, b_gn)
        P.op("act", lambda e: e.copy(out=cat_s[0:tp, :], in_=catf[0:tp, :]), reads=[b_catf], writes=[b_cats])

    def prompt_mixer(s):
        if s > 0:
            P.barrier()
        A.reset()
        dtab = A.alloc([TW], F32)
        multt = A.alloc([TW], BF16)
        caust = A.alloc([TW], BF16)
        kdec = A.alloc([4 * NT], F32)
        b_tab = Buf("tab")
        for (dst, src) in ((dtab, dtab_d), (multt, multt_d), (caust, caust_d), (kdec, kdec_d)):
            P.dma("sp", lambda e, dst=dst, src=src: e.dma_start(out=dst, in_=src), writes=[b_tab], sem=b_tab)
        xt = [A.alloc([D], F32) for i in range(2)]
        b_xt = [Buf("xt%d" % i) for i in range(2)]
        xb = [A.alloc([D], BF16) for i in range(2)]
        b_xb = [Buf("xb%d" % i) for i in range(2)]
        junk = A.alloc([D], BF16)
        b_junk = Buf("junk")
        stat = A.alloc([NT, 4], F32)
        b_stat = [Buf("stat%d" % i) for i in range(NT)]
        xT = A.alloc([8, SEQ], BF16)
        b_xT = [Buf("xT%d" % i) for i in range(NT)]
        wkv = A.alloc([8, 1024], BF16)
        b_wkv = Buf("wkv")
        kvst = [A.alloc([1024], F32) for i in range(2)]
        b_kvst = [Buf("kvst%d" % i) for i in range(2)]
        wu = [A.alloc([8, 4, 128], BF16) for i in range(2)]
        b_wu = [Buf("wu%d" % i) for i in range(2)]
        QT = [A.alloc([SEQ], BF16) for i in range(2)]
        KT = [A.alloc([SEQ], BF16) for i in range(2)]
        b_QT = [Buf("QT%d" % i) for i in range(2)]
        b_KT = [Buf("KT%d" % i) for i in range(2)]
        Vb = [A.alloc([NT, 130], BF16) for i in range(2)]
        b_Vb = [Buf("Vb%d" % i) for i in range(2)]
        Kd = A.alloc([NT, 128], BF16)
        b_Kd = Buf("Kd")
        sg = A.alloc([NT, 128], BF16)
        b_sg = Buf("sg")
        Et = [A.alloc([TW], BF16) for i in range(2)]
        b_Et = [Buf("Et%d" % i) for i in range(2)]
        Etmp = A.alloc([TW], BF16)
        b_Etmp = Buf("Etmp")
        NPB = 4
        Pe = [A.alloc([512], BF16) for i in range(NPB)]
        b_Pe = [Buf("Pe%d" % i) for i in range(NPB)]
        Pm = [A.alloc([512], BF16) for i in range(NPB)]
        b_Pm = [Buf("Pm%d" % i) for i in range(NPB)]
        gn_a = A.alloc([4, 128], F32)
        gn_b = A.alloc([4, 128], F32)
        gn_s = A.alloc([16], F32)
        b_gn = Buf("gn")
        rden = A.alloc([4], F32)
        b_rden = Buf("rden")
        sfin = A.alloc([128], F32)
        b_sfin = Buf("sfin")

        for k in range(8):
            P.dma("sp", lambda e, k=k: e.dma_start(out=wkv[:, k, :], in_=win_bf[k * 128:(k + 1) * 128, 512:1536]),
                  reads=[b_winbf], writes=[b_wkv], sem=b_wkv)

        for i in range(NT):
            j = i % 2
            r0 = s * SEQ + i * 128
            P.dma("sp", lambda e, j=j, r0=r0: e.dma_start(out=xt[j], in_=xp[r0:r0 + 128, :]),
                  writes=[b_xt[j]], sem=b_xt[j])
            rmsnorm_T(xt[j], b_xt[j], xb[j], b_xb[j], junk, b_junk, stat[:, i, :], b_stat[i], j, g1_sb[:],
                      xT[:, :, i * 128:(i + 1) * 128], b_xT[i])

        for i in range(NT):
            j = i % 2
            r0 = s * SEQ + i * 128
            for half in range(2):
                pa = 2 + 2 * j + half
                for k in range(8):
                    P.op("pe", lambda e, pa=pa, k=k, i=i, half=half: e.matmul(
                        pbank[pa][:], lhsT=xT[:, k, i * 128:(i + 1) * 128], rhs=wkv[:, k, half * 512:(half + 1) * 512],
                        start=(k == 0), stop=(k == 7)),
                        reads=[b_xT[i], b_wkv], writes=[b_pb[pa]])
                P.op("act", lambda e, pa=pa, j=j, half=half: e.copy(out=kvst[j][:, half * 512:(half + 1) * 512],
                                                                   in_=pbank[pa][:]),
                     reads=[b_pb[pa]], writes=[b_kvst[j]])
            P.dma("sp", lambda e, j=j, r0=r0: e.dma_start(out=kwp[r0:r0 + 128, :], in_=kvst[j][:, 0:512]),
                  reads=[b_kvst[j]], sem=b_kvst[j])
            P.dma("sp", lambda e, j=j, r0=r0: e.dma_start(out=vwp[r0:r0 + 128, :], in_=kvst[j][:, 512:1024]),
                  reads=[b_kvst[j]], sem=b_kvst[j])

        if stop <= 1:
            return
        for j in range(2):
            P.op("pool", lambda e, j=j: e.memset(Vb[j][:, :, 64:65], 1.0), writes=[b_Vb[j]])
            P.op("pool", lambda e, j=j: e.memset(Vb[j][:, :, 129:130], 1.0), writes=[b_Vb[j]])

        for u in range(8):
            if u >= stop - 10:
                return
            j = u % 2
            is_att = u < 4
            if is_att:
                cols = [u * 128, 512 + u * 128, 1024 + u * 128]
            else:
                r = u - 4
                cols = [1536 + r * 128, 2048 + r * 128, 2560 + r * 128, 3072 + r * 128]
            for m, c0 in enumerate(cols):
                P.dma("sp", lambda e, j=j, m=m, c0=c0: e.dma_start(
                    out=wu[j][:, :, m, :], in_=win_bf[:, c0:c0 + 128].rearrange("(k p) c -> p k c", p=128)),
                    reads=[b_winbf], writes=[b_wu[j]], sem=b_wu[j])
            for tg in range(NQG):
                for m in range(2):
                    pa = 2 + (2 * tg + m) % 4
                    for k in range(8):
                        P.op("pe", lambda e, pa=pa, k=k, m=m, tg=tg, j=j: e.matmul(
                            pbank[pa][:], lhsT=wu[j][:, k, m, :], rhs=xT[:, k, tg * 512:(tg + 1) * 512],
                            start=(k == 0), stop=(k == 7)),
                            reads=[b_wu[j]] + b_xT[4 * tg:4 * tg + 4], writes=[b_pb[pa]])
                    dst = (QT if m == 0 else KT)[j]
                    bd = (b_QT if m == 0 else b_KT)[j]
                    sc = 1.0 if (is_att or m == 0) else 128.0 ** -0.5
                    P.op("act", lambda e, pa=pa, dst=dst, tg=tg, sc=sc: e.activation(
                        out=dst[:, tg * 512:(tg + 1) * 512], in_=pbank[pa][:], func=AF.Copy, scale=sc),
                        reads=[b_pb[pa]], writes=[bd])
            if u == 4 and RSTOP <= 1:
                return
            for i in range(NT):
                pa = 2 + i % 4
                ncol = 128 if is_att else 384
                m0 = 2 if is_att else 1
                for k in range(8):
                    P.op("pe", lambda e, pa=pa, k=k, i=i, j=j, m0=m0, ncol=ncol: e.matmul(
                        pbank[pa][:, 0:ncol], lhsT=xT[:, k, i * 128:(i + 1) * 128],
                        rhs=wu[j][:, k, m0:m0 + ncol // 128, :].rearrange("p m c -> p (m c)"),
                        start=(k == 0), stop=(k == 7)),
                        reads=[b_xT[i], b_wu[j]], writes=[b_pb[pa]])
                if is_att:
                    P.op("act", lambda e, pa=pa, i=i, j=j: e.copy(
                        out=Vb[j][:, i, :].rearrange("p (h c) -> p h c", h=2)[:, :, 0:64],
                        in_=pbank[pa][:, 0:128].rearrange("p (h c) -> p h c", h=2)),
                        reads=[b_pb[pa]], writes=[b_Vb[j]])
                else:
                    r = u - 4
                    P.op("dve", lambda e, pa=pa, i=i, r=r: e.tensor_scalar(
                        out=Kd[:, i, :], in0=pbank[pa][:, 0:128], scalar1=kdec[:, r * NT + i:r * NT + i + 1],
                        scalar2=None, op0=ALU.mult), reads=[b_pb[pa], b_tab], writes=[b_Kd])
                    P.op("act", lambda e, pa=pa, i=i, j=j: e.copy(out=Vb[j][:, i, 0:128], in_=pbank[pa][:, 128:256]),
                         reads=[b_pb[pa]], writes=[b_Vb[j]])
                    P.op("act", lambda e, pa=pa, i=i: e.activation(out=sg[:, i, :], in_=pbank[pa][:, 256:384], func=AF.Silu),
                         reads=[b_pb[pa]], writes=[b_sg])

            if stop <= 2 or (u == 4 and RSTOP <= 2):
                return
            heads = [0, 1] if is_att else [0]
            for hl in heads:
                if is_att:
                    h = 2 * u + hl
                    prow = slice(hl * 64, hl * 64 + 64)
                    tabsc = -SLOPES[h]
                    mt = multt
                    vw = 65
                else:
                    r = u - 4
                    prow = slice(0, 128)
                    tabsc = LG[r]
                    mt = caust
                    vw = 128
                ej = (2 * u + hl) % 2
                P.op("act", lambda e, tabsc=tabsc: e.activation(out=Etmp, in_=dtab, func=AF.Exp, scale=tabsc),
                     reads=[b_tab], writes=[b_Etmp])
                P.op("dve", lambda e, ej=ej, mt=mt: e.tensor_tensor(out=Et[ej], in0=Etmp, in1=mt, op=ALU.mult),
                     reads=[b_Etmp, b_tab], writes=[b_Et[ej]])
                items = [(qg, kb) for qg in range(NQG) for kb in range(4 * qg + 4)]
                LA = 3

                def front(n, qg, kb, j=j, prow=prow, ej=ej):
                    pa = 2 + n % 4
                    pb_ = n % NPB
                    off = qg * 512 - kb * 128 + 384
                    P.op("pe", lambda e: e.matmul(pbank[pa][:], lhsT=KT[j][prow, kb * 128:(kb + 1) * 128],
                                                  rhs=QT[j][prow, qg * 512:(qg + 1) * 512], start=True, stop=True),
                         reads=[b_KT[j], b_QT[j]], writes=[b_pb[pa]])
                    if is_att:
                        P.op("act", lambda e: e.activation(out=Pe[pb_], in_=pbank[pa][:], func=AF.Exp, scale=0.125),
                             reads=[b_pb[pa]], writes=[b_Pe[pb_]])
                        P.op("dve", lambda e: e.tensor_tensor(out=Pm[pb_], in0=Pe[pb_], in1=Et[ej][:, off:off + 512],
                                                              op=ALU.mult),
                             reads=[b_Pe[pb_], b_Et[ej]], writes=[b_Pm[pb_]])
                    else:
                        P.op("dve", lambda e: e.tensor_tensor(out=Pm[pb_], in0=pbank[pa][:], in1=Et[ej][:, off:off + 512],
                                                              op=ALU.mult),
                             reads=[b_pb[pa], b_Et[ej]], writes=[b_Pm[pb_]])

                def back(n, qg, kb, j=j, hl=hl, vw=vw, u=u):
                    pb_ = n % NPB
                    po = qg % 2
                    ov = pbank[po][:].rearrange("p (q c) -> p q c", q=4)
                    for qb in range(4):
                        QB = 4 * qg + qb
                        if kb > QB:
                            continue
                        if is_att:
                            rhs = Vb[j][:, kb, hl * 65:hl * 65 + 65]
                        else:
                            rhs = Vb[j][:, kb, 0:128]
                        P.op("pe", lambda e, qb=qb, QB=QB, rhs=rhs: e.matmul(
                            ov[:, qb, 0:vw], lhsT=Pm[pb_][:, qb * 128:(qb + 1) * 128], rhs=rhs,
                            start=(kb == 0 and qb == 0), stop=(kb == QB), skip_group_check=True),
                            reads=[b_Pm[pb_], b_Vb[j]], writes=[b_pb[po]])
                    if kb == 4 * qg + 3:
                        bc = b_cat[4 * qg:4 * qg + 4]
                        if is_att:
                            h = 2 * u + hl
                            P.op("dve", lambda e: e.reciprocal(out=rden, in_=ov[:, :, 64]), reads=[b_pb[po]], writes=[b_rden])
                            P.op("dve", lambda e: e.tensor_tensor(
                                out=catTok[:, 4 * qg:4 * qg + 4, h * 64:(h + 1) * 64], in0=ov[:, :, 0:64],
                                in1=rden.unsqueeze(2).broadcast_to([128, 4, 64]), op=ALU.mult),
                                reads=[b_pb[po], b_rden], writes=bc)
                        else:
                            r = u - 4
                            groupnorm(128, ov, b_pb[po],
                                      ggn_bc[:, r * 128:(r + 1) * 128].unsqueeze(1).broadcast_to([128, 4, 128]),
                                      sg[:, 4 * qg:4 * qg + 4, :], b_sg,
                                      catTok[:, 4 * qg:4 * qg + 4, 512 + r * 128:512 + (r + 1) * 128], bc,
                                      gn_a, gn_b, gn_s, b_gn)

                for n in range(len(items) + LA):
                    if n < len(items):
                        front(n, *items[n])
                    if n >= LA:
                        back(n - LA, *items[n - LA])

            if stop <= 3 or (u == 4 and RSTOP <= 3):
                return
            if not is_att:
                r = u - 4
                pa = 2
                for kb in range(NT):
                    P.op("pe", lambda e, kb=kb, j=j: e.matmul(pbank[pa][:, 0:128], lhsT=Kd[:, kb, :], rhs=Vb[j][:, kb, 0:128],
                                                              start=(kb == 0), stop=(kb == NT - 1)),
                         reads=[b_Kd, b_Vb[j]], writes=[b_pb[pa]])
                P.op("act", lambda e: e.copy(out=sfin, in_=pbank[pa][:, 0:128]), reads=[b_pb[pa]], writes=[b_sfin])
                P.dma("sp", lambda e, r=r: e.dma_start(out=rp[s * 4 + r], in_=sfin), reads=[b_sfin], sem=b_sfin)


    def peer_phase(tp, nsub, n_ptiles, cat_src, x_src, y_dst):
        P.barrier()
        A.reset()
        slot = [A.alloc([4096], BF16) for _ in range(4)]
        b_slot = [Buf("slot%d" % i) for i in range(4)]
        su = [sl.rearrange("p (k e) -> p k e", k=8) for sl in slot]
        sv_ = [sl.rearrange("p (c d) -> p c d", c=4) for sl in slot]
        WT = A.alloc([128, 256], BF16)
        b_WT = Buf("WT")
        skT_sb = A.alloc([16, 128], BF16)
        b_skT = Buf("skT")
        P.dma("pool", lambda e: e.dma_start(out=skT_sb, in_=skT.rearrange("p (a b) -> p a b", a=16)),
              writes=[b_skT], sem=b_skT)
        xn2T = A.alloc([8, 256], BF16)
        b_xn2T = [Buf("xn2T%d" % a) for a in range(2)]
        qT = A.alloc([16, 256], BF16)
        b_qT = Buf("qT")
        h_sb = [A.alloc([D], F32) for a in range(2)]
        b_h = [Buf("h%d" % a) for a in range(2)]
        xt2 = [A.alloc([D], F32)] * 2
        b_xt2 = [Buf("xt2")] * 2
        hb = A.alloc([D], BF16)
        b_hb = Buf("hb")
        catT_t = A.alloc([8, 128], BF16)
        b_catT = Buf("catT")
        NGB = 4
        Gt = [A.alloc([256], BF16) for _ in range(NGB)]
        b_Gt = [Buf("Gt%d" % i) for i in range(NGB)]
        Ht = [A.alloc([256], BF16) for _ in range(NGB)]
        b_Ht = [Buf("Ht%d" % i) for i in range(NGB)]
        pk = [A.alloc([256], BF16), A.alloc([256], BF16), A.alloc([256], F32)]
        iota_b = A.alloc([128], BF16)
        b_pk = Buf("pk")
        tk = [A.alloc([128], F32) for _ in range(3)]
        b_tk = Buf("tk")
        sv = A.alloc([8, 2, 16], F32)
        si = A.alloc([8, 2, 16], U32)
        si_f = A.alloc([8, 2, 16], F32)
        _swp = [A.alloc([256], F32) for _ in range(4)]
        swork = [_swp[i // 2][:, (i % 2) * 128:(i % 2 + 1) * 128] for i in range(8)]
        b_sw = [Buf("sw%d" % i) for i in range(8)]
        cand = [A.alloc([16, 16], F32) for _ in range(4)]
        cwork = [_swp[i].rearrange("p (a b) -> p a b", a=16) for i in range(4)]
        b_cand = [Buf("cand%d" % i) for i in range(4)]
        b_cw = [Buf("cw%d" % i) for i in range(4)]
        b_svc = [[Buf("sv%d_%d" % (h, c)) for c in range(2)] for h in range(8)]
        b_sic = [[Buf("si%d_%d" % (h, c)) for c in range(2)] for h in range(8)]
        b_best = [Buf("best%d" % h) for h in range(8)]
        b_pos = [Buf("pos%d" % h) for h in range(8)]
        best = A.alloc([8, 16], F32)
        pos = A.alloc([8, 16], U32)
        pa_u = A.alloc([8, 16], U32)
        pb_u = A.alloc([8, 16], U32)
        pa_f = A.alloc([8, 16], F32)
        pb_f = A.alloc([8, 16], F32)
        oh = [A.alloc([16, 16], F32) for _ in range(2)]
        b_oh = [Buf("oh%d" % i) for i in range(2)]
        ebest = A.alloc([8, 16], F32)
        esum = A.alloc([16], F32)
        b_tkw = Buf("topk_work")
        NOB = 3
        OIb = [A.alloc([8, 128], BF16) for _ in range(NOB)]
        OJb = [A.alloc([8, 128], BF16) for _ in range(NOB)]
        b_OI = [Buf("OI%d" % i) for i in range(NOB)]
        b_OJ = [Buf("OJ%d" % i) for i in range(NOB)]
        stat2 = A.alloc([2, 4], F32)
        b_stat2 = [Buf("stat2%d" % a) for a in range(2)]
        fstat = A.alloc([2, 4], F32)
        b_fstat = [Buf("fstat%d" % a) for a in range(2)]
        identf = A.alloc([128], F32)
        b_identf = Buf("identf")
        P.op("dve", lambda e: e.tensor_copy(out=identf, in_=ident[:]), reads=[b_const], writes=[b_identf])
        P.op("dve", lambda e: e.tensor_copy(out=iota_b, in_=iota[:]), reads=[b_const], writes=[b_identf])

        for tile in range(n_ptiles):
            T = tp * nsub
            for half in range(2):
                P.dma("sp", lambda e, half=half: e.dma_start(
                    out=su[half], in_=wout_bf[:, half * 512:(half + 1) * 512].rearrange("(k p) c -> p k c", p=128)),
                    reads=[b_woutbf], writes=[b_slot[half]], sem=b_slot[half])
            for a in range(nsub):
                cat_ap, cat_b = cat_src(tile, a)
                x_dr = x_src(tile, a)
                P.dma("sp", lambda e, a=a, x_dr=x_dr: e.dma_start(out=xt2[a][0:tp], in_=x_dr),
                      writes=[b_xt2[a]], sem=b_xt2[a])
                pst = pbank[0][:].bitcast(BF16)
                for k in range(8):
                    P.op("pe", lambda e, k=k, cat_ap=cat_ap: e.transpose(out=pst[:, k * 128:k * 128 + tp],
                                                                        in_=cat_ap[0:tp, k * 128:(k + 1) * 128],
                                                                        identity=ident[0:tp, 0:tp]),
                         reads=[cat_b, b_const], writes=[b_pb[0]])
                P.op("act", lambda e: e.copy(out=catT_t[:, :, 0:tp], in_=pst.rearrange("p (k t) -> p k t", k=8)[:, :, 0:tp]),
                     reads=[b_pb[0]], writes=[b_catT])
                for half in range(2):
                    pa = 2 + half
                    for k in range(8):
                        P.op("pe", lambda e, k=k, half=half, pa=pa: e.matmul(
                            pbank[pa][0:tp, :], lhsT=catT_t[:, k, 0:tp], rhs=su[half][:, k, :], start=(k == 0), stop=(k == 7)),
                            reads=[b_catT, b_slot[half]], writes=[b_pb[pa]])
                    P.op("dve", lambda e, a=a, half=half, pa=pa: e.tensor_tensor(
                        out=h_sb[a][0:tp, half * 512:(half + 1) * 512], in0=pbank[pa][0:tp, :],
                        in1=xt2[a][0:tp, half * 512:(half + 1) * 512], op=ALU.add),
                        reads=[b_pb[pa], b_xt2[a]], writes=[b_h[a]])
                rmsnorm_T(h_sb[a], b_h[a], hb, b_hb, hb, b_hb, stat2[:, a, :], b_stat2[a], 1, g2_sb[:],
                          xn2T[:, :, a * tp:(a + 1) * tp], b_xn2T[a], tp=tp)

            for qq in range(4):
                sl = (2 + qq) % 4
                P.dma("sp", lambda e, qq=qq, sl=sl: e.dma_start(
                    out=su[sl], in_=wpq_bf[:, qq * 512:(qq + 1) * 512].rearrange("(k p) c -> p k c", p=128)),
                    reads=[b_wpqbf], writes=[b_slot[sl]], sem=b_slot[sl])
                for cc in range(4):
                    hc = qq * 4 + cc
                    pa = 2 + hc % 4
                    for k in range(8):
                        P.op("pe", lambda e, k=k, cc=cc, sl=sl, pa=pa: e.matmul(
                            pbank[pa][:, 0:T], lhsT=su[sl][:, k, cc * 128:(cc + 1) * 128], rhs=xn2T[:, k, 0:T],
                            start=(k == 0), stop=(k == 7)),
                            reads=[b_slot[sl]] + b_xn2T, writes=[b_pb[pa]])
                    P.op("act", lambda e, hc=hc, pa=pa: e.copy(out=qT[:, hc, 0:T], in_=pbank[pa][:, 0:T]),
                         reads=[b_pb[pa]], writes=[b_qT])

            for a in range(nsub):
                for h in range(8):
                    pa = 2 + (h // 2) % 4
                    for c in range(2):
                        P.op("pe", lambda e, h=h, c=c, a=a, pa=pa: e.matmul(
                            pbank[pa][0:tp, (h % 2) * 256 + c * 128:(h % 2) * 256 + (c + 1) * 128],
                            lhsT=qT[:, 2 * h + c, a * tp:(a + 1) * tp], rhs=skT_sb[:, 2 * h + c, :],
                            start=True, stop=True), reads=[b_qT, b_skT], writes=[b_pb[pa]])
                W_ = [b_tkw]
                R_ = [b_tkw]
                for hg in range(2):
                    chains = [(h, c) for h in range(4 * hg, 4 * hg + 4) for c in range(2)]

                    def srcof(h, c):
                        return pbank[2 + (h // 2) % 4][0:tp, (h % 2) * 256 + c * 128:(h % 2) * 256 + (c + 1) * 128], b_pb[2 + (h // 2) % 4]

                    for ci, (h, c) in enumerate(chains):
                        src, bsrc = srcof(h, c)
                        P.op("dve", lambda e, h=h, c=c, src=src: e.max(out=sv[0:tp, h, c, 0:8], in_=src), reads=[bsrc], writes=[b_svc[h][c]])
                    for ci, (h, c) in enumerate(chains):
                        src, bsrc = srcof(h, c)
                        P.op("dve", lambda e, h=h, c=c, src=src: e.max_index(out=si[0:tp, h, c, 0:8], in_max=sv[0:tp, h, c, 0:8], in_values=src),
                             reads=[bsrc, b_svc[h][c]], writes=[b_sic[h][c]])
                    for ci, (h, c) in enumerate(chains):
                        src, bsrc = srcof(h, c)
                        P.op("dve", lambda e, h=h, c=c, src=src, ci=ci: e.match_replace(out=swork[ci][0:tp], in_to_replace=sv[0:tp, h, c, 0:8],
                                                                                      in_values=src, imm_value=-1e30),
                             reads=[bsrc, b_svc[h][c]], writes=[b_sw[ci]])
                    for ci, (h, c) in enumerate(chains):
                        P.op("dve", lambda e, h=h, c=c, ci=ci: e.max(out=sv[0:tp, h, c, 8:16], in_=swork[ci][0:tp]), reads=[b_sw[ci]], writes=[b_svc[h][c]])
                    for ci, (h, c) in enumerate(chains):
                        P.op("dve", lambda e, h=h, c=c, ci=ci: e.max_index(out=si[0:tp, h, c, 8:16], in_max=sv[0:tp, h, c, 8:16], in_values=swork[ci][0:tp]),
                             reads=[b_sw[ci], b_svc[h][c]], writes=[b_sic[h][c]])
                    hs = list(range(4 * hg, 4 * hg + 4))
                    for hi, h in enumerate(hs):
                        P.op("dve", lambda e, h=h, hi=hi: e.tensor_tensor(
                            out=cand[hi][0:tp], in0=sv[0:tp, h, 0, :].unsqueeze(2).broadcast_to([tp, 16, 16]),
                            in1=sv[0:tp, h, 1, :].unsqueeze(1).broadcast_to([tp, 16, 16]), op=ALU.add),
                            reads=[b_svc[h][0], b_svc[h][1]], writes=[b_cand[hi]])
                    cf = [cand[hi][0:tp].rearrange("p a b -> p (a b)") for hi in range(4)]
                    cwf = [cwork[hi][0:tp].rearrange("p a b -> p (a b)") for hi in range(4)]
                    for hi, h in enumerate(hs):
                        P.op("dve", lambda e, h=h, hi=hi: e.max(out=best[0:tp, h, 0:8], in_=cf[hi]), reads=[b_cand[hi]], writes=[b_best[h]])
                    for hi, h in enumerate(hs):
                        P.op("dve", lambda e, h=h, hi=hi: e.max_index(out=pos[0:tp, h, 0:8], in_max=best[0:tp, h, 0:8], in_values=cf[hi]),
                             reads=[b_cand[hi], b_best[h]], writes=[b_pos[h]])
                    for hi, h in enumerate(hs):
                        P.op("dve", lambda e, h=h, hi=hi: e.match_replace(out=cwf[hi], in_to_replace=best[0:tp, h, 0:8], in_values=cf[hi], imm_value=-1e30),
                             reads=[b_cand[hi], b_best[h]], writes=[b_cw[hi], b_sw[2 * hi], b_sw[2 * hi + 1]])
                    for hi, h in enumerate(hs):
                        P.op("dve", lambda e, h=h, hi=hi: e.max(out=best[0:tp, h, 8:16], in_=cwf[hi]), reads=[b_cw[hi], b_sw[2 * hi], b_sw[2 * hi + 1]], writes=[b_best[h]])
                    for hi, h in enumerate(hs):
                        P.op("dve", lambda e, h=h, hi=hi: e.max_index(out=pos[0:tp, h, 8:16], in_max=best[0:tp, h, 8:16], in_values=cwf[hi]),
                             reads=[b_cw[hi], b_sw[2 * hi], b_sw[2 * hi + 1], b_best[h]], writes=[b_pos[h]])
                allsv = [b_svc[h][c] for h in range(8) for c in range(2)] + [b_sic[h][c] for h in range(8) for c in range(2)]
                allbp = b_best + b_pos
                P.op("dve", lambda e: e.tensor_copy(out=si_f[0:tp], in_=si[0:tp]), reads=R_ + allsv, writes=W_)
                P.op("dve", lambda e: e.tensor_single_scalar(out=pa_u[0:tp], in_=pos[0:tp], scalar=4, op=ALU.logical_shift_right), reads=R_ + allbp, writes=W_)
                P.op("dve", lambda e: e.tensor_single_scalar(out=pb_u[0:tp], in_=pos[0:tp], scalar=15, op=ALU.bitwise_and), reads=R_ + allbp, writes=W_)
                P.op("dve", lambda e: e.tensor_copy(out=pa_f[0:tp], in_=pa_u[0:tp]), reads=R_, writes=W_)
                P.op("dve", lambda e: e.tensor_copy(out=pb_f[0:tp], in_=pb_u[0:tp]), reads=R_, writes=W_)
                combos = [(which, h) for which in range(2) for h in range(8)]
                for g0 in range(0, 16, 2):
                    grp = combos[g0:g0 + 2]
                    for oi, (which, h) in enumerate(grp):
                        pf = pa_f if which == 0 else pb_f
                        P.op("dve", lambda e, h=h, pf=pf, oi=oi: e.tensor_tensor(
                            out=oh[oi][0:tp], in0=iota[0:tp, 0:16].unsqueeze(1).broadcast_to([tp, 16, 16]),
                            in1=pf[0:tp, h, :].unsqueeze(2).broadcast_to([tp, 16, 16]), op=ALU.is_equal),
                            reads=[b_tkw, b_const], writes=[b_oh[oi]])
                    for oi, (which, h) in enumerate(grp):
                        P.op("dve", lambda e, h=h, which=which, oi=oi: e.tensor_tensor(
                            out=oh[oi][0:tp], in0=oh[oi][0:tp],
                            in1=si_f[0:tp, h, which, :].unsqueeze(1).broadcast_to([tp, 16, 16]), op=ALU.mult),
                            reads=[b_tkw, b_oh[oi]], writes=[b_oh[oi]])
                    for oi, (which, h) in enumerate(grp):
                        P.op("dve", lambda e, which=which, h=h, oi=oi: e.tensor_reduce(
                            out=tk[which][0:tp, h * 16:(h + 1) * 16], in_=oh[oi][0:tp], axis=AX.X, op=ALU.add),
                            reads=[b_oh[oi]], writes=[b_tk])
                P.op("dve", lambda e: e.tensor_tensor(out=ebest[0:tp], in0=best[0:tp], in1=best[0:tp, :, 0:1].broadcast_to([tp, 8, 16]),
                                                      op=ALU.subtract), reads=R_ + allbp, writes=W_)
                P.op("act", lambda e: e.activation(out=ebest[0:tp], in_=ebest[0:tp], func=AF.Exp), reads=R_, writes=W_)
                P.op("dve", lambda e: e.tensor_reduce(out=esum[0:tp, 0:8], in_=ebest[0:tp], axis=AX.X, op=ALU.add), reads=R_, writes=W_)
                P.op("dve", lambda e: e.reciprocal(out=esum[0:tp, 8:16], in_=esum[0:tp, 0:8]), reads=R_, writes=W_)
                P.op("dve", lambda e: e.tensor_tensor(out=tk[2][0:tp].rearrange("p (h k) -> p h k", h=8), in0=ebest[0:tp],
                                                      in1=esum[0:tp, 8:16].unsqueeze(2).broadcast_to([tp, 8, 16]), op=ALU.mult),
                     reads=R_, writes=[b_tk])
                pt = pbank[1][:]
                for w3 in range(3):
                    P.op("pe", lambda e, w3=w3: e.transpose(out=pt[:, w3 * 128:w3 * 128 + tp], in_=tk[w3][0:tp], identity=identf[0:tp, 0:tp]),
                         reads=[b_tk, b_identf], writes=[b_pb[1]])
                for w3 in range(3):
                    P.op("act", lambda e, w3=w3, a=a: e.copy(out=pk[w3][:, a * tp:(a + 1) * tp], in_=pt[:, w3 * 128:w3 * 128 + tp]),
                         reads=[b_pb[1]], writes=[b_pk])

            TB = 8
            for t0 in range(0, T, TB):
                ob = (t0 // TB) % NOB
                P.op("dve", lambda e, t0=t0, ob=ob: e.tensor_tensor(
                    out=OIb[ob], in0=iota_b.unsqueeze(1).broadcast_to([128, TB, 128]),
                    in1=pk[0][:, t0:t0 + TB].unsqueeze(2).broadcast_to([128, TB, 128]), op=ALU.is_equal),
                    reads=[b_pk, b_identf], writes=[b_OI[ob]])
                P.op("dve", lambda e, t0=t0, ob=ob: e.tensor_tensor(
                    out=OJb[ob], in0=iota_b.unsqueeze(1).broadcast_to([128, TB, 128]),
                    in1=pk[1][:, t0:t0 + TB].unsqueeze(2).broadcast_to([128, TB, 128]), op=ALU.is_equal),
                    reads=[b_pk, b_identf], writes=[b_OJ[ob]])
                P.op("pool", lambda e, t0=t0, ob=ob: e.tensor_tensor(
                    out=OJb[ob], in0=OJb[ob],
                    in1=pk[2][:, t0:t0 + TB].unsqueeze(2).broadcast_to([128, TB, 128]), op=ALU.mult),
                    reads=[b_pk, b_OJ[ob]], writes=[b_OJ[ob]])
                for tt in range(TB):
                    t = t0 + tt
                    pw = 6 + (t // 4) % 2
                    P.op("pe", lambda e, t=t, tt=tt, ob=ob, pw=pw: e.matmul(pbank[pw][:, (t % 4) * 128:(t % 4 + 1) * 128],
                                                                           lhsT=OJb[ob][:, tt, :], rhs=OIb[ob][:, tt, :],
                                                                           start=True, stop=True),
                         reads=[b_OI[ob], b_OJ[ob]], writes=[b_pb[pw]])
                    if t % 4 == 3:
                        t4 = t - 3
                        P.op("act", lambda e, t4=t4, pw=pw: e.copy(
                            out=WT[:, :, t4:t4 + 4], in_=pbank[pw][:].rearrange("p (t i) -> p i t", t=4)),
                            reads=[b_pb[pw]], writes=[b_WT])

            items = [(g, c) for g in range(NG) for c in range(4)]
            LA = 3

            def pfront(n, g, c):
                i = g * 4 + c
                us, vs = g % 2, 2 + g % 2
                if c == 0:
                    P.dma("sp", lambda e: e.dma_start(out=slot[us], in_=uT_bf[g]),
                          reads=[b_uTbf], writes=[b_slot[us]], sem=b_slot[us])
                    P.dma("sp", lambda e: e.dma_start(out=slot[vs], in_=pv_bf[g]),
                          reads=[b_pvbf], writes=[b_slot[vs]], sem=b_slot[vs])
                pa = 6 + n % 2
                gb = n % NGB
                for k in range(8):
                    P.op("pe", lambda e, k=k: e.matmul(pbank[pa][:, 0:T], lhsT=su[us][:, k, c * 128:(c + 1) * 128], rhs=xn2T[:, k, 0:T],
                                                       start=(k == 0), stop=(k == 7)),
                         reads=[b_slot[us]] + b_xn2T, writes=[b_pb[pa]])
                P.op("act", lambda e: e.activation(out=Gt[gb][:, 0:T], in_=pbank[pa][:, 0:T], func=GELU), reads=[b_pb[pa]], writes=[b_Gt[gb]])
                P.op("dve", lambda e: e.tensor_tensor(out=Ht[gb][:, 0:T], in0=Gt[gb][:, 0:T], in1=WT[:, i, 0:T], op=ALU.mult),
                     reads=[b_Gt[gb], b_WT], writes=[b_Ht[gb]])

            def pback(n, g, c):
                i = g * 4 + c
                vs = 2 + g % 2
                gb = n % NGB
                for a in range(nsub):
                    for half in range(2):
                        pa = 2 + a * 2 + half
                        P.op("pe", lambda e, a=a, half=half, pa=pa: e.matmul(
                            pbank[pa][0:tp, :], lhsT=Ht[gb][:, a * tp:(a + 1) * tp], rhs=sv_[vs][:, c, half * 512:(half + 1) * 512],
                            start=(i == 0), stop=(i == NI - 1)),
                            reads=[b_Ht[gb], b_slot[vs]], writes=[b_pb[pa]])

            for n in range(len(items) + LA):
                if n < len(items):
                    pfront(n, *items[n])
                if n >= LA:
                    pback(n - LA, *items[n - LA])

            for a in range(nsub):
                y_dr = y_dst(tile, a)
                for half in range(2):
                    pa = 2 + a * 2 + half
                    P.op("dve", lambda e, a=a, half=half, pa=pa: e.tensor_tensor(
                        out=h_sb[a][0:tp, half * 512:(half + 1) * 512], in0=pbank[pa][0:tp, :],
                        in1=h_sb[a][0:tp, half * 512:(half + 1) * 512], op=ALU.add),
                        reads=[b_pb[pa], b_h[a]], writes=[b_h[a]])
                fs = fstat[0:tp, a, :]
                P.op("act", lambda e, a=a, fs=fs: e.activation(out=hb[0:tp], in_=h_sb[a][0:tp], func=AF.Square, accum_out=fs[:, 0:1]),
                     reads=[b_h[a]], writes=[b_hb, b_fstat[a]])
                P.op("dve", lambda e, fs=fs: e.tensor_scalar(out=fs[:, 1:2], in0=fs[:, 0:1], scalar1=1.0 / D, scalar2=1e-6,
                                                            op0=ALU.mult, op1=ALU.add), reads=[b_fstat[a]], writes=[b_fstat[a]])
                P.op("act", lambda e, fs=fs: e.activation(out=fs[:, 2:3], in_=fs[:, 1:2], func=AF.Sqrt),
                     reads=[b_fstat[a]], writes=[b_fstat[a]])
                P.op("dve", lambda e, fs=fs: e.reciprocal(out=fs[:, 3:4], in_=fs[:, 2:3]), reads=[b_fstat[a]], writes=[b_fstat[a]])
                P.op("dve", lambda e, a=a, fs=fs: e.scalar_tensor_tensor(out=h_sb[a][0:tp], in0=h_sb[a][0:tp], scalar=fs[:, 3:4], in1=gf_bc[0:tp, :],
                                                                       op0=ALU.mult, op1=ALU.mult),
                     reads=[b_h[a], b_fstat[a], b_const], writes=[b_h[a]])
                P.dma("sp", lambda e, a=a, y_dr=y_dr: e.dma_start(out=y_dr, in_=h_sb[a][0:tp]),
                      reads=[b_h[a]], sem=b_h[a])

    for s in range(NSEQ):
        prompt_mixer(s)
        if do_peer:
            def rows(tile, a, s=s):
                r0 = s * SEQ + (tile * 2 + a) * 128
                return slice(r0, r0 + 128)
            peer_phase(128, 2, SEQ // 256,
                       lambda tile, a: (catTok[:, tile * 2 + a, :], b_cat[tile * 2 + a]),
                       lambda tile, a, rows=rows: xp[rows(tile, a), :],
                       lambda tile, a, rows=rows: yp[rows(tile, a), :])
    if do_sample:
        sample_mixer()
        if do_peer:
            peer_phase(NS, 1, 1, lambda tile, a: (cat_s[:], b_cats), lambda tile, a: xs, lambda tile, a: ys)

    P.barrier()
    P.emit(nc, es)
    es.close()
    return nc


_NC_CACHE = {}


def kernel(x_prompt, x_sample, cache_k_win, cache_v_win, state_ret, norm1_g, w_in, ret_gn_g, w_out,
           norm2_g, w_pq, peer_sub_keys, peer_u, peer_v, norm_f_g):
    f32 = np.float32
    if "nc" not in _NC_CACHE:
        _NC_CACHE["nc"] = build_program()
    nc = _NC_CACHE["nc"]
    x_prompt = np.asarray(x_prompt, f32)
    x_sample = np.asarray(x_sample, f32)
    ck = np.asarray(cache_k_win, f32)[0].reshape(128, 2048, 512)
    cv = np.asarray(cache_v_win, f32)[0].reshape(128, 2048, 512)
    stt = np.asarray(state_ret, f32)[0].reshape(128 * 4, 128, 128)
    shared = {
        "g1": np.ascontiguousarray(np.asarray(norm1_g, f32)[0].reshape(8, 128).T),
        "g2": np.ascontiguousarray(np.asarray(norm2_g, f32)[0].reshape(8, 128).T),
        "gf": np.asarray(norm_f_g, f32).reshape(1, D),
        "ggn": np.asarray(ret_gn_g, f32).reshape(1, 512),
        "w_in": np.asarray(w_in, f32)[0],
        "w_out": np.asarray(w_out, f32)[0],
        "w_pq": np.asarray(w_pq, f32)[0],
        "skT": np.ascontiguousarray(np.asarray(peer_sub_keys, f32)[0].reshape(16, 128, 128).transpose(2, 0, 1)).reshape(128, 2048),
        "uT": np.ascontiguousarray(np.asarray(peer_u, f32)[0].T),
        "pv": np.asarray(peer_v, f32)[0],
    }
    shared.update(make_tables(SEQ))
    in_maps = []
    for c in range(NCORES):
        m = dict(shared)
        m["xp"] = x_prompt[2 * c:2 * c + 2].reshape(NSEQ * SEQ, D)
        m["xs"] = x_sample[16 * c:16 * c + 16, 0, :]
        m["ck"] = ck[16 * c:16 * c + 16]
        m["cv"] = cv[16 * c:16 * c + 16]
        m["st"] = stt[64 * c:64 * c + 64]
        in_maps.append(m)
    res = run_bass_kernel_spmd(nc, in_maps, core_ids=list(range(NCORES)))
    R = res.results
    cat = lambda k: np.concatenate([np.asarray(R[c][k], f32) for c in range(NCORES)], axis=0)
    y_prompt = cat("yp").reshape(16, SEQ, D)
    y_sample = cat("ys").reshape(128, 1, D)
    k_win = cat("kwp").reshape(1, 16, SEQ, 8, 64)
    v_win = cat("vwp").reshape(1, 16, SEQ, 8, 64)
    ret_p = cat("rp").reshape(1, 16, 4, 128, 128)
    k_new = cat("kns").reshape(1, 128, 1, 8, 64)
    v_new = cat("vns").reshape(1, 128, 1, 8, 64)
    ret_s = cat("rs").reshape(1, 128, 4, 128, 128)
    return (y_prompt, y_sample, k_win, v_win, ret_p, k_new, v_new, ret_s)
```

```python
import numpy as np
import ml_dtypes
from contextlib import ExitStack
import concourse.bass as bass
import concourse.mybir as mybir
from concourse.bass_utils import run_bass_kernel_spmd

F32 = mybir.dt.float32
BF16 = mybir.dt.bfloat16
I32 = mybir.dt.int32
U32 = mybir.dt.uint32
AF = mybir.ActivationFunctionType
ALU = mybir.AluOpType
AX = mybir.AxisListType

NCORES = 8
D = 1024
SEQ = 2048
NSEQ = 2
NS = 16
INC = 3584
NT = SEQ // 128

ENGS = ("pe", "act", "dve", "pool", "sp")
SAME_ENGINE_SYNC = True


class Buf:
    __slots__ = ("name", "w", "r", "dsem", "dcnt", "excl")

    def __init__(self, name, excl=False):
        self.name = name
        self.excl = excl
        self.w = None
        self.r = {}
        self.dsem = None
        self.dcnt = 0


class Prog:
    def __init__(self):
        self.q = {e: [] for e in ENGS}
        self.cnt = {e: 0 for e in ENGS}
        self.waited = {e: {} for e in ENGS}
        self.ndsem = 0
        self.dtotal = {}
        self.needed = {e: set() for e in ENGS}

    def _deps(self, eng, reads, writes, is_dma_sem=None):
        evs = []
        for b in reads:
            if b.w is not None:
                evs.append(b.w)
            if b.excl:
                evs.extend((k, v) for (k, v) in b.r.items() if k != eng)
        for b in writes:
            if b.w is not None:
                if not (is_dma_sem is not None and b.w[0] == is_dma_sem):
                    evs.append(b.w)
            evs.extend(b.r.items())
        waits = {}
        for (s, v) in evs:
            if s == eng and (eng == "pe" or not SAME_ENGINE_SYNC):
                continue
            if s == is_dma_sem and False:
                continue
            if self.waited[eng].get(s, 0) >= v:
                continue
            if waits.get(s, 0) < v:
                waits[s] = v
        for s, v in waits.items():
            self.waited[eng][s] = v
            if s in ENGS:
                self.needed[s].add(v)
        return list(waits.items())

    def op(self, eng, fn, reads=(), writes=()):
        waits = self._deps(eng, reads, writes)
        self.cnt[eng] += 1
        ev = (eng, self.cnt[eng])
        self.q[eng].append((waits, fn, None, self.cnt[eng]))
        for b in reads:
            b.r[ev[0]] = ev[1]
        for b in writes:
            b.w = ev
            b.r = {}

    def dma(self, eng, fn, reads=(), writes=(), sem=None):
        if sem.dsem is None:
            sem.dsem = ("d", self.ndsem)
            self.ndsem += 1
        key = sem.dsem
        waits = self._deps(eng, reads, writes, is_dma_sem=key)
        sem.dcnt += 16
        self.dtotal[key] = sem.dcnt
        ev = (key, sem.dcnt)
        self.q[eng].append((waits, fn, key, None))
        for b in reads:
            b.r[ev[0]] = ev[1]
        for b in writes:
            b.w = ev
            b.r = {}

    def barrier(self):
        for e in ENGS:
            waits = {}
            for o in ENGS:
                if o != e and self.cnt[o] > 0 and self.waited[e].get(o, 0) < self.cnt[o]:
                    waits[o] = self.cnt[o]
            for k, v in self.dtotal.items():
                if self.waited[e].get(k, 0) < v:
                    waits[k] = v
            for s, v in waits.items():
                self.waited[e][s] = v
                if s in ENGS:
                    self.needed[s].add(v)
            if waits:
                self.q[e].append((list(waits.items()), None, None, None))

    def emit(self, nc, es):
        esem = {e: es.enter_context(nc.semaphore("prog_" + e)) for e in ENGS}
        dsem = {("d", i): es.enter_context(nc.semaphore("dma%d" % i)) for i in range(self.ndsem)}
        rank = {}
        for e in ENGS:
            srt = sorted(self.needed[e])
            rank[e] = {v: i + 1 for i, v in enumerate(srt)}
        block = es.enter_context(nc.Block())

        def run(e, engobj):
            for (waits, fn, dkey, seq) in self.q[e]:
                for (s, v) in waits:
                    if s in ENGS:
                        engobj.wait_ge(esem[s], rank[s][v])
                    else:
                        engobj.wait_ge(dsem[s], v)
                if fn is None:
                    continue
                ins = fn(engobj)
                if dkey is not None:
                    ins.then_inc(dsem[dkey], 16)
                elif seq in rank[e]:
                    ins.then_inc(esem[e], 1)

        @block.tensor
        def _(eng):
            run("pe", eng)

        @block.scalar
        def _(eng):
            run("act", eng)

        @block.vector
        def _(eng):
            run("dve", eng)

        @block.gpsimd
        def _(eng):
            run("pool", eng)

        @block.sync
        def _(eng):
            run("sp", eng)


def _prod(xs):
    n = 1
    for x in xs:
        n *= int(x)
    return n


_DTSIZE = {F32: 4, BF16: 2, U32: 4, I32: 4}


class Arena:
    def __init__(self, t, n16):
        self.t = t
        self.n = n16
        self.off = 0

    def reset(self, off=0):
        self.off = off

    def alloc(self, shape, dt):
        n = _prod(shape)
        e16 = (n * _DTSIZE[dt] + 1) // 2
        a = self.off
        self.off += (e16 + 15) // 16 * 16
        assert self.off <= self.n, ("arena overflow", self.off, self.n)
        v = self.t[:, a:a + e16]
        if dt != BF16:
            v = v.bitcast(dt)
        if len(shape) == 2:
            v = v.rearrange("p (a b) -> p a b", a=shape[0])
        elif len(shape) == 3:
            v = v.rearrange("p (a b c) -> p a b c", a=shape[0], b=shape[1])
        elif len(shape) == 4:
            v = v.rearrange("p (a b c d) -> p a b c d", a=shape[0], b=shape[1], c=shape[2])
        return v


SLOPES = [float(2.0 ** (-8.0 * (h + 1) / 8)) for h in range(8)]
GELU = AF.Gelu_apprx_tanh
RSTOP = 99
LG = [float(np.log(1.0 - 2.0 ** (-5.0 - h)).astype(np.float32)) for h in range(4)]
GAM = [float(np.exp(np.float32(x))) for x in LG]


def make_tables(SEQ):
    f32 = np.float32
    TW = SEQ + 384
    NT = SEQ // 128
    r = np.arange(128)[:, None]
    x = np.arange(TW)[None, :]
    d = x - r - 384
    dpos = np.maximum(d, 0).astype(f32)
    mult = (((d >= 0) & (d <= 128)).astype(f32) + ((d >= 0) & (d % 4 == 0) & (d <= 512)).astype(f32)
            + ((d >= 0) & (d % 16 == 0) & (d <= 2048)).astype(f32))
    caus = (d >= 0).astype(f32)
    pos = np.arange(NT)[None, :] * 128 + np.arange(128)[:, None]
    kdec = np.stack([np.exp((SEQ - 1 - pos) * LG[h]) * (128.0 ** -0.5) for h in range(4)], 1)
    rr = np.arange(128)
    sbias = np.zeros((128, 3, 8), f32)
    for g, dil in enumerate((1, 4, 16)):
        for h in range(8):
            sbias[:, g, h] = -SLOPES[h] * dil * (128 - rr)
    return {
        "dtab": dpos.astype(f32),
        "multt": mult.astype(ml_dtypes.bfloat16),
        "caust": caus.astype(ml_dtypes.bfloat16),
        "kdec": kdec.reshape(128, 4 * NT).astype(f32),
        "sbias": sbias.reshape(128, 24),
        "ident": np.eye(128, dtype=f32).astype(ml_dtypes.bfloat16),
        "iota": np.tile(np.arange(128, dtype=f32)[None, :], (128, 1)),
        "selrow": np.repeat(np.eye(16, dtype=f32), 128, axis=1).reshape(16, 2048).copy(),
        "sel16": np.repeat(np.eye(16, dtype=f32), 128, axis=0).reshape(16, 128, 16).transpose(1, 0, 2).reshape(128, 256).copy(),
    }


def build_program(SEQ=2048, NSEQ=2, NS=16, do_sample=True, do_peer=True, NE=16384, stop=99):
    nc = bass.Bass("TRN2", target_bir_lowering=False)
    es = ExitStack()
    P = Prog()
    NT = SEQ // 128
    NQG = SEQ // 512
    TW = SEQ + 384
    NTOK = NSEQ * SEQ
    NI = NE // 128
    NG = NI // 4

    def din(name, shape, dt=F32):
        return nc.dram_tensor(name, list(shape), dt, kind="ExternalInput").ap()

    def dout(name, shape, dt=F32):
        return nc.dram_tensor(name, list(shape), dt, kind="ExternalOutput").ap()

    def dscr(name, shape, dt):
        return nc.dram_tensor(name, list(shape), dt).ap()

    xp = din("xp", [NTOK, D])
    xs = din("xs", [NS, D])
    ck = din("ck", [NS, 2048, 512])
    cv = din("cv", [NS, 2048, 512])
    st = din("st", [NS * 4, 128, 128])
    g1 = din("g1", [128, 8])
    g2 = din("g2", [128, 8])
    gf = din("gf", [1, D])
    ggn = din("ggn", [1, 512])
    w_in = din("w_in", [D, INC])
    w_out = din("w_out", [D, D])
    w_pq = din("w_pq", [D, 2048])
    skT = din("skT", [128, 16 * 128])
    uT = din("uT", [D, NE])
    pv = din("pv", [NE, D])
    ident_d = din("ident", [128, 128], BF16)
    iota_d = din("iota", [128, 128])
    dtab_d = din("dtab", [128, TW])
    multt_d = din("multt", [128, TW], BF16)
    caust_d = din("caust", [128, TW], BF16)
    kdec_d = din("kdec", [128, 4 * NT])
    sbias_d = din("sbias", [128, 24])
    sel16_d = din("sel16", [128, 256])
    selrow_d = din("selrow", [16, 16 * 128])

    yp = dout("yp", [NTOK, D])
    ys = dout("ys", [NS, D])
    kwp = dout("kwp", [NTOK, 512])
    vwp = dout("vwp", [NTOK, 512])
    rp = dout("rp", [NSEQ * 4, 128, 128])
    kns = dout("kns", [NS, 512])
    vns = dout("vns", [NS, 512])
    rs = dout("rs", [NS * 4, 128, 128])

    uT_bf = dscr("uT_bf", [NE // 512, 128, 8 * 512], BF16)
    pv_bf = dscr("pv_bf", [NE // 512, 128, 4 * 1024], BF16)
    wout_bf = dscr("wout_bf", [D, D], BF16)
    wpq_bf = dscr("wpq_bf", [D, 2048], BF16)
    win_bf = dscr("win_bf", [D, INC], BF16)
    b_winbf = Buf("win_bf")
    b_uTbf = Buf("uT_bf")
    b_pvbf = Buf("pv_bf")
    b_woutbf = Buf("wout_bf")
    b_wpqbf = Buf("wpq_bf")

    def sb(name, shape, dt):
        return es.enter_context(nc.sbuf_tensor(name, list(shape), dt))

    def ps(name, shape, dt):
        return es.enter_context(nc.psum_tensor(name, list(shape), dt))

    ident = sb("ident_sb", [128, 128], BF16)
    iota = sb("iota_sb", [128, 128], F32)
    g1_sb = sb("g1_sb", [128, 8], F32)
    g2_sb = sb("g2_sb", [128, 8], F32)
    gf_bc = sb("gf_bc", [128, D], F32)
    ggn_bc = sb("ggn_bc", [128, 512], F32)
    catTok = sb("catTok", [128, NT, D], BF16)
    b_const = Buf("const")
    b_cat = [Buf("cat%d" % i) for i in range(NT)]
    for (dst, src) in ((ident[:], ident_d), (iota[:], iota_d), (g1_sb[:], g1), (g2_sb[:], g2),
                       (gf_bc[:], gf.broadcast_to([128, D])), (ggn_bc[:], ggn.broadcast_to([128, 512]))):
        P.dma("sp", lambda e, dst=dst, src=src: e.dma_start(out=dst, in_=src), writes=[b_const], sem=b_const)

    for r0 in range(0, D, 128):
        P.dma("pool", lambda e, r0=r0: e.dma_start(out=win_bf[r0:r0 + 128, :], in_=w_in[r0:r0 + 128, :]),
              writes=[b_winbf], sem=b_winbf)
    if do_peer:
        RC = 128
        for r0 in range(0, D, RC):
            P.dma("pool", lambda e, r0=r0: e.dma_start(out=wout_bf[r0:r0 + RC, :], in_=w_out[r0:r0 + RC, :]),
                  writes=[b_woutbf], sem=b_woutbf)
            P.dma("pool", lambda e, r0=r0: e.dma_start(out=wpq_bf[r0:r0 + RC, :], in_=w_pq[r0:r0 + RC, :]),
                  writes=[b_wpqbf], sem=b_wpqbf)
        for g in range(NE // 512):
            for k in range(8):
                P.dma("pool", lambda e, g=g, k=k: e.dma_start(out=uT_bf[g, :, k * 512:(k + 1) * 512],
                                                             in_=uT[k * 128:(k + 1) * 128, g * 512:(g + 1) * 512]),
                      writes=[b_uTbf], sem=b_uTbf)
            for c in range(4):
                P.dma("pool", lambda e, g=g, c=c: e.dma_start(out=pv_bf[g, :, c * 1024:(c + 1) * 1024],
                                                             in_=pv[(g * 4 + c) * 128:(g * 4 + c + 1) * 128, :]),
                      writes=[b_pvbf], sem=b_pvbf)

    pbank = [ps("pb%d" % i, [128, 512], F32) for i in range(8)]
    b_pb = [Buf("pb%d" % i, excl=True) for i in range(8)]

    AR16 = (nc.sbuf_bytes_remaining - 2048) // 2 // 16 * 16
    arena_t = sb("arena", [128, AR16], BF16)
    A = Arena(arena_t, AR16)

    def rmsnorm_T(src_ap, b_src, xb_ap, b_xb, junk_ap, b_junk, stat_ap, b_stat, pst_i, gain_ap, dst_ap, b_dst, tp=128):
        src_ap = src_ap[0:tp]
        xb_ap = xb_ap[0:tp]
        junk_ap = junk_ap[0:tp]
        stat_ap = stat_ap[0:tp]
        P.op("act", lambda e: e.activation(out=junk_ap, in_=src_ap, func=AF.Square, accum_out=stat_ap[:, 0:1]),
             reads=[b_src], writes=[b_junk, b_stat])
        P.op("dve", lambda e: e.tensor_scalar(out=stat_ap[:, 1:2], in0=stat_ap[:, 0:1], scalar1=1.0 / D, scalar2=1e-6,
                                              op0=ALU.mult, op1=ALU.add), reads=[b_stat], writes=[b_stat])
        P.op("act", lambda e: e.activation(out=stat_ap[:, 2:3], in_=stat_ap[:, 1:2], func=AF.Sqrt),
             reads=[b_stat], writes=[b_stat])
        P.op("dve", lambda e: e.reciprocal(out=stat_ap[:, 3:4], in_=stat_ap[:, 2:3]), reads=[b_stat], writes=[b_stat])
        P.op("dve", lambda e: e.tensor_scalar(out=xb_ap, in0=src_ap, scalar1=stat_ap[:, 3:4], scalar2=None, op0=ALU.mult),
             reads=[b_src, b_stat], writes=[b_xb])
        pst = pbank[pst_i][:].bitcast(BF16)
        for k in range(8):
            P.op("pe", lambda e, k=k: e.transpose(out=pst[:, k * 128:k * 128 + tp], in_=xb_ap[:, k * 128:(k + 1) * 128],
                                                  identity=ident[0:tp, 0:tp]),
                 reads=[b_xb, b_const], writes=[b_pb[pst_i]])
        P.op("dve", lambda e: e.tensor_tensor(out=dst_ap, in0=pst.rearrange("p (k t) -> p k t", k=8)[:, :, 0:tp],
                                              in1=gain_ap.unsqueeze(2).broadcast_to([128, 8, tp]), op=ALU.mult),
             reads=[b_pb[pst_i], b_const], writes=[b_dst])

    def groupnorm(tp, src, b_src, gain_ap, gate_ap, b_gate, out_ap, b_outs, gn_a, gn_b, gn_s, b_gn):
        ga, gb_, gs = gn_a[0:tp], gn_b[0:tp], gn_s[0:tp]
        P.op("dve", lambda e: e.tensor_reduce(out=gs[:, 0:4], in_=src, axis=AX.X, op=ALU.add), reads=[b_src], writes=[b_gn])
        P.op("dve", lambda e: e.tensor_scalar(out=gs[:, 0:4], in0=gs[:, 0:4], scalar1=1.0 / 128, scalar2=None, op0=ALU.mult),
             reads=[b_gn], writes=[b_gn])
        P.op("dve", lambda e: e.tensor_tensor(out=ga, in0=src, in1=gs[:, 0:4].unsqueeze(2).broadcast_to([tp, 4, 128]),
                                              op=ALU.subtract), reads=[b_src, b_gn], writes=[b_gn])
        P.op("dve", lambda e: e.tensor_tensor(out=gb_, in0=ga, in1=ga, op=ALU.mult), reads=[b_gn], writes=[b_gn])
        P.op("dve", lambda e: e.tensor_reduce(out=gs[:, 4:8], in_=gb_, axis=AX.X, op=ALU.add), reads=[b_gn], writes=[b_gn])
        P.op("dve", lambda e: e.tensor_scalar(out=gs[:, 4:8], in0=gs[:, 4:8], scalar1=1.0 / 128, scalar2=1e-5,
                                              op0=ALU.mult, op1=ALU.add), reads=[b_gn], writes=[b_gn])
        P.op("act", lambda e: e.activation(out=gs[:, 8:12], in_=gs[:, 4:8], func=AF.Sqrt), reads=[b_gn], writes=[b_gn])
        P.op("dve", lambda e: e.reciprocal(out=gs[:, 12:16], in_=gs[:, 8:12]), reads=[b_gn], writes=[b_gn])
        P.op("dve", lambda e: e.tensor_tensor(out=ga, in0=ga, in1=gs[:, 12:16].unsqueeze(2).broadcast_to([tp, 4, 128]),
                                              op=ALU.mult), reads=[b_gn], writes=[b_gn])
        P.op("dve", lambda e: e.tensor_tensor(out=ga, in0=ga, in1=gain_ap, op=ALU.mult), reads=[b_gn, b_const], writes=[b_gn])
        P.op("dve", lambda e: e.tensor_tensor(out=out_ap, in0=ga, in1=gate_ap, op=ALU.mult), reads=[b_gn, b_gate], writes=b_outs)

    cat_s = catTok[:, 0, :]
    b_cats = Buf("cat_s")

    def sample_mixer():
        tp = NS
        P.barrier()
        A.reset()
        xs_sb = A.alloc([D], F32)
        b_xs = Buf("xs")
        xsb = A.alloc([D], BF16)
        b_xsb = Buf("xsb")
        junk = A.alloc([D], BF16)
        b_junk = Buf("junk_s")
        stat_s = A.alloc([4], F32)
        b_stat_s = Buf("stat_s")
        xsT = A.alloc([8, 16], BF16)
        b_xsT = Buf("xsT")
        wblk = [A.alloc([8, 512], BF16) for _ in range(2)]
        b_wblk = [Buf("wblk%d" % i) for i in range(2)]
        z = A.alloc([INC], F32)
        b_z = Buf("z")
        selcol = A.alloc([16, 16], F32)
        selrow = A.alloc([16 * 128], F32)
        sbias_sb = A.alloc([3, 8], F32)
        identf = A.alloc([128], F32)
        b_sc = Buf("sconst")
        kg = [A.alloc([512], F32) for _ in range(2)]
        b_kg = [Buf("kg%d" % i) for i in range(2)]
        vg_ = [A.alloc([512], F32) for _ in range(2)]
        b_vg = [Buf("vg%d" % i) for i in range(2)]
        prod = A.alloc([512], F32)
        b_prod = Buf("prod")
        sc = A.alloc([8], F32)
        b_scr = Buf("sc")
        p_ = [A.alloc([8], F32) for _ in range(2)]
        b_p = [Buf("p%d" % i) for i in range(2)]
        pvt = [A.alloc([512], F32) for _ in range(2)]
        b_pvt = [Buf("pvt%d" % i) for i in range(2)]
        tmp16 = A.alloc([512], F32)
        sself = A.alloc([8], F32)
        pself = A.alloc([8], F32)
        den = A.alloc([8], F32)
        rden = A.alloc([8], F32)
        b_tm = Buf("tok_misc")
        catf = A.alloc([D], F32)
        b_catf = Buf("catf")
        rqT = A.alloc([4, 16], F32)
        b_rqT = Buf("rqT")
        qm = A.alloc([4, 16, 16], F32)
        b_qm = Buf("qm")
        S_sb = [A.alloc([128], F32) for _ in range(2)]
        b_S = [Buf("S%d" % i) for i in range(2)]
        sout = [A.alloc([128], F32) for _ in range(2)]
        b_sout = [Buf("sout%d" % i) for i in range(2)]
        kmask = [A.alloc([128], F32) for _ in range(2)]
        b_kmask = [Buf("kmask%d" % i) for i in range(2)]
        ro = A.alloc([4, 128], F32)
        b_ro = Buf("ro")
        qk = A.alloc([4], F32)
        sgs = A.alloc([4, 128], F32)
        b_sgs = Buf("sgs")
        gn_a = A.alloc([4, 128], F32)
        gn_b = A.alloc([4, 128], F32)
        gn_s = A.alloc([16], F32)
        b_gn = Buf("gn_s")

        P.dma("sp", lambda e: e.dma_start(out=selcol, in_=sel16_d.rearrange("p (a b) -> p a b", a=16)), writes=[b_sc], sem=b_sc)
        P.dma("sp", lambda e: e.dma_start(out=selrow[0:16], in_=selrow_d), writes=[b_sc], sem=b_sc)
        P.dma("sp", lambda e: e.dma_start(out=sbias_sb, in_=sbias_d.rearrange("p (a b) -> p a b", a=3)), writes=[b_sc], sem=b_sc)
        P.op("dve", lambda e: e.tensor_copy(out=identf, in_=ident[:]), reads=[b_const], writes=[b_sc])
        P.dma("sp", lambda e: e.dma_start(out=xs_sb[0:tp], in_=xs), writes=[b_xs], sem=b_xs)
        rmsnorm_T(xs_sb, b_xs, xsb, b_xsb, junk, b_junk, stat_s, b_stat_s, 0, g1_sb[:], xsT[:, :, 0:tp], b_xsT, tp=tp)
        for blk in range(7):
            jb = blk % 2
            P.dma("sp", lambda e, blk=blk, jb=jb: e.dma_start(
                out=wblk[jb], in_=win_bf[:, blk * 512:(blk + 1) * 512].rearrange("(k p) c -> p k c", p=128)),
                reads=[b_winbf], writes=[b_wblk[jb]], sem=b_wblk[jb])
            pa = 2 + jb
            for k in range(8):
                P.op("pe", lambda e, k=k, jb=jb, pa=pa: e.matmul(pbank[pa][0:tp, :], lhsT=xsT[:, k, 0:tp], rhs=wblk[jb][:, k, :],
                                                                start=(k == 0), stop=(k == 7)),
                     reads=[b_xsT, b_wblk[jb]], writes=[b_pb[pa]])
            P.op("act", lambda e, blk=blk, pa=pa: e.copy(out=z[0:tp, blk * 512:(blk + 1) * 512], in_=pbank[pa][0:tp, :]),
                 reads=[b_pb[pa]], writes=[b_z])
        P.dma("sp", lambda e: e.dma_start(out=kns, in_=z[0:tp, 512:1024]), reads=[b_z], sem=b_z)
        P.dma("sp", lambda e: e.dma_start(out=vns, in_=z[0:tp, 1024:1536]), reads=[b_z], sem=b_z)

        RT, WT_ = [b_tm], [b_tm]
        P.op("dve", lambda e: e.tensor_tensor(out=tmp16[0:tp], in0=z[0:tp, 0:512], in1=z[0:tp, 512:1024], op=ALU.mult),
             reads=[b_z], writes=WT_)
        P.op("dve", lambda e: e.tensor_reduce(out=sself[0:tp], in_=tmp16[0:tp].rearrange("p (h c) -> p h c", h=8), axis=AX.X, op=ALU.add),
             reads=RT, writes=WT_)
        P.op("act", lambda e: e.activation(out=pself[0:tp], in_=sself[0:tp], func=AF.Exp, scale=0.125), reads=RT, writes=WT_)
        P.op("dve", lambda e: e.tensor_scalar(out=pself[0:tp], in0=pself[0:tp], scalar1=3.0, scalar2=None, op0=ALU.mult), reads=RT, writes=WT_)
        NTOT = tp * 3
        for b in range(tp):
            P.op("pe", lambda e, b=b: e.matmul(pbank[4][:], lhsT=selrow[0:16, b * 128:(b + 1) * 128], rhs=z[0:16, 0:512],
                                               start=True, stop=True), reads=[b_sc, b_z], writes=[b_pb[4]])
            for g, dil in enumerate((1, 4, 16)):
                n = b * 3 + g
                jj = n % 2
                st0 = 2048 - 128 * dil
                if dil == 1:
                    ksrc = ck[b, st0:2048, :]
                    vsrc = cv[b, st0:2048, :]
                else:
                    ksrc = ck[b, st0:2048, :].rearrange("(r d) c -> r d c", d=dil)[:, 0, :]
                    vsrc = cv[b, st0:2048, :].rearrange("(r d) c -> r d c", d=dil)[:, 0, :]
                P.dma("sp", lambda e, jj=jj, ksrc=ksrc: e.dma_start(out=kg[jj], in_=ksrc), writes=[b_kg[jj]], sem=b_kg[jj])
                P.dma("sp", lambda e, jj=jj, vsrc=vsrc: e.dma_start(out=vg_[jj], in_=vsrc), writes=[b_vg[jj]], sem=b_vg[jj])
                P.op("dve", lambda e, jj=jj: e.tensor_tensor(out=prod, in0=kg[jj], in1=pbank[4][:], op=ALU.mult),
                     reads=[b_kg[jj], b_pb[4]], writes=[b_prod])
                P.op("dve", lambda e: e.tensor_reduce(out=sc, in_=prod.rearrange("p (h c) -> p h c", h=8), axis=AX.X, op=ALU.add),
                     reads=[b_prod], writes=[b_scr])
                P.op("dve", lambda e, g=g: e.scalar_tensor_tensor(out=sc, in0=sc, scalar=0.125, in1=sbias_sb[:, g, :],
                                                                 op0=ALU.mult, op1=ALU.add), reads=[b_scr, b_sc], writes=[b_scr])
                P.op("act", lambda e, jj=jj: e.activation(out=p_[jj], in_=sc, func=AF.Exp), reads=[b_scr], writes=[b_p[jj]])
                P.op("dve", lambda e, jj=jj: e.tensor_tensor(out=pvt[jj].rearrange("p (h c) -> p h c", h=8),
                                                             in0=vg_[jj].rearrange("p (h c) -> p h c", h=8),
                                                             in1=p_[jj].unsqueeze(2).broadcast_to([128, 8, 64]), op=ALU.mult),
                     reads=[b_vg[jj], b_p[jj]], writes=[b_pvt[jj]])
                P.op("pe", lambda e, jj=jj, b=b, n=n: e.matmul(pbank[5][0:16, :], lhsT=selcol[:, b, :], rhs=pvt[jj],
                                                              start=(n == 0), stop=(n == NTOT - 1)),
                     reads=[b_pvt[jj], b_sc], writes=[b_pb[5]])
                P.op("pe", lambda e, jj=jj, b=b, n=n: e.matmul(pbank[6][0:16, 0:8], lhsT=selcol[:, b, :], rhs=p_[jj],
                                                              start=(n == 0), stop=(n == NTOT - 1)),
                     reads=[b_p[jj], b_sc], writes=[b_pb[6]])
        P.op("dve", lambda e: e.tensor_tensor(out=tmp16[0:tp].rearrange("p (h c) -> p h c", h=8),
                                              in0=z[0:tp, 1024:1536].rearrange("p (h c) -> p h c", h=8),
                                              in1=pself[0:tp].unsqueeze(2).broadcast_to([tp, 8, 64]), op=ALU.mult),
             reads=[b_z, b_tm], writes=WT_)
        P.op("dve", lambda e: e.tensor_tensor(out=tmp16[0:tp], in0=tmp16[0:tp], in1=pbank[5][0:tp, :], op=ALU.add),
             reads=[b_tm, b_pb[5]], writes=WT_)
        P.op("dve", lambda e: e.tensor_tensor(out=den[0:tp], in0=pself[0:tp], in1=pbank[6][0:tp, 0:8], op=ALU.add),
             reads=[b_tm, b_pb[6]], writes=WT_)
        P.op("dve", lambda e: e.reciprocal(out=rden[0:tp], in_=den[0:tp]), reads=RT, writes=WT_)
        P.op("dve", lambda e: e.tensor_tensor(out=catf[0:tp, 0:512].rearrange("p (h c) -> p h c", h=8),
                                              in0=tmp16[0:tp].rearrange("p (h c) -> p h c", h=8),
                                              in1=rden[0:tp].unsqueeze(2).broadcast_to([tp, 8, 64]), op=ALU.mult),
             reads=RT, writes=[b_catf])

        for r in range(4):
            P.op("pe", lambda e, r=r: e.transpose(out=pbank[1][:, r * 16:(r + 1) * 16], in_=z[0:tp, 1536 + r * 128:1536 + (r + 1) * 128],
                                                  identity=identf[0:tp, 0:tp]), reads=[b_z, b_sc], writes=[b_pb[1]])
        P.op("act", lambda e: e.copy(out=rqT, in_=pbank[1][:, 0:64].rearrange("p (r b) -> p r b", r=4)), reads=[b_pb[1]], writes=[b_rqT])
        for r in range(4):
            P.op("dve", lambda e, r=r: e.tensor_tensor(out=qm[:, r], in0=rqT[:, r, :].unsqueeze(1).broadcast_to([128, 16, 16]),
                                                       in1=selcol, op=ALU.mult), reads=[b_rqT, b_sc], writes=[b_qm])
        for r in range(4):
            for b in range(tp):
                n = r * tp + b
                jj = n % 2
                P.dma("sp", lambda e, jj=jj, b=b, r=r: e.dma_start(out=S_sb[jj], in_=st[b * 4 + r]), writes=[b_S[jj]], sem=b_S[jj])
                P.op("pe", lambda e, jj=jj, b=b, r=r: e.matmul(pbank[7][0:16, r * 128:(r + 1) * 128], lhsT=qm[:, r, b, :], rhs=S_sb[jj],
                                                              start=(b == 0), stop=(b == tp - 1)),
                     reads=[b_qm, b_S[jj]], writes=[b_pb[7]])
                P.op("dve", lambda e, jj=jj, b=b, r=r: e.tensor_scalar(out=kmask[jj][0:tp], in0=z[0:tp, 2048 + r * 128:2048 + (r + 1) * 128],
                                                                      scalar1=identf[0:tp, b:b + 1], scalar2=128.0 ** -0.5,
                                                                      op0=ALU.mult, op1=ALU.mult),
                     reads=[b_z, b_sc], writes=[b_kmask[jj]])
                pa = 2 + jj
                P.op("pe", lambda e, jj=jj, r=r, pa=pa: e.matmul(pbank[pa][:, 0:128], lhsT=kmask[jj][0:tp], rhs=z[0:tp, 2560 + r * 128:2560 + (r + 1) * 128],
                                                                start=True, stop=True),
                     reads=[b_kmask[jj], b_z], writes=[b_pb[pa]])
                P.op("dve", lambda e, jj=jj, r=r, pa=pa: e.scalar_tensor_tensor(out=sout[jj], in0=S_sb[jj], scalar=GAM[r], in1=pbank[pa][:, 0:128],
                                                                               op0=ALU.mult, op1=ALU.add),
                     reads=[b_S[jj], b_pb[pa]], writes=[b_sout[jj]])
                P.dma("sp", lambda e, jj=jj, b=b, r=r: e.dma_start(out=rs[b * 4 + r], in_=sout[jj]), reads=[b_sout[jj]], sem=b_sout[jj])
        P.op("dve", lambda e: e.tensor_tensor(out=tmp16[0:tp], in0=z[0:tp, 1536:2048], in1=z[0:tp, 2048:2560], op=ALU.mult),
             reads=[b_z, b_tm], writes=WT_)
        P.op("dve", lambda e: e.tensor_reduce(out=qk[0:tp], in_=tmp16[0:tp].rearrange("p (r c) -> p r c", r=4), axis=AX.X, op=ALU.add),
             reads=RT, writes=WT_)
        P.op("dve", lambda e: e.scalar_tensor_tensor(out=ro[0:tp], in0=z[0:tp, 2560:3072].rearrange("p (r c) -> p r c", r=4),
                                                     scalar=128.0 ** -0.5, in1=qk[0:tp].unsqueeze(2).broadcast_to([tp, 4, 128]),
                                                     op0=ALU.mult, op1=ALU.mult), reads=[b_z, b_tm], writes=[b_ro])
        for r in range(4):
            P.op("dve", lambda e, r=r: e.scalar_tensor_tensor(out=ro[0:tp, r, :], in0=pbank[7][0:tp, r * 128:(r + 1) * 128], scalar=GAM[r],
                                                             in1=ro[0:tp, r, :], op0=ALU.mult, op1=ALU.add),
                 reads=[b_pb[7], b_ro], writes=[b_ro])
        P.op("act", lambda e: e.activation(out=sgs[0:tp], in_=z[0:tp, 3072:3584].rearrange("p (r c) -> p r c", r=4), func=AF.Silu),
             reads=[b_z], writes=[b_sgs])
        groupnorm(tp, ro[0:tp], b_ro, ggn_bc[0:tp, :].rearrange("p (r c) -> p r c", r=4), sgs[0:tp], b_sgs,
                  catf[0:tp, 512:1024].rearrange("p (r c) -> p r c", r=4), [b_catf], gn_a, gn_b, gn_s, b_gn)
        P.op("act", lambda e: e.copy(out=cat_s[0:tp, :], in_=catf[0:tp, :]), reads=[b_catf], writes=[b_cats])

    def prompt_mixer(s):
        if s > 0:
            P.barrier()
        A.reset()
        dtab = A.alloc([TW], F32)
        multt = A.alloc([TW], BF16)
        caust = A.alloc([TW], BF16)
        kdec = A.alloc([4 * NT], F32)
        b_tab = Buf("tab")
        for (dst, src) in ((dtab, dtab_d), (multt, multt_d), (caust, caust_d), (kdec, kdec_d)):
            P.dma("sp", lambda e, dst=dst, src=src: e.dma_start(out=dst, in_=src), writes=[b_tab], sem=b_tab)
        xt = [A.alloc([D], F32) for i in range(2)]
        b_xt = [Buf("xt%d" % i) for i in range(2)]
        xb = [A.alloc([D], BF16) for i in range(2)]
        b_xb = [Buf("xb%d" % i) for i in range(2)]
        junk = A.alloc([D], BF16)
        b_junk = Buf("junk")
        stat = A.alloc([NT, 4], F32)
        b_stat = [Buf("stat%d" % i) for i in range(NT)]
        xT = A.alloc([8, SEQ], BF16)
        b_xT = [Buf("xT%d" % i) for i in range(NT)]
        wkv = A.alloc([8, 1024], BF16)
        b_wkv = Buf("wkv")
        kvst = [A.alloc([1024], F32) for i in range(2)]
        b_kvst = [Buf("kvst%d" % i) for i in range(2)]
        wu = [A.alloc([8, 4, 128], BF16) for i in range(2)]
        b_wu = [Buf("wu%d" % i) for i in range(2)]
        QT = [A.alloc([SEQ], BF16) for i in range(2)]
        KT = [A.alloc([SEQ], BF16) for i in range(2)]
        b_QT = [Buf("QT%d" % i) for i in range(2)]
        b_KT = [Buf("KT%d" % i) for i in range(2)]
        Vb = [A.alloc([NT, 130], BF16) for i in range(2)]
        b_Vb = [Buf("Vb%d" % i) for i in range(2)]
        Kd = A.alloc([NT, 128], BF16)
        b_Kd = Buf("Kd")
        sg = A.alloc([NT, 128], BF16)
        b_sg = Buf("sg")
        Et = [A.alloc([TW], BF16) for i in range(2)]
        b_Et = [Buf("Et%d" % i) for i in range(2)]
        Etmp = A.alloc([TW], BF16)
        b_Etmp = Buf("Etmp")
        NPB = 4
        Pe = [A.alloc([512], BF16) for i in range(NPB)]
        b_Pe = [Buf("Pe%d" % i) for i in range(NPB)]
        Pm = [A.alloc([512], BF16) for i in range(NPB)]
        b_Pm = [Buf("Pm%d" % i) for i in range(NPB)]
        gn_a = A.alloc([4, 128], F32)
        gn_b = A.alloc([4, 128], F32)
        gn_s = A.alloc([16], F32)
        b_gn = Buf("gn")
        rden = A.alloc([4], F32)
        b_rden = Buf("rden")
        sfin = A.alloc([128], F32)
        b_sfin = Buf("sfin")

        for k in range(8):
            P.dma("sp", lambda e, k=k: e.dma_start(out=wkv[:, k, :], in_=win_bf[k * 128:(k + 1) * 128, 512:1536]),
                  reads=[b_winbf], writes=[b_wkv], sem=b_wkv)

        for i in range(NT):
            j = i % 2
            r0 = s * SEQ + i * 128
            P.dma("sp", lambda e, j=j, r0=r0: e.dma_start(out=xt[j], in_=xp[r0:r0 + 128, :]),
                  writes=[b_xt[j]], sem=b_xt[j])
            rmsnorm_T(xt[j], b_xt[j], xb[j], b_xb[j], junk, b_junk, stat[:, i, :], b_stat[i], j, g1_sb[:],
                      xT[:, :, i * 128:(i + 1) * 128], b_xT[i])

        for i in range(NT):
            j = i % 2
            r0 = s * SEQ + i * 128
            for half in range(2):
                pa = 2 + 2 * j + half
                for k in range(8):
                    P.op("pe", lambda e, pa=pa, k=k, i=i, half=half: e.matmul(
                        pbank[pa][:], lhsT=xT[:, k, i * 128:(i + 1) * 128], rhs=wkv[:, k, half * 512:(half + 1) * 512],
                        start=(k == 0), stop=(k == 7)),
                        reads=[b_xT[i], b_wkv], writes=[b_pb[pa]])
                P.op("act", lambda e, pa=pa, j=j, half=half: e.copy(out=kvst[j][:, half * 512:(half + 1) * 512],
                                                                   in_=pbank[pa][:]),
                     reads=[b_pb[pa]], writes=[b_kvst[j]])
            P.dma("sp", lambda e, j=j, r0=r0: e.dma_start(out=kwp[r0:r0 + 128, :], in_=kvst[j][:, 0:512]),
                  reads=[b_kvst[j]], sem=b_kvst[j])
            P.dma("sp", lambda e, j=j, r0=r0: e.dma_start(out=vwp[r0:r0 + 128, :], in_=kvst[j][:, 512:1024]),
                  reads=[b_kvst[j]], sem=b_kvst[j])

        if stop <= 1:
            return
        for j in range(2):
            P.op("pool", lambda e, j=j: e.memset(Vb[j][:, :, 64:65], 1.0), writes=[b_Vb[j]])
            P.op("pool", lambda e, j=j: e.memset(Vb[j][:, :, 129:130], 1.0), writes=[b_Vb[j]])

        for u in range(8):
            if u >= stop - 10:
                return
            j = u % 2
            is_att = u < 4
            if is_att:
                cols = [u * 128, 512 + u * 128, 1024 + u * 128]
            else:
                r = u - 4
                cols = [1536 + r * 128, 2048 + r * 128, 2560 + r * 128, 3072 + r * 128]
            for m, c0 in enumerate(cols):
                P.dma("sp", lambda e, j=j, m=m, c0=c0: e.dma_start(
                    out=wu[j][:, :, m, :], in_=win_bf[:, c0:c0 + 128].rearrange("(k p) c -> p k c", p=128)),
                    reads=[b_winbf], writes=[b_wu[j]], sem=b_wu[j])
            for tg in range(NQG):
                for m in range(2):
                    pa = 2 + (2 * tg + m) % 4
                    for k in range(8):
                        P.op("pe", lambda e, pa=pa, k=k, m=m, tg=tg, j=j: e.matmul(
                            pbank[pa][:], lhsT=wu[j][:, k, m, :], rhs=xT[:, k, tg * 512:(tg + 1) * 512],
                            start=(k == 0), stop=(k == 7)),
                            reads=[b_wu[j]] + b_xT[4 * tg:4 * tg + 4], writes=[b_pb[pa]])
                    dst = (QT if m == 0 else KT)[j]
                    bd = (b_QT if m == 0 else b_KT)[j]
                    sc = 1.0 if (is_att or m == 0) else 128.0 ** -0.5
                    P.op("act", lambda e, pa=pa, dst=dst, tg=tg, sc=sc: e.activation(
                        out=dst[:, tg * 512:(tg + 1) * 512], in_=pbank[pa][:], func=AF.Copy, scale=sc),
                        reads=[b_pb[pa]], writes=[bd])
            if u == 4 and RSTOP <= 1:
                return
            for i in range(NT):
                pa = 2 + i % 4
                ncol = 128 if is_att else 384
                m0 = 2 if is_att else 1
                for k in range(8):
                    P.op("pe", lambda e, pa=pa, k=k, i=i, j=j, m0=m0, ncol=ncol: e.matmul(
                        pbank[pa][:, 0:ncol], lhsT=xT[:, k, i * 128:(i + 1) * 128],
                        rhs=wu[j][:, k, m0:m0 + ncol // 128, :].rearrange("p m c -> p (m c)"),
                        start=(k == 0), stop=(k == 7)),
                        reads=[b_xT[i], b_wu[j]], writes=[b_pb[pa]])
                if is_att:
                    P.op("act", lambda e, pa=pa, i=i, j=j: e.copy(
                        out=Vb[j][:, i, :].rearrange("p (h c) -> p h c", h=2)[:, :, 0:64],
                        in_=pbank[pa][:, 0:128].rearrange("p (h c) -> p h c", h=2)),
                        reads=[b_pb[pa]], writes=[b_Vb[j]])
                else:
                    r = u - 4
                    P.op("dve", lambda e, pa=pa, i=i, r=r: e.tensor_scalar(
                        out=Kd[:, i, :], in0=pbank[pa][:, 0:128], scalar1=kdec[:, r * NT + i:r * NT + i + 1],
                        scalar2=None, op0=ALU.mult), reads=[b_pb[pa], b_tab], writes=[b_Kd])
                    P.op("act", lambda e, pa=pa, i=i, j=j: e.copy(out=Vb[j][:, i, 0:128], in_=pbank[pa][:, 128:256]),
                         reads=[b_pb[pa]], writes=[b_Vb[j]])
                    P.op("act", lambda e, pa=pa, i=i: e.activation(out=sg[:, i, :], in_=pbank[pa][:, 256:384], func=AF.Silu),
                         reads=[b_pb[pa]], writes=[b_sg])

            if stop <= 2 or (u == 4 and RSTOP <= 2):
                return
            heads = [0, 1] if is_att else [0]
            for hl in heads:
                if is_att:
                    h = 2 * u + hl
                    prow = slice(hl * 64, hl * 64 + 64)
                    tabsc = -SLOPES[h]
                    mt = multt
                    vw = 65
                else:
                    r = u - 4
                    prow = slice(0, 128)
                    tabsc = LG[r]
                    mt = caust
                    vw = 128
                ej = (2 * u + hl) % 2
                P.op("act", lambda e, tabsc=tabsc: e.activation(out=Etmp, in_=dtab, func=AF.Exp, scale=tabsc),
                     reads=[b_tab], writes=[b_Etmp])
                P.op("dve", lambda e, ej=ej, mt=mt: e.tensor_tensor(out=Et[ej], in0=Etmp, in1=mt, op=ALU.mult),
                     reads=[b_Etmp, b_tab], writes=[b_Et[ej]])
                items = [(qg, kb) for qg in range(NQG) for kb in range(4 * qg + 4)]
                LA = 3

                def front(n, qg, kb, j=j, prow=prow, ej=ej):
                    pa = 2 + n % 4
                    pb_ = n % NPB
                    off = qg * 512 - kb * 128 + 384
                    P.op("pe", lambda e: e.matmul(pbank[pa][:], lhsT=KT[j][prow, kb * 128:(kb + 1) * 128],
                                                  rhs=QT[j][prow, qg * 512:(qg + 1) * 512], start=True, stop=True),
                         reads=[b_KT[j], b_QT[j]], writes=[b_pb[pa]])
                    if is_att:
                        P.op("act", lambda e: e.activation(out=Pe[pb_], in_=pbank[pa][:], func=AF.Exp, scale=0.125),
                             reads=[b_pb[pa]], writes=[b_Pe[pb_]])
                        P.op("dve", lambda e: e.tensor_tensor(out=Pm[pb_], in0=Pe[pb_], in1=Et[ej][:, off:off + 512],
                                                              op=ALU.mult),
                             reads=[b_Pe[pb_], b_Et[ej]], writes=[b_Pm[pb_]])
                    else:
                        P.op("dve", lambda e: e.tensor_tensor(out=Pm[pb_], in0=pbank[pa][:], in1=Et[ej][:, off:off + 512],
                                                              op=ALU.mult),
                             reads=[b_pb[pa], b_Et[ej]], writes=[b_Pm[pb_]])

                def back(n, qg, kb, j=j, hl=hl, vw=vw, u=u):
                    pb_ = n % NPB
                    po = qg % 2
                    ov = pbank[po][:].rearrange("p (q c) -> p q c", q=4)
                    for qb in range(4):
                        QB = 4 * qg + qb
                        if kb > QB:
                            continue
                        if is_att:
                            rhs = Vb[j][:, kb, hl * 65:hl * 65 + 65]
                        else:
                            rhs = Vb[j][:, kb, 0:128]
                        P.op("pe", lambda e, qb=qb, QB=QB, rhs=rhs: e.matmul(
                            ov[:, qb, 0:vw], lhsT=Pm[pb_][:, qb * 128:(qb + 1) * 128], rhs=rhs,
                            start=(kb == 0 and qb == 0), stop=(kb == QB), skip_group_check=True),
                            reads=[b_Pm[pb_], b_Vb[j]], writes=[b_pb[po]])
                    if kb == 4 * qg + 3:
                        bc = b_cat[4 * qg:4 * qg + 4]
                        if is_att:
                            h = 2 * u + hl
                            P.op("dve", lambda e: e.reciprocal(out=rden, in_=ov[:, :, 64]), reads=[b_pb[po]], writes=[b_rden])
                            P.op("dve", lambda e: e.tensor_tensor(
                                out=catTok[:, 4 * qg:4 * qg + 4, h * 64:(h + 1) * 64], in0=ov[:, :, 0:64],
                                in1=rden.unsqueeze(2).broadcast_to([128, 4, 64]), op=ALU.mult),
                                reads=[b_pb[po], b_rden], writes=bc)
                        else:
                            r = u - 4
                            groupnorm(128, ov, b_pb[po],
                                      ggn_bc[:, r * 128:(r + 1) * 128].unsqueeze(1).broadcast_to([128, 4, 128]),
                                      sg[:, 4 * qg:4 * qg + 4, :], b_sg,
                                      catTok[:, 4 * qg:4 * qg + 4, 512 + r * 128:512 + (r + 1) * 128], bc,
                                      gn_a, gn_b, gn_s, b_gn)

                for n in range(len(items) + LA):
                    if n < len(items):
                        front(n, *items[n])
                    if n >= LA:
                        back(n - LA, *items[n - LA])

            if stop <= 3 or (u == 4 and RSTOP <= 3):
                return
            if not is_att:
                r = u - 4
                pa = 2
                for kb in range(NT):
                    P.op("pe", lambda e, kb=kb, j=j: e.matmul(pbank[pa][:, 0:128], lhsT=Kd[:, kb, :], rhs=Vb[j][:, kb, 0:128],
                                                              start=(kb == 0), stop=(kb == NT - 1)),
                         reads=[b_Kd, b_Vb[j]], writes=[b_pb[pa]])
                P.op("act", lambda e: e.copy(out=sfin, in_=pbank[pa][:, 0:128]), reads=[b_pb[pa]], writes=[b_sfin])
                P.dma("sp", lambda e, r=r: e.dma_start(out=rp[s * 4 + r], in_=sfin), reads=[b_sfin], sem=b_sfin)


    def peer_phase(tp, nsub, n_ptiles, cat_src, x_src, y_dst):
        P.barrier()
        A.reset()
        slot = [A.alloc([4096], BF16) for _ in range(4)]
        b_slot = [Buf("slot%d" % i) for i in range(4)]
        su = [sl.rearrange("p (k e) -> p k e", k=8) for sl in slot]
        sv_ = [sl.rearrange("p (c d) -> p c d", c=4) for sl in slot]
        WT = A.alloc([128, 256], BF16)
        b_WT = Buf("WT")
        skT_sb = A.alloc([16, 128], BF16)
        b_skT = Buf("skT")
        P.dma("pool", lambda e: e.dma_start(out=skT_sb, in_=skT.rearrange("p (a b) -> p a b", a=16)),
              writes=[b_skT], sem=b_skT)
        xn2T = A.alloc([8, 256], BF16)
        b_xn2T = [Buf("xn2T%d" % a) for a in range(2)]
        qT = A.alloc([16, 256], BF16)
        b_qT = Buf("qT")
        h_sb = [A.alloc([D], F32) for a in range(2)]
        b_h = [Buf("h%d" % a) for a in range(2)]
        xt2 = [A.alloc([D], F32)] * 2
        b_xt2 = [Buf("xt2")] * 2
        hb = A.alloc([D], BF16)
        b_hb = Buf("hb")
        catT_t = A.alloc([8, 128], BF16)
        b_catT = Buf("catT")
        NGB = 4
        Gt = [A.alloc([256], BF16) for _ in range(NGB)]
        b_Gt = [Buf("Gt%d" % i) for i in range(NGB)]
        Ht = [A.alloc([256], BF16) for _ in range(NGB)]
        b_Ht = [Buf("Ht%d" % i) for i in range(NGB)]
        pk = [A.alloc([256], BF16), A.alloc([256], BF16), A.alloc([256], F32)]
        iota_b = A.alloc([128], BF16)
        b_pk = Buf("pk")
        tk = [A.alloc([128], F32) for _ in range(3)]
        b_tk = Buf("tk")
        sv = A.alloc([8, 2, 16], F32)
        si = A.alloc([8, 2, 16], U32)
        si_f = A.alloc([8, 2, 16], F32)
        _swp = [A.alloc([256], F32) for _ in range(4)]
        swork = [_swp[i // 2][:, (i % 2) * 128:(i % 2 + 1) * 128] for i in range(8)]
        b_sw = [Buf("sw%d" % i) for i in range(8)]
        cand = [A.alloc([16, 16], F32) for _ in range(4)]
        cwork = [_swp[i].rearrange("p (a b) -> p a b", a=16) for i in range(4)]
        b_cand = [Buf("cand%d" % i) for i in range(4)]
        b_cw = [Buf("cw%d" % i) for i in range(4)]
        b_svc = [[Buf("sv%d_%d" % (h, c)) for c in range(2)] for h in range(8)]
        b_sic = [[Buf("si%d_%d" % (h, c)) for c in range(2)] for h in range(8)]
        b_best = [Buf("best%d" % h) for h in range(8)]
        b_pos = [Buf("pos%d" % h) for h in range(8)]
        best = A.alloc([8, 16], F32)
        pos = A.alloc([8, 16], U32)
        pa_u = A.alloc([8, 16], U32)
        pb_u = A.alloc([8, 16], U32)
        pa_f = A.alloc([8, 16], F32)
        pb_f = A.alloc([8, 16], F32)
        oh = [A.alloc([16, 16], F32) for _ in range(2)]
        b_oh = [Buf("oh%d" % i) for i in range(2)]
        ebest = A.alloc([8, 16], F32)
        esum = A.alloc([16], F32)
        b_tkw = Buf("topk_work")
        NOB = 3
        OIb = [A.alloc([8, 128], BF16) for _ in range(NOB)]
        OJb = [A.alloc([8, 128], BF16) for _ in range(NOB)]
        b_OI = [Buf("OI%d" % i) for i in range(NOB)]
        b_OJ = [Buf("OJ%d" % i) for i in range(NOB)]
        stat2 = A.alloc([2, 4], F32)
        b_stat2 = [Buf("stat2%d" % a) for a in range(2)]
        fstat = A.alloc([2, 4], F32)
        b_fstat = [Buf("fstat%d" % a) for a in range(2)]
        identf = A.alloc([128], F32)
        b_identf = Buf("identf")
        P.op("dve", lambda e: e.tensor_copy(out=identf, in_=ident[:]), reads=[b_const], writes=[b_identf])
        P.op("dve", lambda e: e.tensor_copy(out=iota_b, in_=iota[:]), reads=[b_const], writes=[b_identf])

        for tile in range(n_ptiles):
            T = tp * nsub
            for half in range(2):
                P.dma("sp", lambda e, half=half: e.dma_start(
                    out=su[half], in_=wout_bf[:, half * 512:(half + 1) * 512].rearrange("(k p) c -> p k c", p=128)),
                    reads=[b_woutbf], writes=[b_slot[half]], sem=b_slot[half])
            for a in range(nsub):
                cat_ap, cat_b = cat_src(tile, a)
                x_dr = x_src(tile, a)
                P.dma("sp", lambda e, a=a, x_dr=x_dr: e.dma_start(out=xt2[a][0:tp], in_=x_dr),
                      writes=[b_xt2[a]], sem=b_xt2[a])
                pst = pbank[0][:].bitcast(BF16)
                for k in range(8):
                    P.op("pe", lambda e, k=k, cat_ap=cat_ap: e.transpose(out=pst[:, k * 128:k * 128 + tp],
                                                                        in_=cat_ap[0:tp, k * 128:(k + 1) * 128],
                                                                        identity=ident[0:tp, 0:tp]),
                         reads=[cat_b, b_const], writes=[b_pb[0]])
                P.op("act", lambda e: e.copy(out=catT_t[:, :, 0:tp], in_=pst.rearrange("p (k t) -> p k t", k=8)[:, :, 0:tp]),
                     reads=[b_pb[0]], writes=[b_catT])
                for half in range(2):
                    pa = 2 + half
                    for k in range(8):
                        P.op("pe", lambda e, k=k, half=half, pa=pa: e.matmul(
                            pbank[pa][0:tp, :], lhsT=catT_t[:, k, 0:tp], rhs=su[half][:, k, :], start=(k == 0), stop=(k == 7)),
                            reads=[b_catT, b_slot[half]], writes=[b_pb[pa]])
                    P.op("dve", lambda e, a=a, half=half, pa=pa: e.tensor_tensor(
                        out=h_sb[a][0:tp, half * 512:(half + 1) * 512], in0=pbank[pa][0:tp, :],
                        in1=xt2[a][0:tp, half * 512:(half + 1) * 512], op=ALU.add),
                        reads=[b_pb[pa], b_xt2[a]], writes=[b_h[a]])
                rmsnorm_T(h_sb[a], b_h[a], hb, b_hb, hb, b_hb, stat2[:, a, :], b_stat2[a], 1, g2_sb[:],
                          xn2T[:, :, a * tp:(a + 1) * tp], b_xn2T[a], tp=tp)

            for qq in range(4):
                sl = (2 + qq) % 4
                P.dma("sp", lambda e, qq=qq, sl=sl: e.dma_start(
                    out=su[sl], in_=wpq_bf[:, qq * 512:(qq + 1) * 512].rearrange("(k p) c -> p k c", p=128)),
                    reads=[b_wpqbf], writes=[b_slot[sl]], sem=b_slot[sl])
                for cc in range(4):
                    hc = qq * 4 + cc
                    pa = 2 + hc % 4
                    for k in range(8):
                        P.op("pe", lambda e, k=k, cc=cc, sl=sl, pa=pa: e.matmul(
                            pbank[pa][:, 0:T], lhsT=su[sl][:, k, cc * 128:(cc + 1) * 128], rhs=xn2T[:, k, 0:T],
                            start=(k == 0), stop=(k == 7)),
                            reads=[b_slot[sl]] + b_xn2T, writes=[b_pb[pa]])
                    P.op("act", lambda e, hc=hc, pa=pa: e.copy(out=qT[:, hc, 0:T], in_=pbank[pa][:, 0:T]),
                         reads=[b_pb[pa]], writes=[b_qT])

            for a in range(nsub):
                for h in range(8):
                    pa = 2 + (h // 2) % 4
                    for c in range(2):
                        P.op("pe", lambda e, h=h, c=c, a=a, pa=pa: e.matmul(
                            pbank[pa][0:tp, (h % 2) * 256 + c * 128:(h % 2) * 256 + (c + 1) * 128],
                            lhsT=qT[:, 2 * h + c, a * tp:(a + 1) * tp], rhs=skT_sb[:, 2 * h + c, :],
                            start=True, stop=True), reads=[b_qT, b_skT], writes=[b_pb[pa]])
                W_ = [b_tkw]
                R_ = [b_tkw]
                for hg in range(2):
                    chains = [(h, c) for h in range(4 * hg, 4 * hg + 4) for c in range(2)]

                    def srcof(h, c):
                        return pbank[2 + (h // 2) % 4][0:tp, (h % 2) * 256 + c * 128:(h % 2) * 256 + (c + 1) * 128], b_pb[2 + (h // 2) % 4]

                    for ci, (h, c) in enumerate(chains):
                        src, bsrc = srcof(h, c)
                        P.op("dve", lambda e, h=h, c=c, src=src: e.max(out=sv[0:tp, h, c, 0:8], in_=src), reads=[bsrc], writes=[b_svc[h][c]])
                    for ci, (h, c) in enumerate(chains):
                        src, bsrc = srcof(h, c)
                        P.op("dve", lambda e, h=h, c=c, src=src: e.max_index(out=si[0:tp, h, c, 0:8], in_max=sv[0:tp, h, c, 0:8], in_values=src),
                             reads=[bsrc, b_svc[h][c]], writes=[b_sic[h][c]])
                    for ci, (h, c) in enumerate(chains):
                        src, bsrc = srcof(h, c)
                        P.op("dve", lambda e, h=h, c=c, src=src, ci=ci: e.match_replace(out=swork[ci][0:tp], in_to_replace=sv[0:tp, h, c, 0:8],
                                                                                      in_values=src, imm_value=-1e30),
                             reads=[bsrc, b_svc[h][c]], writes=[b_sw[ci]])
                    for ci, (h, c) in enumerate(chains):
                        P.op("dve", lambda e, h=h, c=c, ci=ci: e.max(out=sv[0:tp, h, c, 8:16], in_=swork[ci][0:tp]), reads=[b_sw[ci]], writes=[b_svc[h][c]])
                    for ci, (h, c) in enumerate(chains):
                        P.op("dve", lambda e, h=h, c=c, ci=ci: e.max_index(out=si[0:tp, h, c, 8:16], in_max=sv[0:tp, h, c, 8:16], in_values=swork[ci][0:tp]),
                             reads=[b_sw[ci], b_svc[h][c]], writes=[b_sic[h][c]])
                    hs = list(range(4 * hg, 4 * hg + 4))
                    for hi, h in enumerate(hs):
                        P.op("dve", lambda e, h=h, hi=hi: e.tensor_tensor(
                            out=cand[hi][0:tp], in0=sv[0:tp, h, 0, :].unsqueeze(2).broadcast_to([tp, 16, 16]),
                            in1=sv[0:tp, h, 1, :].unsqueeze(1).broadcast_to([tp, 16, 16]), op=ALU.add),
                            reads=[b_svc[h][0], b_svc[h][1]], writes=[b_cand[hi]])
                    cf = [cand[hi][0:tp].rearrange("p a b -> p (a b)") for hi in range(4)]
                    cwf = [cwork[hi][0:tp].rearrange("p a b -> p (a b)") for hi in range(4)]
                    for hi, h in enumerate(hs):
                        P.op("dve", lambda e, h=h, hi=hi: e.max(out=best[0:tp, h, 0:8], in_=cf[hi]), reads=[b_cand[hi]], writes=[b_best[h]])
                    for hi, h in enumerate(hs):
                        P.op("dve", lambda e, h=h, hi=hi: e.max_index(out=pos[0:tp, h, 0:8], in_max=best[0:tp, h, 0:8], in_values=cf[hi]),
                             reads=[b_cand[hi], b_best[h]], writes=[b_pos[h]])
                    for hi, h in enumerate(hs):
                        P.op("dve", lambda e, h=h, hi=hi: e.match_replace(out=cwf[hi], in_to_replace=best[0:tp, h, 0:8], in_values=cf[hi], imm_value=-1e30),
                             reads=[b_cand[hi], b_best[h]], writes=[b_cw[hi], b_sw[2 * hi], b_sw[2 * hi + 1]])
                    for hi, h in enumerate(hs):
                        P.op("dve", lambda e, h=h, hi=hi: e.max(out=best[0:tp, h, 8:16], in_=cwf[hi]), reads=[b_cw[hi], b_sw[2 * hi], b_sw[2 * hi + 1]], writes=[b_best[h]])
                    for hi, h in enumerate(hs):
                        P.op("dve", lambda e, h=h, hi=hi: e.max_index(out=pos[0:tp, h, 8:16], in_max=best[0:tp, h, 8:16], in_values=cwf[hi]),
                             reads=[b_cw[hi], b_sw[2 * hi], b_sw[2 * hi + 1], b_best[h]], writes=[b_pos[h]])
                allsv = [b_svc[h][c] for h in range(8) for c in range(2)] + [b_sic[h][c] for h in range(8) for c in range(2)]
                allbp = b_best + b_pos
                P.op("dve", lambda e: e.tensor_copy(out=si_f[0:tp], in_=si[0:tp]), reads=R_ + allsv, writes=W_)
                P.op("dve", lambda e: e.tensor_single_scalar(out=pa_u[0:tp], in_=pos[0:tp], scalar=4, op=ALU.logical_shift_right), reads=R_ + allbp, writes=W_)
                P.op("dve", lambda e: e.tensor_single_scalar(out=pb_u[0:tp], in_=pos[0:tp], scalar=15, op=ALU.bitwise_and), reads=R_ + allbp, writes=W_)
                P.op("dve", lambda e: e.tensor_copy(out=pa_f[0:tp], in_=pa_u[0:tp]), reads=R_, writes=W_)
                P.op("dve", lambda e: e.tensor_copy(out=pb_f[0:tp], in_=pb_u[0:tp]), reads=R_, writes=W_)
                combos = [(which, h) for which in range(2) for h in range(8)]
                for g0 in range(0, 16, 2):
                    grp = combos[g0:g0 + 2]
                    for oi, (which, h) in enumerate(grp):
                        pf = pa_f if which == 0 else pb_f
                        P.op("dve", lambda e, h=h, pf=pf, oi=oi: e.tensor_tensor(
                            out=oh[oi][0:tp], in0=iota[0:tp, 0:16].unsqueeze(1).broadcast_to([tp, 16, 16]),
                            in1=pf[0:tp, h, :].unsqueeze(2).broadcast_to([tp, 16, 16]), op=ALU.is_equal),
                            reads=[b_tkw, b_const], writes=[b_oh[oi]])
                    for oi, (which, h) in enumerate(grp):
                        P.op("dve", lambda e, h=h, which=which, oi=oi: e.tensor_tensor(
                            out=oh[oi][0:tp], in0=oh[oi][0:tp],
                            in1=si_f[0:tp, h, which, :].unsqueeze(1).broadcast_to([tp, 16, 16]), op=ALU.mult),
                            reads=[b_tkw, b_oh[oi]], writes=[b_oh[oi]])
                    for oi, (which, h) in enumerate(grp):
                        P.op("dve", lambda e, which=which, h=h, oi=oi: e.tensor_reduce(
                            out=tk[which][0:tp, h * 16:(h + 1) * 16], in_=oh[oi][0:tp], axis=AX.X, op=ALU.add),
                            reads=[b_oh[oi]], writes=[b_tk])
                P.op("dve", lambda e: e.tensor_tensor(out=ebest[0:tp], in0=best[0:tp], in1=best[0:tp, :, 0:1].broadcast_to([tp, 8, 16]),
                                                      op=ALU.subtract), reads=R_ + allbp, writes=W_)
                P.op("act", lambda e: e.activation(out=ebest[0:tp], in_=ebest[0:tp], func=AF.Exp), reads=R_, writes=W_)
                P.op("dve", lambda e: e.tensor_reduce(out=esum[0:tp, 0:8], in_=ebest[0:tp], axis=AX.X, op=ALU.add), reads=R_, writes=W_)
                P.op("dve", lambda e: e.reciprocal(out=esum[0:tp, 8:16], in_=esum[0:tp, 0:8]), reads=R_, writes=W_)
                P.op("dve", lambda e: e.tensor_tensor(out=tk[2][0:tp].rearrange("p (h k) -> p h k", h=8), in0=ebest[0:tp],
                                                      in1=esum[0:tp, 8:16].unsqueeze(2).broadcast_to([tp, 8, 16]), op=ALU.mult),
                     reads=R_, writes=[b_tk])
                pt = pbank[1][:]
                for w3 in range(3):
                    P.op("pe", lambda e, w3=w3: e.transpose(out=pt[:, w3 * 128:w3 * 128 + tp], in_=tk[w3][0:tp], identity=identf[0:tp, 0:tp]),
                         reads=[b_tk, b_identf], writes=[b_pb[1]])
                for w3 in range(3):
                    P.op("act", lambda e, w3=w3, a=a: e.copy(out=pk[w3][:, a * tp:(a + 1) * tp], in_=pt[:, w3 * 128:w3 * 128 + tp]),
                         reads=[b_pb[1]], writes=[b_pk])

            TB = 8
            for t0 in range(0, T, TB):
                ob = (t0 // TB) % NOB
                P.op("dve", lambda e, t0=t0, ob=ob: e.tensor_tensor(
                    out=OIb[ob], in0=iota_b.unsqueeze(1).broadcast_to([128, TB, 128]),
                    in1=pk[0][:, t0:t0 + TB].unsqueeze(2).broadcast_to([128, TB, 128]), op=ALU.is_equal),
                    reads=[b_pk, b_identf], writes=[b_OI[ob]])
                P.op("dve", lambda e, t0=t0, ob=ob: e.tensor_tensor(
                    out=OJb[ob], in0=iota_b.unsqueeze(1).broadcast_to([128, TB, 128]),
                    in1=pk[1][:, t0:t0 + TB].unsqueeze(2).broadcast_to([128, TB, 128]), op=ALU.is_equal),
                    reads=[b_pk, b_identf], writes=[b_OJ[ob]])
                P.op("dve", lambda e, t0=t0, ob=ob: e.tensor_tensor(
                    out=OJb[ob], in0=OJb[ob],
                    in1=pk[2][:, t0:t0 + TB].unsqueeze(2).broadcast_to([128, TB, 128]), op=ALU.mult),
                    reads=[b_pk, b_OJ[ob]], writes=[b_OJ[ob]])
                for tt in range(TB):
                    t = t0 + tt
                    pw = 6 + (t // 4) % 2
                    P.op("pe", lambda e, t=t, tt=tt, ob=ob, pw=pw: e.matmul(pbank[pw][:, (t % 4) * 128:(t % 4 + 1) * 128],
                                                                           lhsT=OJb[ob][:, tt, :], rhs=OIb[ob][:, tt, :],
                                                                           start=True, stop=True),
                         reads=[b_OI[ob], b_OJ[ob]], writes=[b_pb[pw]])
                    if t % 4 == 3:
                        t4 = t - 3
                        P.op("act", lambda e, t4=t4, pw=pw: e.copy(
                            out=WT[:, :, t4:t4 + 4], in_=pbank[pw][:].rearrange("p (t i) -> p i t", t=4)),
                            reads=[b_pb[pw]], writes=[b_WT])

            items = [(g, c) for g in range(NG) for c in range(4)]
            LA = 3

            def pfront(n, g, c):
                i = g * 4 + c
                us, vs = g % 2, 2 + g % 2
                if c == 0:
                    P.dma("sp", lambda e: e.dma_start(out=slot[us], in_=uT_bf[g]),
                          reads=[b_uTbf], writes=[b_slot[us]], sem=b_slot[us])
                    P.dma("sp", lambda e: e.dma_start(out=slot[vs], in_=pv_bf[g]),
                          reads=[b_pvbf], writes=[b_slot[vs]], sem=b_slot[vs])
                pa = 6 + n % 2
                gb = n % NGB
                for k in range(8):
                    P.op("pe", lambda e, k=k: e.matmul(pbank[pa][:, 0:T], lhsT=su[us][:, k, c * 128:(c + 1) * 128], rhs=xn2T[:, k, 0:T],
                                                       start=(k == 0), stop=(k == 7)),
                         reads=[b_slot[us]] + b_xn2T, writes=[b_pb[pa]])
                P.op("act", lambda e: e.activation(out=Gt[gb][:, 0:T], in_=pbank[pa][:, 0:T], func=GELU), reads=[b_pb[pa]], writes=[b_Gt[gb]])
                P.op("dve", lambda e: e.tensor_tensor(out=Ht[gb][:, 0:T], in0=Gt[gb][:, 0:T], in1=WT[:, i, 0:T], op=ALU.mult),
                     reads=[b_Gt[gb], b_WT], writes=[b_Ht[gb]])

            def pback(n, g, c):
                i = g * 4 + c
                vs = 2 + g % 2
                gb = n % NGB
                for a in range(nsub):
                    for half in range(2):
                        pa = 2 + a * 2 + half
                        P.op("pe", lambda e, a=a, half=half, pa=pa: e.matmul(
                            pbank[pa][0:tp, :], lhsT=Ht[gb][:, a * tp:(a + 1) * tp], rhs=sv_[vs][:, c, half * 512:(half + 1) * 512],
                            start=(i == 0), stop=(i == NI - 1)),
                            reads=[b_Ht[gb], b_slot[vs]], writes=[b_pb[pa]])

            for n in range(len(items) + LA):
                if n < len(items):
                    pfront(n, *items[n])
                if n >= LA:
                    pback(n - LA, *items[n - LA])

            for a in range(nsub):
                y_dr = y_dst(tile, a)
                for half in range(2):
                    pa = 2 + a * 2 + half
                    P.op("dve", lambda e, a=a, half=half, pa=pa: e.tensor_tensor(
                        out=h_sb[a][0:tp, half * 512:(half + 1) * 512], in0=pbank[pa][0:tp, :],
                        in1=h_sb[a][0:tp, half * 512:(half + 1) * 512], op=ALU.add),
                        reads=[b_pb[pa], b_h[a]], writes=[b_h[a]])
                fs = fstat[0:tp, a, :]
                P.op("act", lambda e, a=a, fs=fs: e.activation(out=hb[0:tp], in_=h_sb[a][0:tp], func=AF.Square, accum_out=fs[:, 0:1]),
                     reads=[b_h[a]], writes=[b_hb, b_fstat[a]])
                P.op("dve", lambda e, fs=fs: e.tensor_scalar(out=fs[:, 1:2], in0=fs[:, 0:1], scalar1=1.0 / D, scalar2=1e-6,
                                                            op0=ALU.mult, op1=ALU.add), reads=[b_fstat[a]], writes=[b_fstat[a]])
                P.op("act", lambda e, fs=fs: e.activation(out=fs[:, 2:3], in_=fs[:, 1:2], func=AF.Sqrt),
                     reads=[b_fstat[a]], writes=[b_fstat[a]])
                P.op("dve", lambda e, fs=fs: e.reciprocal(out=fs[:, 3:4], in_=fs[:, 2:3]), reads=[b_fstat[a]], writes=[b_fstat[a]])
                P.op("dve", lambda e, a=a, fs=fs: e.scalar_tensor_tensor(out=h_sb[a][0:tp], in0=h_sb[a][0:tp], scalar=fs[:, 3:4], in1=gf_bc[0:tp, :],
                                                                       op0=ALU.mult, op1=ALU.mult),
                     reads=[b_h[a], b_fstat[a], b_const], writes=[b_h[a]])
                P.dma("sp", lambda e, a=a, y_dr=y_dr: e.dma_start(out=y_dr, in_=h_sb[a][0:tp]),
                      reads=[b_h[a]], sem=b_h[a])

    for s in range(NSEQ):
        prompt_mixer(s)
        if do_peer:
            def rows(tile, a, s=s):
                r0 = s * SEQ + (tile * 2 + a) * 128
                return slice(r0, r0 + 128)
            peer_phase(128, 2, SEQ // 256,
                       lambda tile, a: (catTok[:, tile * 2 + a, :], b_cat[tile * 2 + a]),
                       lambda tile, a, rows=rows: xp[rows(tile, a), :],
                       lambda tile, a, rows=rows: yp[rows(tile, a), :])
    if do_sample:
        sample_mixer()
        if do_peer:
            peer_phase(NS, 1, 1, lambda tile, a: (cat_s[:], b_cats), lambda tile, a: xs, lambda tile, a: ys)

    P.barrier()
    P.emit(nc, es)
    es.close()
    return nc


_NC_CACHE = {}


def kernel(x_prompt, x_sample, cache_k_win, cache_v_win, state_ret, norm1_g, w_in, ret_gn_g, w_out,
           norm2_g, w_pq, peer_sub_keys, peer_u, peer_v, norm_f_g):
    f32 = np.float32
    if "nc" not in _NC_CACHE:
        _NC_CACHE["nc"] = build_program()
    nc = _NC_CACHE["nc"]
    x_prompt = np.asarray(x_prompt, f32)
    x_sample = np.asarray(x_sample, f32)
    ck = np.asarray(cache_k_win, f32)[0].reshape(128, 2048, 512)
    cv = np.asarray(cache_v_win, f32)[0].reshape(128, 2048, 512)
    stt = np.asarray(state_ret, f32)[0].reshape(128 * 4, 128, 128)
    shared = {
        "g1": np.ascontiguousarray(np.asarray(norm1_g, f32)[0].reshape(8, 128).T),
        "g2": np.ascontiguousarray(np.asarray(norm2_g, f32)[0].reshape(8, 128).T),
        "gf": np.asarray(norm_f_g, f32).reshape(1, D),
        "ggn": np.asarray(ret_gn_g, f32).reshape(1, 512),
        "w_in": np.asarray(w_in, f32)[0],
        "w_out": np.asarray(w_out, f32)[0],
        "w_pq": np.asarray(w_pq, f32)[0],
        "skT": np.ascontiguousarray(np.asarray(peer_sub_keys, f32)[0].reshape(16, 128, 128).transpose(2, 0, 1)).reshape(128, 2048),
        "uT": np.ascontiguousarray(np.asarray(peer_u, f32)[0].T),
        "pv": np.asarray(peer_v, f32)[0],
    }
    shared.update(make_tables(SEQ))
    in_maps = []
    for c in range(NCORES):
        m = dict(shared)
        m["xp"] = x_prompt[2 * c:2 * c + 2].reshape(NSEQ * SEQ, D)
        m["xs"] = x_sample[16 * c:16 * c + 16, 0, :]
        m["ck"] = ck[16 * c:16 * c + 16]
        m["cv"] = cv[16 * c:16 * c + 16]
        m["st"] = stt[64 * c:64 * c + 64]
        in_maps.append(m)
    res = run_bass_kernel_spmd(nc, in_maps, core_ids=list(range(NCORES)))
    R = res.results
    cat = lambda k: np.concatenate([np.asarray(R[c][k], f32) for c in range(NCORES)], axis=0)
    y_prompt = cat("yp").reshape(16, SEQ, D)
    y_sample = cat("ys").reshape(128, 1, D)
    k_win = cat("kwp").reshape(1, 16, SEQ, 8, 64)
    v_win = cat("vwp").reshape(1, 16, SEQ, 8, 64)
    ret_p = cat("rp").reshape(1, 16, 4, 128, 128)
    k_new = cat("kns").reshape(1, 128, 1, 8, 64)
    v_new = cat("vns").reshape(1, 128, 1, 8, 64)
    ret_s = cat("rs").reshape(1, 128, 4, 128, 128)
    return (y_prompt, y_sample, k_win, v_win, ret_p, k_new, v_new, ret_s)
```
